# Optimizing a Trainium2 kernel written in Bass

```python
import math, functools
import jax, jax.numpy as jnp
from jax import lax
import numpy as np

D_MODEL = 2048
BATCH = 16
SEQ = 256
DEPTH = 2
DEC_BATCH = 8
DEC_SEQ = 1024
PAST_LEN = 512

GRID_W = 64
N_EVEN = (DEPTH + 1) // 2
N_ODD = DEPTH // 2
HEAD_DIM = 128
HA = 8
NOPE_DIM = 128
ROPE_DIM = 64
V_DIM = 128
QK_DIM = NOPE_DIM + ROPE_DIM
Q_LORA = 512
KV_LORA = 512
HB = 8
NA_KH = 8
NA_KW = 16
HC = 16
KVH_C = 4
GROUPS_C = HC // KVH_C
WINDOW = 128
WINDOW_BLOCK = 128
D_FF = 5632
N_MOD = 9
IN_EVEN = Q_LORA + KV_LORA + ROPE_DIM + 3 * HB * HEAD_DIM
MIX_EVEN = HA * V_DIM + HB * HEAD_DIM
IN_ODD = HC * HEAD_DIM + 2 * KVH_C * HEAD_DIM
MIX_ODD = HC * HEAD_DIM
ROPE_THETA = 10000.0
EPS = 1e-6
NEG_INF = -1e30
QBLOCK = 128

kernel_name = "hybrid_dit_mla_natten_swa_step"


def rms_norm(x, g):
    xf = x.astype(jnp.float32)
    xf = xf * lax.rsqrt(jnp.mean(xf * xf, axis=-1, keepdims=True) + EPS)
    return (xf * g.astype(jnp.float32)).astype(x.dtype)


def modulate(x, g, shift, scale):
    return rms_norm(x, g) * (1 + scale) + shift


def modulation(cond, ada_w, ada_b):
    m = jax.nn.silu(cond) @ ada_w + ada_b
    return jnp.split(m[:, None, :], N_MOD, axis=-1)


def swiglu(h, w_in, w_out):
    gate, up = jnp.split(h @ w_in, 2, axis=-1)
    return (jax.nn.silu(gate) * up) @ w_out


def axial_rope(x):
    S, R = x.shape[1], x.shape[-1]
    half = R // 2
    nf = half // 2
    t = jnp.arange(S)
    inv_freq = ROPE_THETA ** (-jnp.arange(nf, dtype=jnp.float32) / nf)

    def rot(xa, pos):
        ang = pos.astype(jnp.float32)[:, None] * inv_freq[None, :]
        cos = jnp.cos(ang)[None, :, None, :]
        sin = jnp.sin(ang)[None, :, None, :]
        xa = xa.astype(jnp.float32)
        x1, x2 = xa[..., :nf], xa[..., nf:]
        return jnp.concatenate([x1 * cos - x2 * sin, x1 * sin + x2 * cos], axis=-1)

    out = jnp.concatenate([rot(x[..., :half], t // GRID_W), rot(x[..., half:], t % GRID_W)], axis=-1)
    return out.astype(x.dtype)


def rope_tail(x, n_rot):
    return jnp.concatenate([x[..., :-n_rot], axial_rope(x[..., -n_rot:])], axis=-1)


def blocked_attention(q, k, v, scale, sink=None):
    B, S = q.shape[:2]
    nq = S // QBLOCK
    qb = q.reshape(B, nq, QBLOCK, *q.shape[2:]).swapaxes(0, 1)

    def one(qblk):
        s = jnp.einsum('bqhgd,bthd->bhgqt', qblk, k).astype(jnp.float32) * scale
        if sink is not None:
            s_sink = jnp.broadcast_to(sink[None, :, :, None, None].astype(jnp.float32), s.shape[:-1] + (1,))
            s = jnp.concatenate([s, s_sink], axis=-1)
        p = jax.nn.softmax(s, axis=-1)
        if sink is not None:
            p = p[..., :-1]
        return jnp.einsum('bhgqt,bthd->bqhgd', p.astype(v.dtype), v)

    out = lax.map(one, qb)
    return out.swapaxes(0, 1).reshape(B, S, *out.shape[3:])


def neighborhood_attention(q, k, v, k_ctx, v_ctx, rpb):
    B, S, H, Dh = q.shape
    rows = S // GRID_W
    kh = min(NA_KH, rows)
    kw = NA_KW
    ncb = GRID_W // kw
    cbw = 2 * kw
    scale = 1.0 / math.sqrt(Dh)
    qg = q.reshape(B, rows, GRID_W, H, Dh)
    kg = k.reshape(B, rows, GRID_W, H, Dh)
    vg = v.reshape(B, rows, GRID_W, H, Dh)
    band_start = np.clip(np.arange(ncb) * kw - kw // 2, 0, GRID_W - cbw)
    key_col = band_start[:, None] + np.arange(cbw)[None, :]
    q_col = np.arange(ncb)[:, None] * kw + np.arange(kw)[None, :]
    win_start = np.clip(q_col - kw // 2, 0, GRID_W - kw)
    kc = key_col[:, None, :]
    col_valid = jnp.asarray((kc >= win_start[..., None]) & (kc < win_start[..., None] + kw))
    col_off = np.clip(kc - q_col[..., None], -(NA_KW - 1), NA_KW - 1) + (NA_KW - 1)
    col_bias = rpb[:, :, col_off]

    def one_row(r):
        rs = jnp.clip(r - kh // 2, 0, rows - kh)
        k_band = lax.dynamic_slice_in_dim(kg, rs, kh, axis=1)[:, :, key_col]
        v_band = lax.dynamic_slice_in_dim(vg, rs, kh, axis=1)[:, :, key_col]
        q_r = lax.dynamic_index_in_dim(qg, r, axis=1, keepdims=False).reshape(B, ncb, kw, H, Dh)
        s_loc = jnp.einsum('bjqhd,bajchd->bjqhac', q_r, k_band).astype(jnp.float32) * scale
        row_off = rs + jnp.arange(kh) - r + (NA_KH - 1)
        bias = col_bias[:, row_off].transpose(2, 3, 0, 1, 4).astype(jnp.float32)
        s_loc = jnp.where(col_valid[:, :, None, None, :], s_loc + bias, NEG_INF)
        s_loc = s_loc.reshape(B, ncb, kw, H, kh * cbw)
        s_ctx = jnp.einsum('bjqhd,blhd->bjqhl', q_r, k_ctx).astype(jnp.float32) * scale
        p = jax.nn.softmax(jnp.concatenate([s_loc, s_ctx], axis=-1), axis=-1).astype(v.dtype)
        p_loc = p[..., :kh * cbw].reshape(B, ncb, kw, H, kh, cbw)
        p_ctx = p[..., kh * cbw:]
        out = (jnp.einsum('bjqhac,bajchd->bjqhd', p_loc, v_band)
               + jnp.einsum('bjqhl,blhd->bjqhd', p_ctx, v_ctx))
        return out.reshape(B, GRID_W, H, Dh)

    out = lax.map(one_row, jnp.arange(rows))
    return out.transpose(1, 0, 2, 3, 4).reshape(B, S, H, Dh)


def windowed_attention(q, k, v, k_ctx, v_ctx, sink):
    B, S, KVH, G, Dh = q.shape
    wb = WINDOW_BLOCK
    nb = S // wb
    scale = 1.0 / math.sqrt(Dh)
    pad = ((0, 0), (wb, wb), (0, 0), (0, 0))
    kp = jnp.pad(k, pad).reshape(B, nb + 2, wb, KVH, Dh)
    vp = jnp.pad(v, pad).reshape(B, nb + 2, wb, KVH, Dh)
    k_band = jnp.concatenate([kp[:, :-2], kp[:, 1:-1], kp[:, 2:]], axis=2)
    v_band = jnp.concatenate([vp[:, :-2], vp[:, 1:-1], vp[:, 2:]], axis=2)
    qb = q.reshape(B, nb, wb, KVH, G, Dh)

    def one_block(args):
        qblk, kblk, vblk, n = args
        s_loc = jnp.einsum('bqhgd,bthd->bhgqt', qblk, kblk).astype(jnp.float32) * scale
        kabs = n * wb - wb + jnp.arange(3 * wb)
        qabs = n * wb + jnp.arange(wb)
        valid = (jnp.abs(qabs[:, None] - kabs[None, :]) <= WINDOW) & (kabs[None, :] >= 0) & (kabs[None, :] < S)
        s_loc = jnp.where(valid, s_loc, NEG_INF)
        s_ctx = jnp.einsum('bqhgd,blhd->bhgql', qblk, k_ctx).astype(jnp.float32) * scale
        s_sink = jnp.broadcast_to(sink[None, :, :, None, None].astype(jnp.float32), (B, KVH, G, wb, 1))
        p = jax.nn.softmax(jnp.concatenate([s_loc, s_ctx, s_sink], axis=-1), axis=-1).astype(v.dtype)
        return (jnp.einsum('bhgqt,bthd->bqhgd', p[..., :3 * wb], vblk)
                + jnp.einsum('bhgql,blhd->bqhgd', p[..., 3 * wb:-1], v_ctx))

    out = lax.map(one_block, (qb.swapaxes(0, 1), k_band.swapaxes(0, 1), v_band.swapaxes(0, 1), jnp.arange(nb)))
    return out.swapaxes(0, 1).reshape(B, S, KVH, G, Dh)


def even_projections(h, w_in, q_norm, w_q_up, kv_norm, mla_qk_norm, na_qk_norm):
    B, T, _ = h.shape
    i0 = Q_LORA
    i1 = i0 + KV_LORA
    i2 = i1 + ROPE_DIM
    i3 = i2 + HB * HEAD_DIM
    i4 = i3 + HB * HEAD_DIM
    cq, ckv, k_rope, qn, kn, vn = jnp.split(h @ w_in, [i0, i1, i2, i3, i4], axis=-1)
    q_mla = (rms_norm(cq, q_norm) @ w_q_up).reshape(B, T, HA, QK_DIM)
    q_mla = rms_norm(q_mla, mla_qk_norm[0])
    ckv = rms_norm(ckv, kv_norm)
    q_na = rms_norm(qn.reshape(B, T, HB, HEAD_DIM), na_qk_norm[0])
    k_na = rms_norm(kn.reshape(B, T, HB, HEAD_DIM), na_qk_norm[1])
    v_na = vn.reshape(B, T, HB, HEAD_DIM)
    return q_mla, ckv, k_rope, q_na, k_na, v_na


def mla_keys_values(ckv, k_rope, w_kv_up, k_norm):
    B, T, _ = ckv.shape
    kv = (ckv @ w_kv_up).reshape(B, T, HA, NOPE_DIM + V_DIM)
    k = jnp.concatenate([kv[..., :NOPE_DIM], jnp.broadcast_to(k_rope[:, :, None, :], (B, T, HA, ROPE_DIM))], axis=-1)
    return rms_norm(k, k_norm), kv[..., NOPE_DIM:]


def even_mixer_context(w_in, w_out, q_norm, w_q_up, kv_norm, w_kv_up, mla_qk_norm, na_qk_norm, h):
    B, T, _ = h.shape
    q_mla, ckv, k_rope, q_na, k_na, v_na = even_projections(h, w_in, q_norm, w_q_up, kv_norm, mla_qk_norm, na_qk_norm)
    k_mla, v_mla = mla_keys_values(ckv, k_rope, w_kv_up, mla_qk_norm[1])
    o_mla = blocked_attention(q_mla[:, :, :, None], k_mla, v_mla, 1.0 / math.sqrt(QK_DIM))
    o_na = blocked_attention(q_na[:, :, :, None], k_na, v_na, 1.0 / math.sqrt(HEAD_DIM))
    y = jnp.concatenate([o_mla.reshape(B, T, HA * V_DIM), o_na.reshape(B, T, HB * HEAD_DIM)], axis=-1) @ w_out
    return y, (ckv, k_rope, k_na, v_na)


def even_mixer_latent(cache_ckv, cache_krope, cache_k, cache_v, rpb, w_in, w_out, q_norm, w_q_up, kv_norm, w_kv_up,
                      mla_qk_norm, na_qk_norm, h):
    B, S, _ = h.shape
    q_mla, ckv, k_rope, q_na, k_na, v_na = even_projections(h, w_in, q_norm, w_q_up, kv_norm, mla_qk_norm, na_qk_norm)
    k_lat, v_lat = mla_keys_values(ckv, k_rope, w_kv_up, mla_qk_norm[1])
    q_mla = rope_tail(q_mla, ROPE_DIM)
    k_lat = rope_tail(k_lat, ROPE_DIM)
    k_ctx, v_ctx = mla_keys_values(cache_ckv, cache_krope, w_kv_up, mla_qk_norm[1])
    o_mla = blocked_attention(q_mla[:, :, :, None], jnp.concatenate([k_lat, k_ctx], axis=1),
                              jnp.concatenate([v_lat, v_ctx], axis=1), 1.0 / math.sqrt(QK_DIM))
    o_na = neighborhood_attention(q_na, k_na, v_na, cache_k, cache_v, rpb)
    y = jnp.concatenate([o_mla.reshape(B, S, HA * V_DIM), o_na.reshape(B, S, HB * HEAD_DIM)], axis=-1) @ w_out
    return y, ()


def odd_projections(h, w_in, qk_norm):
    B, T, _ = h.shape
    q, k, v = jnp.split(h @ w_in, [HC * HEAD_DIM, (HC + KVH_C) * HEAD_DIM], axis=-1)
    q = rms_norm(q.reshape(B, T, HC, HEAD_DIM), qk_norm[0])
    k = rms_norm(k.reshape(B, T, KVH_C, HEAD_DIM), qk_norm[1])
    return q, k, v.reshape(B, T, KVH_C, HEAD_DIM)


def odd_mixer_context(w_in, w_out, qk_norm, sink, h):
    B, T, _ = h.shape
    q, k, v = odd_projections(h, w_in, qk_norm)
    o = blocked_attention(q.reshape(B, T, KVH_C, GROUPS_C, HEAD_DIM), k, v, 1.0 / math.sqrt(HEAD_DIM),
                          sink=sink.reshape(KVH_C, GROUPS_C))
    return o.reshape(B, T, MIX_ODD) @ w_out, (k, v)


def odd_mixer_latent(cache_k, cache_v, w_in, w_out, qk_norm, sink, h):
    B, S, _ = h.shape
    q, k, v = odd_projections(h, w_in, qk_norm)
    q = axial_rope(q)
    k = axial_rope(k)
    o = windowed_attention(q.reshape(B, S, KVH_C, GROUPS_C, HEAD_DIM), k, v, cache_k, cache_v,
                           sink.reshape(KVH_C, GROUPS_C))
    return o.reshape(B, S, MIX_ODD) @ w_out, ()


def macaron_layer(x, mods, norm_g, ffn_w_in, ffn_w_out, mixer):
    sh1, sc1, g1, sh2, sc2, g2, sh3, sc3, g3 = mods
    x = x + 0.5 * g1 * swiglu(modulate(x, norm_g[0], sh1, sc1), ffn_w_in[0], ffn_w_out[0])
    y, extras = mixer(modulate(x, norm_g[1], sh2, sc2))
    x = x + g2 * y
    x = x + 0.5 * g3 * swiglu(modulate(x, norm_g[2], sh3, sc3), ffn_w_in[1], ffn_w_out[1])
    return x, extras


def setup_inputs(seed: int = 0) -> dict:
    key = jax.random.key(seed)
    ks = iter(jax.random.split(key, 32))

    def normal(shape, scale=1.0):
        return jax.random.normal(next(ks), shape, jnp.float32) * scale

    def gain(shape):
        return 1.0 + normal(shape, 0.02)

    D = D_MODEL
    return {
        "x_prompt": normal((BATCH, SEQ, D)),
        "x_sample": normal((DEC_BATCH, DEC_SEQ, D)),
        "cache_mla_ckv": normal((DEC_BATCH, N_EVEN, PAST_LEN, KV_LORA)),
        "cache_mla_krope": normal((DEC_BATCH, N_EVEN, PAST_LEN, ROPE_DIM)),
        "cache_na_k": normal((DEC_BATCH, N_EVEN, PAST_LEN, HB, HEAD_DIM)),
        "cache_na_v": normal((DEC_BATCH, N_EVEN, PAST_LEN, HB, HEAD_DIM)),
        "cache_gqa_k": normal((DEC_BATCH, N_ODD, PAST_LEN, KVH_C, HEAD_DIM)),
        "cache_gqa_v": normal((DEC_BATCH, N_ODD, PAST_LEN, KVH_C, HEAD_DIM)),
        "c": normal((DEC_BATCH, D)),
        "c_ctx": normal((D,)),
        "ada_w": normal((DEPTH, D, N_MOD * D), 0.5 * D ** -0.5),
        "ada_b": normal((DEPTH, N_MOD * D), 0.02),
        "norm_g": gain((DEPTH, 3, D)),
        "ffn_w_in": normal((DEPTH, 2, D, 2 * D_FF), D ** -0.5),
        "ffn_w_out": normal((DEPTH, 2, D_FF, D), D_FF ** -0.5),
        "even_w_in": normal((N_EVEN, D, IN_EVEN), D ** -0.5),
        "even_w_out": normal((N_EVEN, MIX_EVEN, D), MIX_EVEN ** -0.5),
        "mla_q_norm": gain((N_EVEN, Q_LORA)),
        "mla_w_q_up": normal((N_EVEN, Q_LORA, HA * QK_DIM), Q_LORA ** -0.5),
        "mla_kv_norm": gain((N_EVEN, KV_LORA)),
        "mla_w_kv_up": normal((N_EVEN, KV_LORA, HA * (NOPE_DIM + V_DIM)), KV_LORA ** -0.5),
        "mla_qk_norm": gain((N_EVEN, 2, QK_DIM)),
        "na_qk_norm": gain((N_EVEN, 2, HEAD_DIM)),
        "na_rpb": normal((N_EVEN, HB, 2 * NA_KH - 1, 2 * NA_KW - 1), 0.1),
        "odd_w_in": normal((N_ODD, D, IN_ODD), D ** -0.5),
        "odd_w_out": normal((N_ODD, MIX_ODD, D), MIX_ODD ** -0.5),
        "gqa_qk_norm": gain((N_ODD, 2, HEAD_DIM)),
        "gqa_sink": normal((N_ODD, HC), 0.5),
    }


def reference(x_prompt, x_sample, cache_mla_ckv, cache_mla_krope, cache_na_k, cache_na_v, cache_gqa_k, cache_gqa_v,
              c, c_ctx, ada_w, ada_b, norm_g, ffn_w_in, ffn_w_out, even_w_in, even_w_out, mla_q_norm, mla_w_q_up,
              mla_kv_norm, mla_w_kv_up, mla_qk_norm, na_qk_norm, na_rpb, odd_w_in, odd_w_out, gqa_qk_norm, gqa_sink):
    xp, xs = x_prompt, x_sample
    ctx_cond = c_ctx[None, :]
    ckv_l, kr_l, nak_l, nav_l, gk_l, gv_l = [], [], [], [], [], []
    for layer in range(DEPTH):
        mods_ctx = modulation(ctx_cond, ada_w[layer], ada_b[layer])
        mods_lat = modulation(c, ada_w[layer], ada_b[layer])
        common = (norm_g[layer], ffn_w_in[layer], ffn_w_out[layer])
        e = layer // 2
        if layer % 2 == 0:
            mix_w = (even_w_in[e], even_w_out[e], mla_q_norm[e], mla_w_q_up[e], mla_kv_norm[e], mla_w_kv_up[e],
                     mla_qk_norm[e], na_qk_norm[e])
            xp, (ckv, kr, nak, nav) = macaron_layer(xp, mods_ctx, *common,
                                                    functools.partial(even_mixer_context, *mix_w))
            xs, _ = macaron_layer(xs, mods_lat, *common,
                                  functools.partial(even_mixer_latent, cache_mla_ckv[:, e], cache_mla_krope[:, e],
                                                    cache_na_k[:, e], cache_na_v[:, e], na_rpb[e], *mix_w))
            ckv_l.append(ckv)
            kr_l.append(kr)
            nak_l.append(nak)
            nav_l.append(nav)
        else:
            mix_w = (odd_w_in[e], odd_w_out[e], gqa_qk_norm[e], gqa_sink[e])
            xp, (gk, gv) = macaron_layer(xp, mods_ctx, *common, functools.partial(odd_mixer_context, *mix_w))
            xs, _ = macaron_layer(xs, mods_lat, *common,
                                  functools.partial(odd_mixer_latent, cache_gqa_k[:, e], cache_gqa_v[:, e], *mix_w))
            gk_l.append(gk)
            gv_l.append(gv)
    return (xp, xs, jnp.stack(ckv_l, axis=1), jnp.stack(kr_l, axis=1), jnp.stack(nak_l, axis=1),
            jnp.stack(nav_l, axis=1), jnp.stack(gk_l, axis=1), jnp.stack(gv_l, axis=1))
```

```python
import contextlib
import numpy as np
import concourse.bass as bass
import concourse.mybir as mybir
from concourse.bass_utils import run_bass_kernel_spmd

F32 = mybir.dt.float32
BF16 = mybir.dt.bfloat16
AF = mybir.ActivationFunctionType
ALU = mybir.AluOpType

ENGS = ('pe', 'act', 'dve', 'pool', 'sp')
SAME_ENG_SYNC = True
NDSEM = 8
NSLOT = 6
GROUPS = [5, 5, 5, 5, 5, 5, 5, 5, 4]
GMAX = 5
EPS = 1e-6
NEG = -30000.0

D = 2048
NTOK = 1536
DFF = 5632


class _Op:
    __slots__ = ('eng', 'fn', 'dma', 'users', 'id', 'key', 'deps', 'sem', 'val', 'selfwait')


class Sched:
    def __init__(self, nc):
        self.nc = nc
        self.ops = {e: [] for e in ENGS}
        self.cells = {}
        self.uid = 0
        self.pending = {e: [] for e in ENGS}
        self.dmas = []

    def add(self, eng, fn, r=(), w=(), dma=False):
        op = _Op()
        op.eng = eng; op.fn = fn; op.dma = dma; op.users = False
        op.id = self.uid; self.uid += 1
        op.key = ('d', op.id) if dma else eng
        op.sem = None; op.val = 0; op.selfwait = 0
        deps = {}
        cells = self.cells
        for c in r:
            st = cells.get(c)
            if st is not None and st[0] is not None:
                deps[st[0].id] = st[0]
        for c in w:
            st = cells.get(c)
            if st is not None:
                if st[0] is not None:
                    deps[st[0].id] = st[0]
                for o in st[1].values():
                    deps[o.id] = o
        for c in r:
            st = cells.get(c)
            if st is None:
                cells[c] = [None, {op.key: op}]
            else:
                st[1][op.key] = op
        for c in w:
            cells[c] = [op, {}]
        for o in self.pending[eng]:
            deps[o.id] = o
        self.pending[eng] = []
        dl = []
        for d in deps.values():
            if d is op:
                continue
            if (not d.dma) and (not dma) and d.eng == eng and (eng == 'pe' or not SAME_ENG_SYNC):
                continue
            d.users = True
            dl.append(d)
        op.deps = dl
        self.ops[eng].append(op)
        if dma:
            self.dmas.append(op)
        return op

    def barrier(self):
        last = []
        for e in ENGS:
            for op in reversed(self.ops[e]):
                if not op.dma:
                    last.append(op)
                    break
        last.extend(self.dmas)
        self.dmas = []
        for e in ENGS:
            self.pending[e] = self.pending[e] + list(last)

    def emit(self, limit=None):
        nc = self.nc
        if limit is not None:
            for e in ENGS:
                self.ops[e] = [o for o in self.ops[e] if o.id < limit]
        engsem = {e: nc.alloc_semaphore("s_" + e) for e in ENGS}
        dmasem = {q: [nc.alloc_semaphore("d_%s%d" % (q, i)) for i in range(NDSEM)] for q in ('sp', 'pool')}
        semkey = {}
        final = {}
        for e in ENGS:
            cnt = 0
            di = 0
            duse = [0] * NDSEM
            for op in self.ops[e]:
                if op.dma:
                    i = di % NDSEM
                    di += 1
                    duse[i] += 1
                    op.sem = dmasem[e][i]
                    semkey[id(op)] = (e, i)
                    op.val = 16 * duse[i]
                    op.selfwait = 16 * (duse[i] - 1)
                    final[(e, i)] = (op.sem, op.val)
                elif op.users:
                    cnt += 1
                    op.sem = engsem[e]
                    semkey[id(op)] = (e, -1)
                    op.val = cnt
            assert cnt < 60000, (e, cnt)
        ops = self.ops

        def run(e, eng):
            waited = {}
            for op in ops[e]:
                for d in op.deps:
                    k = semkey[id(d)]
                    if waited.get(k, 0) < d.val:
                        eng.wait_ge(d.sem, d.val)
                        waited[k] = d.val
                if op.dma and op.selfwait > 0:
                    k = semkey[id(op)]
                    if waited.get(k, 0) < op.selfwait:
                        eng.wait_ge(op.sem, op.selfwait)
                        waited[k] = op.selfwait
                ins = op.fn(eng)
                if op.dma:
                    ins.then_inc(op.sem, 16)
                elif op.users:
                    ins.then_inc(op.sem, 1)
            if e == 'sp':
                for k, (sem, val) in final.items():
                    if waited.get(k, 0) < val:
                        eng.wait_ge(sem, val)

        with nc.Block() as block:
            @block.tensor
            def _(eng):
                run('pe', eng)

            @block.scalar
            def _(eng):
                run('act', eng)

            @block.vector
            def _(eng):
                run('dve', eng)

            @block.gpsimd
            def _(eng):
                run('pool', eng)

            @block.sync
            def _(eng):
                run('sp', eng)


class Ring:
    def __init__(self, name, aps):
        self.name = name
        self.aps = aps
        self.i = 0

    def next(self):
        i = self.i % len(self.aps)
        self.i += 1
        return self.aps[i], (self.name, i)


def build(cfg):
    n_sub = cfg.get('n_sub', 6)
    debug = cfg.get('debug', False)
    skip_ffn = cfg.get('skip_ffn', False)
    mstop = cfg.get('mstop', 99)
    nc = bass.Bass("TRN2", target_bir_lowering=False)
    S = Sched(nc)
    es = contextlib.ExitStack()

    def din(name, shape):
        return nc.dram_tensor(name, list(shape), F32, kind="ExternalInput").ap()

    def dout(name, shape):
        return nc.dram_tensor(name, list(shape), F32, kind="ExternalOutput").ap()

    def sb(name, shape, dt):
        return es.enter_context(nc.sbuf_tensor("sb_" + name, list(shape), dt))

    xin = din("xin", [NTOK, D])
    c_ckv = din("c_ckv", [512, 512]); c_kr = din("c_kr", [512, 64])
    c_nak = din("c_nak", [512, 1024]); c_nav = din("c_nav", [512, 1024])
    c_gk = din("c_gk", [512, 512]); c_gv = din("c_gv", [512, 512])
    vecs = din("vecs", [640, 128])
    sinkb = din("sinkb", [128, 16])
    ident_d = din("ident", [128, 128])
    ropeg_c = din("ropeg_c", [128, 1024]); ropeg_s = din("ropeg_s", [128, 1024])
    ropem_c = din("ropem_c", [64, 1024]); ropem_s = din("ropem_s", [64, 1024])
    p128_d = din("p128", [128, 128]); p64_d = din("p64", [64, 64])
    wmask = din("wmask", [1024, 1024])
    nab = din("nab", [8 * 1024, 1024])
    ada_w = din("ada_w", [2 * 2048, 18432])
    ffn_w_in = din("ffn_w_in", [4 * 2048, 11264])
    ffn_w_out = din("ffn_w_out", [4 * 5632, 2048])
    even_w_in = din("even_w_in", [2048, 4160]); even_w_out = din("even_w_out", [2048, 2048])
    w_q_up = din("w_q_up", [512, 1536]); w_kv_up = din("w_kv_up", [512, 2048])
    odd_w_in = din("odd_w_in", [2048, 3072]); odd_w_out = din("odd_w_out", [2048, 2048])

    y_out = dout("y", [NTOK, D])
    o_ckv = dout("o_ckv", [512, 512]); o_kr = dout("o_kr", [512, 64])
    o_nak = dout("o_nak", [512, 1024]); o_nav = dout("o_nav", [512, 1024])
    o_gk = dout("o_gk", [512, 512]); o_gv = dout("o_gv", [512, 512])
    XD = nc.dram_tensor("xd_scratch", [128, 16, NTOK], F32, kind="ExternalOutput").ap()

    XE = 16 * NTOK * 2
    HE = GMAX * NTOK
    BIG = sb("BIG", [128, XE + HE], BF16)
    X = BIG[:, 0:XE].bitcast(F32).rearrange("p (c t) -> p c t", c=16)
    H = BIG[:, XE:XE + HE].rearrange("p (j t) -> p j t", j=GMAX)
    XMT = sb("XM", [128, 16 * NTOK], BF16)
    XM = XMT[:, :].rearrange("p (c t) -> p c t", c=16)
    WR = sb("WR", [128, NSLOT, 2048], BF16)
    VT = sb("VT", [128, 640], F32)
    MODS = sb("MODS", [128, 144, 2], F32)
    AMOD = sb("AMOD", [128, 3, 16, 2], F32)
    GMOD = sb("GMOD", [128, 3, 16, 2], F32)
    ident = sb("ident", [128, 128], F32)
    ones_f = sb("ones_f", [128, 128], F32)
    ones_b = sb("ones_b", [128, 128], BF16)
    sT = sb("sT", [128, 32], BF16)
    sinkE = sb("sinkE", [128, 16], F32)
    p128 = sb("p128", [128, 128], F32)
    p64 = sb("p64", [64, 64], F32)
    TR = sb("TR", [128, 4, 512], F32)
    RSR = sb("RSR", [128, 2, 512], F32)
    RDR = sb("RDR", [128, 3, 512], F32)
    tring = Ring('TR', [TR[:, i, :] for i in range(4)])
    rsring = Ring('RS', [RSR[:, i, :] for i in range(2)])
    rdring = Ring('RD', [RDR[:, i, :] for i in range(3)])
    PSB = [es.enter_context(nc.psum_tensor("ps%d" % i, [128, 512], F32)) for i in range(8)]
    psfree = list(range(8))

    def psget():
        b = psfree.pop(0)
        return b

    def psput(b):
        psfree.append(b)

    def dma(q, out, in_, r=(), w=()):
        return S.add(q, lambda e, o=out, i=in_: e.dma_start(out=o, in_=i), r=r, w=w, dma=True)

    def mm(ps, lhsT, rhs, start, stop, r, w):
        return S.add('pe', lambda e, a=ps, b=lhsT, c=rhs, s0=start, s1=stop: e.matmul(a, b, c, start=s0, stop=s1), r=r, w=w)

    def tp(ps, in_, idn, r, w):
        return S.add('pe', lambda e, a=ps, b=in_, c=idn: e.transpose(a, b, c), r=r, w=w)

    def act(out, in_, func, r, w, bias=None, scale=None):
        kw = {}
        if bias is not None:
            kw['bias'] = bias
        if scale is not None:
            kw['scale'] = scale
        return S.add('act', lambda e, o=out, i=in_, f=func, k=kw: e.activation(out=o, in_=i, func=f, **k), r=r, w=w)

    def tt(out, in0, in1, op, r, w, eng='dve'):
        return S.add(eng, lambda e, o=out, a=in0, b=in1, p=op: e.tensor_tensor(out=o, in0=a, in1=b, op=p), r=r, w=w)

    def ts(out, in0, s1, s2, op0, op1, r, w, eng='dve'):
        if s2 is None:
            return S.add(eng, lambda e, o=out, a=in0, x=s1, p=op0: e.tensor_scalar(out=o, in0=a, scalar1=x, scalar2=None, op0=p), r=r, w=w)
        return S.add(eng, lambda e, o=out, a=in0, x=s1, y=s2, p=op0, q=op1: e.tensor_scalar(out=o, in0=a, scalar1=x, scalar2=y, op0=p, op1=q), r=r, w=w)

    def stt(out, in0, sc, in1, op0, op1, r, w, eng='dve'):
        return S.add(eng, lambda e, o=out, a=in0, s=sc, b=in1, p=op0, q=op1: e.scalar_tensor_tensor(out=o, in0=a, scalar=s, in1=b, op0=p, op1=q), r=r, w=w)

    def cp(out, in_, r, w, eng='dve'):
        return S.add(eng, lambda e, o=out, i=in_: e.tensor_copy(out=o, in_=i), r=r, w=w)

    def recip(out, in_, r, w):
        return S.add('dve', lambda e, o=out, i=in_: e.reciprocal(out=o, in_=i), r=r, w=w)

    def memset(ap, v, w):
        return S.add('dve', lambda e, a=ap, c=v: e.memset(a, c), w=w)

    wslot = [0]

    def wload(src, a, b):
        s = wslot[0] % NSLOT
        wslot[0] += 1
        view = WR[:, s, 0:a * b].rearrange("p (a b) -> p a b", a=a)
        dma('pool', view, src, w=[('W', s)])
        return view, ('W', s)

    def wblk(wd, row0, nk, c0, nc_):
        src = wd[row0:row0 + nk * 128, c0:c0 + nc_].rearrange("(k p) n -> p k n", p=128)
        return wload(src, nk, nc_)

    def xc(fc, t0, t1):
        return [('X', fc, t) for t in range(t0 // 256, (t1 + 255) // 256)]

    def xmc(fc, t0, t1):
        return [('XM', fc, t) for t in range(t0 // 256, (t1 + 255) // 256)]

    def xdc(fc, t0, t1):
        return [('XD', fc, t) for t in range(t0 // 256, (t1 + 255) // 256)]

    dma('sp', ident[:, :], ident_d[:, :], w=['ident'])
    dma('sp', p128[:, :], p128_d[:, :], w=['p128'])
    dma('sp', p64[:, :], p64_d[:, :], w=['p64'])
    dma('sp', sinkE[:, :], sinkb[:, :], w=['sinkE'])
    memset(ones_f[:, :], 1.0, ['ones_f'])
    memset(ones_b[:, :], 1.0, ['ones_b'])
    act(sinkE[:, :], sinkE[:, :], AF.Exp, r=['sinkE'], w=['sinkE'])
    STGF = XMT[:, :].bitcast(F32)
    for t in range(5):
        st = STGF[:, t * 128:(t + 1) * 128]
        dma('sp', st, vecs[t * 128:(t + 1) * 128, :], w=[('STG', t)])
        b = psget()
        tp(PSB[b][:, 0:128], st, ident[:, :], r=[('STG', t), 'ident'], w=[('P', b)])
        cp(VT[:, t * 128:(t + 1) * 128], PSB[b][:, 0:128], r=[('P', b)], w=['VT'])
        psput(b)
    act(sT[:, :], VT[:, 0:32], AF.Silu, r=['VT'], w=['sT'])

    S.barrier()
    for t in range(12):
        sl = t % 3
        st = STGF[:, 4096 + sl * 2048: 4096 + (sl + 1) * 2048]
        dma('sp', st, xin[t * 128:(t + 1) * 128, :], w=[('STGX', sl)])
        for f4 in range(4):
            b = psget()
            for q in range(4):
                fc = f4 * 4 + q
                tp(PSB[b][:, q * 128:(q + 1) * 128], st[:, fc * 128:(fc + 1) * 128], ident[:, :],
                   r=[('STGX', sl), 'ident'], w=[('P', b)])
            dst = X[:, f4 * 4:(f4 + 1) * 4, t * 128:(t + 1) * 128]
            src = PSB[b][:, :].rearrange("p (q n) -> p q n", q=4)
            wc = []
            for q in range(4):
                wc += xc(f4 * 4 + q, t * 128, (t + 1) * 128)
            if f4 % 2 == 0:
                cp(dst, src, r=[('P', b)], w=wc)
            else:
                S.add('act', lambda e, o=dst, i=src: e.copy(out=o, in_=i), r=[('P', b)], w=wc)
            psput(b)
    S.barrier()

    def mods_mm(l, oc0, oc1, b):
        PM = PSB[b]
        for oc in range(oc0, oc1):
            blk, wc = wblk(ada_w, l * 2048, 16, oc * 128, 128)
            for k in range(16):
                mm(PM[:, 2 * oc:2 * oc + 2], blk[:, k, :], sT[:, k:32:16], k == 0, k == 15,
                   r=[wc, 'sT'], w=[('P', b)])

    def mods_fin(l, b, parts):
        PM = PSB[b]
        for i in parts:
            for g in range(2):
                tt(MODS[:, 48 * i:48 * (i + 1), g], PM[:, 96 * i + g:96 * (i + 1):2],
                   VT[:, 256 + l * 144 + 48 * i:256 + l * 144 + 48 * (i + 1)], ALU.add,
                   r=[('P', b), 'VT'], w=[('MODS', i)])
            for g in range(2):
                ts(AMOD[:, i, :, g], MODS[:, (3 * i + 1) * 16:(3 * i + 2) * 16, g], 1.0, None, ALU.add, None,
                   r=[('MODS', i)], w=[('AMOD', i)])
                tt(AMOD[:, i, :, g], AMOD[:, i, :, g], VT[:, 32 + (l * 3 + i) * 16:32 + (l * 3 + i + 1) * 16], ALU.mult,
                   r=[('AMOD', i), 'VT'], w=[('AMOD', i)])
                ts(GMOD[:, i, :, g], MODS[:, (3 * i + 2) * 16:(3 * i + 3) * 16, g], 0.5 if i != 1 else 1.0, None,
                   ALU.mult, None, r=[('MODS', i)], w=[('GMOD', i)])

    def mods(l):
        b = psget()
        mods_mm(l, 0, 144, b)
        mods_fin(l, b, [0, 1, 2])
        psput(b)

    def Asc(i, fc, g):
        return AMOD[:, i, fc, g:g + 1]

    def Bsc(i, fc, g):
        return MODS[:, 3 * i * 16 + fc, g:g + 1]

    def Gsc(i, fc, g):
        return GMOD[:, i, fc, g:g + 1]

    def rstd_from(srcs, n, dim):
        b = psget()
        ps = PSB[b][:, 0:n]
        for i, (ap, cells, kp) in enumerate(srcs):
            sq, sqc = tring.next()
            act(sq[0:kp, 0:n], ap, AF.Square, r=cells, w=[sqc])
            mm(ps, ones_f[0:kp, :], sq[0:kp, 0:n], i == 0, i == len(srcs) - 1, r=[sqc, 'ones_f'], w=[('P', b)])
        rs, rsc = rsring.next()
        ts(rs[:, 0:n], ps, 1.0 / dim, EPS, ALU.mult, ALU.add, r=[('P', b)], w=[rsc])
        psput(b)
        act(rs[:, 0:n], rs[:, 0:n], AF.Sqrt, r=[rsc], w=[rsc])
        rd, rdc = rdring.next()
        recip(rd[:, 0:n], rs[:, 0:n], r=[rsc], w=[rdc])
        return rd, rdc

    def prepass(i, src_fn, t0, n, g):
        srcs = []
        for fc in range(16):
            ap, cells = src_fn(fc)
            srcs.append((ap, cells, 128))
        rd, rdc = rstd_from(srcs, n, 2048.0)
        for fc in range(16):
            ap, cells = src_fn(fc)
            tmp, tc = tring.next()
            stt(tmp[:, 0:n], ap, Asc(i, fc, g), rd[:, 0:n], ALU.mult, ALU.mult, r=cells + [rdc, ('AMOD', i)], w=[tc])
            act(XM[:, fc, t0:t0 + n], tmp[:, 0:n], AF.Identity, r=[tc, ('MODS', i)], w=xmc(fc, t0, t0 + n),
                bias=Bsc(i, fc, g))

    TILES = [(0, 512, 0), (512, 512, 1), (1024, 512, 1)]
    GROUPS_IDX = {}
    _b = 0
    for _gi, _G in enumerate(GROUPS):
        GROUPS_IDX[_b] = _gi
        _b += _G

    def ffn(l, j, hook=None, hook2=None):
        i = 0 if j == 0 else 2
        for (t0, n, g) in TILES:
            prepass(i, lambda fc, t0=t0, n=n: (X[:, fc, t0:t0 + n], xc(fc, t0, t0 + n)), t0, n, g)
        row_in = (l * 2 + j) * 2048
        row_out = (l * 2 + j) * 5632
        base = 0
        for G in GROUPS:
            for jj in range(G):
                c = base + jj
                bg, wg = wblk(ffn_w_in, row_in, 16, c * 128, 128)
                bu, wu = wblk(ffn_w_in, row_in, 16, (44 + c) * 128, 128)
                for ti, (t0, n, g) in enumerate(TILES):
                    pg = psget(); pu = psget()
                    for k in range(16):
                        mm(PSB[pg][:, :], bg[:, k, :], XM[:, k, t0:t0 + n], k == 0, k == 15,
                           r=[wg] + xmc(k, t0, t0 + n), w=[('P', pg)])
                    for k in range(16):
                        mm(PSB[pu][:, :], bu[:, k, :], XM[:, k, t0:t0 + n], k == 0, k == 15,
                           r=[wu] + xmc(k, t0, t0 + n), w=[('P', pu)])
                    sg, sgc = tring.next()
                    act(sg[:, :], PSB[pg][:, :], AF.Silu, r=[('P', pg)], w=[sgc])
                    tt(H[:, jj, t0:t0 + n], sg[:, :], PSB[pu][:, :], ALU.mult, r=[sgc, ('P', pu)], w=[('H', jj, ti)])
                    psput(pg); psput(pu)
                if hook2 is not None:
                    hook2(c)
            for ocg in range(4):
                blks = []
                for kb in range(0, G, 4):
                    nk = min(4, G - kb)
                    blks.append(wblk(ffn_w_out, row_out + (base + kb) * 128, nk, ocg * 512, 512))
                for ti, (t0, n, g) in enumerate(TILES):
                    for o4 in range(4):
                        oc = ocg * 4 + o4
                        py = psget()
                        for jj in range(G):
                            blk, wc = blks[jj // 4]
                            mm(PSB[py][:, :], blk[:, jj % 4, o4 * 128:(o4 + 1) * 128], H[:, jj, t0:t0 + n],
                               jj == 0, jj == G - 1, r=[wc, ('H', jj, ti)], w=[('P', py)])
                        stt(X[:, oc, t0:t0 + n], PSB[py][:, :], Gsc(i, oc, g), X[:, oc, t0:t0 + n], ALU.mult, ALU.add,
                            r=[('P', py), ('GMOD', i)] + xc(oc, t0, t0 + n), w=xc(oc, t0, t0 + n))
                        psput(py)
            if hook is not None:
                hook(GROUPS_IDX[base])
            base += G


    INV192 = 1.0 / float(np.sqrt(192.0))
    INV128 = 1.0 / float(np.sqrt(128.0))
    aoff = [0]

    def areset(off=0):
        aoff[0] = off

    def ab(shape, dt):
        nel = int(np.prod(shape))
        ne16 = nel * (2 if dt == F32 else 1)
        ne16 = (ne16 + 31) // 32 * 32
        v = BIG[:, aoff[0]:aoff[0] + ne16]
        aoff[0] += ne16
        assert aoff[0] <= XE + HE, aoff[0]
        if dt == F32:
            v = v.bitcast(F32)
        v = v[:, 0:nel]
        if len(shape) == 2:
            v = v.rearrange("p (a b) -> p a b", a=shape[0])
        return v

    def rstd_from2(srcs, n, dim):
        b = psget()
        ps = PSB[b][:, 0:n]
        for i, (ap, cells, kp, presq) in enumerate(srcs):
            if presq:
                mm(ps, ones_f[0:kp, :], ap, i == 0, i == len(srcs) - 1, r=cells + ['ones_f'], w=[('P', b)])
            else:
                sq, sqc = tring.next()
                act(sq[0:kp, 0:n], ap, AF.Square, r=cells, w=[sqc])
                mm(ps, ones_f[0:kp, :], sq[0:kp, 0:n], i == 0, i == len(srcs) - 1, r=[sqc, 'ones_f'], w=[('P', b)])
        rs, rsc = rsring.next()
        ts(rs[:, 0:n], ps, 1.0 / dim, EPS, ALU.mult, ALU.add, r=[('P', b)], w=[rsc])
        psput(b)
        act(rs[:, 0:n], rs[:, 0:n], AF.Sqrt, r=[rsc], w=[rsc])
        rd, rdc = rdring.next()
        recip(rd[:, 0:n], rs[:, 0:n], r=[rsc], w=[rdc])
        return rd, rdc

    def rope_apply(src, srcc, kp, n, pmat, pmc, cs, sn, tabc, pos0, rd, rdc, out, outc, srcf=None, outf=None):
        srcf = src if srcf is None else srcf
        outf = out if outf is None else outf
        b = psget()
        mm(PSB[b][0:kp, 0:n], pmat, src, True, True, r=srcc + [pmc], w=[('P', b)])
        t1, t1c = tring.next()
        tt(t1[:, 0:n], srcf, cs[:, pos0:pos0 + n], ALU.mult, r=srcc + [tabc], w=[t1c])
        t2, t2c = tring.next()
        tt(t2[:, 0:n], PSB[b][:, 0:n], sn[:, pos0:pos0 + n], ALU.mult, r=[('P', b), tabc], w=[t2c])
        psput(b)
        if rd is None:
            tt(outf, t1[:, 0:n], t2[:, 0:n], ALU.add, r=[t1c, t2c], w=outc)
        else:
            tt(t1[:, 0:n], t1[:, 0:n], t2[:, 0:n], ALU.add, r=[t1c, t2c], w=[t1c])
            tt(outf, t1[:, 0:n], rd[:, 0:n], ALU.mult, r=[t1c, rdc], w=outc)

    def attn(qps, kps, vfn, chunks, nq, scale, out_ap, out_cells, ptring, bring, sink=None, ONES=None):
        po = psget(); pd = psget()
        started = {}
        last_idx = {}
        for idx, (kc, q0, q1, bias) in enumerate(chunks):
            last_idx[(q0, q1)] = idx
        pending = []

        def finish(item):
            idx, kc, q0, q1, pt, ptc = item
            vap, vc = vfn(kc)
            first = (q0, q1) not in started
            started[(q0, q1)] = True
            lastf = last_idx[(q0, q1)] == idx
            mm(PSB[po][:, q0:q1], vap, pt[:, q0:q1], first, lastf, r=[ptc] + vc, w=[('P', po)])
            mm(PSB[pd][:, q0:q1], ones_b[:, :], pt[:, q0:q1], first, lastf, r=[ptc, 'ones_b'], w=[('P', pd)])

        for idx, (kc, q0, q1, bias) in enumerate(chunks):
            ps = psget()
            for i, ((qap, qc), kfn) in enumerate(zip(qps, kps)):
                kap, kcells = kfn(kc)
                mm(PSB[ps][:, q0:q1], kap, qap[:, q0:q1], i == 0, i == len(qps) - 1, r=qc + kcells, w=[('P', ps)])
            pt, ptc = ptring.next()
            if bias is not None:
                bt, btc = bring.next()
                dma('sp', bt[:, q0:q1], bias, w=[btc])
                tmp, tc = tring.next()
                stt(tmp[:, q0:q1], PSB[ps][:, q0:q1], scale, bt[:, q0:q1], ALU.mult, ALU.add, r=[('P', ps), btc], w=[tc])
                act(pt[:, q0:q1], tmp[:, q0:q1], AF.Exp, r=[tc], w=[ptc])
            else:
                act(pt[:, q0:q1], PSB[ps][:, q0:q1], AF.Exp, r=[('P', ps)], w=[ptc], scale=scale)
            psput(ps)
            pending.append((idx, kc, q0, q1, pt, ptc))
            if len(pending) > 1:
                finish(pending.pop(0))
        while pending:
            finish(pending.pop(0))
        rd, rdc = rdring.next()
        if sink is not None:
            stt(rd[:, 0:nq], PSB[pd][:, 0:nq], sink, ONES[:, 0:nq], ALU.add, ALU.mult, r=[('P', pd), 'sinkE', 'ONES'], w=[rdc])
            recip(rd[:, 0:nq], rd[:, 0:nq], r=[rdc], w=[rdc])
        else:
            recip(rd[:, 0:nq], PSB[pd][:, 0:nq], r=[('P', pd)], w=[rdc])
        tt(out_ap, PSB[po][:, 0:nq], rd[:, 0:nq], ALU.mult, r=[('P', po), rdc], w=out_cells)
        psput(po); psput(pd)

    def mixer_pass(l, T0, n, latent, hookh=None):
        g = 1 if latent else 0
        ntile = n // 512
        tiles = [(i * 512, 512) for i in range(ntile)]
        nk = n + (512 if latent else 0)
        nkc = nk // 128
        noc = n // 128
        S.barrier()
        areset()
        if mstop <= 0:
            return
        if mstop <= 1:
            return
        S.barrier()
        areset()
        OT = ab([16, n], BF16)
        ptring = Ring('PT', [ab([512], BF16) for _ in range(4)])
        bring = Ring('BR', [ab([512], F32) for _ in range(2)]) if latent else None
        cst = Ring('CST', [ab([512], F32) for _ in range(2)])
        ONES = None
        if l == 1:
            ONES = ab([512], F32)
            memset(ONES[:, :], 1.0, ['ONES'])
        if latent:
            rdim = 64 if l == 0 else 128
            COS = ab([1024], F32); SIN = ab([1024], F32)
            dma('sp', COS[0:rdim, :], (ropem_c if l == 0 else ropeg_c)[:, :], w=['ROPE'])
            dma('sp', SIN[0:rdim, :], (ropem_s if l == 0 else ropeg_s)[:, :], w=['ROPE'])
            PM_, PMC = (p64[:, :], 'p64') if l == 0 else (p128[:, :], 'p128')
        base_off = aoff[0]

        def xsrc(k, lt, nn):
            return XM[:, k, T0 + lt:T0 + lt + nn], xmc(k, T0 + lt, T0 + lt + nn)

        def kchunks(qt, bias_fn=None, own_range=None):
            ch = []
            if latent:
                for kc in range(n // 128):
                    if own_range is not None and not (own_range[qt][0] <= kc <= own_range[qt][1]):
                        continue
                    ch.append((kc, 0, 512, None if bias_fn is None else bias_fn(kc, qt)))
                for kc in range(n // 128, nkc):
                    ch.append((kc, 0, 512, None))
            else:
                ch = [(0, 0, 256, None), (1, 0, 256, None), (2, 256, 512, None), (3, 256, 512, None)]
            return ch

        def tok_major_v(blk, wc, colsl, VH, vhc, srcfn, nsrc, out_d, col0):
            for t4 in range(noc // 4):
                b = psget()
                for j in range(4):
                    tc_ = t4 * 4 + j
                    for k in range(nsrc):
                        sap, sc = srcfn(k, tc_ * 128, 128)
                        mm(PSB[b][:, j * 128:(j + 1) * 128], sap, blk[:, k, colsl], k == 0, k == nsrc - 1,
                           r=[wc] + sc, w=[('P', b)])
                S.add('act', lambda e, o=VH[:, t4 * 4:(t4 + 1) * 4, :], i=PSB[b][:, :].rearrange("p (j c) -> p j c", j=4): e.copy(out=o, in_=i),
                      r=[('P', b)], w=[vhc])
                if out_d is not None:
                    st, stc = cst.next()
                    S.add('act', lambda e, o=st[:, :], i=PSB[b][:, :]: e.copy(out=o, in_=i), r=[('P', b)], w=[stc])
                    dma('sp', out_d[t4 * 512:(t4 + 1) * 512, col0:col0 + 128].rearrange("(j p) c -> p j c", p=128),
                        st[:, :].rearrange("p (j c) -> p j c", j=4), r=[stc])
                psput(b)

        def ctx_kT(cd, col0, ncol, dst, dstc, kp):
            b = psget()
            for tc_ in range(4):
                st, stc = cst.next()
                dma('sp', st[:, 0:ncol], cd[tc_ * 128:(tc_ + 1) * 128, col0:col0 + ncol], w=[stc])
                tp(PSB[b][0:ncol, tc_ * 128:(tc_ + 1) * 128], st[:, 0:ncol], ident[:, :], r=[stc, 'ident'], w=[('P', b)])
            return b

        def out_kT(src, srcc, kp, out_d, col0):
            for tc_ in range(4):
                b = psget()
                tp(PSB[b][:, 0:kp], src[0:kp, tc_ * 128:(tc_ + 1) * 128], ident[0:kp, 0:kp], r=srcc + ['ident'], w=[('P', b)])
                st, stc = cst.next()
                cp(st[:, 0:kp], PSB[b][:, 0:kp], r=[('P', b)], w=[stc])
                psput(b)
                dma('sp', out_d[tc_ * 128:(tc_ + 1) * 128, col0:col0 + kp], st[:, 0:kp], r=[stc])

        if l == 0:
            CQN = ab([4, n], BF16); CKVN = ab([4, nk], BF16)
            KRG = ab([nk], F32); SQR = ab([nk], F32)
            CKF = ab([4, 512], F32) if not latent else None
            KRAW = ab([512], F32) if not latent else None
            msets = [(ab([n], BF16), ab([n], BF16), ab([nk], BF16), ab([nk], BF16), ab([nkc, 128], BF16)) for _ in range(2)]
            for which in range(2):
                blks = [wblk(even_w_in, 0, 16, which * 512 + oc * 128, 128) for oc in range(4)]
                for (lt, nn) in tiles:
                    banks = []
                    for oc in range(4):
                        b = psget()
                        blk, wc = blks[oc]
                        for k in range(16):
                            sap, sc = xsrc(k, lt, nn)
                            mm(PSB[b][:, 0:nn], blk[:, k, :], sap, k == 0, k == 15, r=[wc] + sc, w=[('P', b)])
                        banks.append(b)
                    rd, rdc = rstd_from2([(PSB[b][:, 0:nn], [('P', b)], 128, False) for b in banks], nn, 512.0)
                    for oc in range(4):
                        b = banks[oc]
                        gsc = VT[:, 128 + which * 4 + oc:129 + which * 4 + oc]
                        if which == 0:
                            stt(CQN[:, oc, lt:lt + nn], PSB[b][:, 0:nn], gsc, rd[:, 0:nn], ALU.mult, ALU.mult,
                                r=[('P', b), rdc, 'VT'], w=['CQN'])
                        elif latent:
                            stt(CKVN[:, oc, lt:lt + nn], PSB[b][:, 0:nn], gsc, rd[:, 0:nn], ALU.mult, ALU.mult,
                                r=[('P', b), rdc, 'VT'], w=['CKVN'])
                        else:
                            stt(CKF[:, oc, lt:lt + nn], PSB[b][:, 0:nn], gsc, rd[:, 0:nn], ALU.mult, ALU.mult,
                                r=[('P', b), rdc, 'VT'], w=['CKF'])
                            S.add('act', lambda e, o=CKVN[:, oc, lt:lt + nn], i=CKF[:, oc, lt:lt + nn]: e.copy(out=o, in_=i),
                                  r=['CKF'], w=['CKVN'])
                        psput(b)
            if not latent:
                for tc_ in range(4):
                    b = psget()
                    for oc in range(4):
                        tp(PSB[b][:, oc * 128:(oc + 1) * 128], CKF[:, oc, tc_ * 128:(tc_ + 1) * 128], ident[:, :],
                           r=['CKF', 'ident'], w=[('P', b)])
                    st, stc = cst.next()
                    cp(st[:, :], PSB[b][:, :], r=[('P', b)], w=[stc])
                    psput(b)
                    dma('sp', o_ckv[tc_ * 128:(tc_ + 1) * 128, :], st[:, :], r=[stc])
            else:
                for tc_ in range(4):
                    st, stc = cst.next()
                    dma('sp', st[:, :], c_ckv[tc_ * 128:(tc_ + 1) * 128, :], w=[stc])
                    b = psget()
                    for oc in range(4):
                        tp(PSB[b][:, oc * 128:(oc + 1) * 128], st[:, oc * 128:(oc + 1) * 128], ident[:, :],
                           r=[stc, 'ident'], w=[('P', b)])
                    cp(CKVN[:, :, n + tc_ * 128:n + (tc_ + 1) * 128], PSB[b][:, :].rearrange("p (o t) -> p o t", o=4),
                       r=[('P', b)], w=['CKVN'])
                    psput(b)
            blk, wc = wblk(even_w_in, 0, 16, 1024, 64)
            gkr = VT[:, 139:140]
            for (lt, nn) in tiles:
                b = psget()
                for k in range(16):
                    sap, sc = xsrc(k, lt, nn)
                    mm(PSB[b][0:64, 0:nn], blk[:, k, :], sap, k == 0, k == 15, r=[wc] + sc, w=[('P', b)])
                act(SQR[0:64, lt:lt + nn], PSB[b][0:64, 0:nn], AF.Square, r=[('P', b)], w=['SQR'])
                if latent:
                    t0_, t0c = tring.next()
                    act(t0_[:, 0:nn], PSB[b][:, 0:nn], AF.Identity, r=[('P', b), 'VT'], w=[t0c], scale=gkr)
                    rope_apply(t0_[0:64, 0:nn], [t0c], 64, nn, PM_, PMC, COS, SIN, 'ROPE', lt, None, None,
                               KRG[0:64, lt:lt + nn], ['KRG'], srcf=t0_[:, 0:nn], outf=KRG[:, lt:lt + nn])
                else:
                    act(KRG[:, lt:lt + nn], PSB[b][:, 0:nn], AF.Identity, r=[('P', b), 'VT'], w=['KRG'], scale=gkr)
                    S.add('act', lambda e, o=KRAW[:, lt:lt + nn], i=PSB[b][:, 0:nn]: e.copy(out=o, in_=i), r=[('P', b)], w=['KRAW'])
                psput(b)
            if not latent:
                out_kT(KRAW, ['KRAW'], 64, o_kr, 0)
            else:
                b = ctx_kT(c_kr, 0, 64, None, None, 64)
                act(SQR[0:64, n:n + 512], PSB[b][0:64, :], AF.Square, r=[('P', b)], w=['SQR'])
                act(KRG[:, n:n + 512], PSB[b][:, :], AF.Identity, r=[('P', b), 'VT'], w=['KRG'], scale=gkr)
                psput(b)
            if mstop <= 2:
                return
            for h in range(8 if mstop > 3 else 1):
                QN, QR, KN, KR, VH = msets[h % 2]
                hs = h % 2
                blk, wc = wblk(w_q_up, 0, 4, h * 192, 192)
                for (lt, nn) in tiles:
                    ba = psget(); bb = psget()
                    for k in range(4):
                        mm(PSB[ba][:, 0:nn], blk[:, k, 0:128], CQN[:, k, lt:lt + nn], k == 0, k == 3, r=[wc, 'CQN'], w=[('P', ba)])
                    for k in range(4):
                        mm(PSB[bb][0:64, 0:nn], blk[:, k, 128:192], CQN[:, k, lt:lt + nn], k == 0, k == 3, r=[wc, 'CQN'], w=[('P', bb)])
                    rd, rdc = rstd_from2([(PSB[ba][:, 0:nn], [('P', ba)], 128, False), (PSB[bb][0:64, 0:nn], [('P', bb)], 64, False)], nn, 192.0)
                    stt(QN[:, lt:lt + nn], PSB[ba][:, 0:nn], VT[:, 136:137], rd[:, 0:nn], ALU.mult, ALU.mult,
                        r=[('P', ba), rdc, 'VT'], w=[('QNm', hs)])
                    psput(ba)
                    t0_, t0c = tring.next()
                    act(t0_[:, 0:nn], PSB[bb][:, 0:nn], AF.Identity, r=[('P', bb), 'VT'], w=[t0c], scale=VT[:, 137:138])
                    psput(bb)
                    if latent:
                        rope_apply(t0_[0:64, 0:nn], [t0c], 64, nn, PM_, PMC, COS, SIN, 'ROPE', lt, rd, rdc,
                                   QR[0:64, lt:lt + nn], [('QRm', hs)], srcf=t0_[:, 0:nn], outf=QR[:, lt:lt + nn])
                    else:
                        tt(QR[:, lt:lt + nn], t0_[:, 0:nn], rd[:, 0:nn], ALU.mult, r=[t0c, rdc], w=[('QRm', hs)])
                blk, wc = wblk(w_kv_up, 0, 4, h * 256, 256)
                for kt in range(nk // 512):
                    ba = psget()
                    for k in range(4):
                        mm(PSB[ba][:, :], blk[:, k, 0:128], CKVN[:, k, kt * 512:(kt + 1) * 512], k == 0, k == 3, r=[wc, 'CKVN'], w=[('P', ba)])
                    rd, rdc = rstd_from2([(PSB[ba][:, :], [('P', ba)], 128, False), (SQR[0:64, kt * 512:(kt + 1) * 512], ['SQR'], 64, True)], 512, 192.0)
                    stt(KN[:, kt * 512:(kt + 1) * 512], PSB[ba][:, :], VT[:, 138:139], rd[:, :], ALU.mult, ALU.mult,
                        r=[('P', ba), rdc, 'VT'], w=[('KNm', hs)])
                    psput(ba)
                    tt(KR[:, kt * 512:(kt + 1) * 512], KRG[:, kt * 512:(kt + 1) * 512], rd[:, :], ALU.mult,
                       r=['KRG', rdc], w=[('KRm', hs)])
                for t4 in range(nkc // 4):
                    b = psget()
                    for j in range(4):
                        kc = t4 * 4 + j
                        for k in range(4):
                            mm(PSB[b][:, j * 128:(j + 1) * 128], CKVN[:, k, kc * 128:(kc + 1) * 128], blk[:, k, 128:256],
                               k == 0, k == 3, r=[wc, 'CKVN'], w=[('P', b)])
                    S.add('act', lambda e, o=VH[:, t4 * 4:(t4 + 1) * 4, :], i=PSB[b][:, :].rearrange("p (j c) -> p j c", j=4): e.copy(out=o, in_=i),
                          r=[('P', b)], w=[('VHm', hs)])
                    psput(b)
                for qt, (lt, nn) in enumerate(tiles):
                    attn([(QN[:, lt:lt + nn], [('QNm', hs)]), (QR[0:64, lt:lt + nn], [('QRm', hs)])],
                         [lambda kc, KN=KN, hs=hs: (KN[:, kc * 128:(kc + 1) * 128], [('KNm', hs)]), lambda kc, KR=KR, hs=hs: (KR[0:64, kc * 128:(kc + 1) * 128], [('KRm', hs)])],
                         lambda kc, VH=VH, hs=hs: (VH[:, kc, :], [('VHm', hs)]), kchunks(qt), nn, INV192,
                         OT[:, h, lt:lt + nn], [('OT', h)], ptring, bring)
                if hookh is not None:
                    hookh()
            if mstop <= 4:
                return
            S.barrier()
            areset(base_off)
            sets = []
            for _ in range(2):
                sets.append((ab([n], BF16), ab([nk], BF16), ab([nkc, 128], BF16)))
            KF = ab([512], F32) if not latent else None
            for h in range(8):
                QN, KN, VH = sets[h % 2]
                sfx = h % 2
                blk, wc = wblk(even_w_in, 0, 16, 1088 + h * 128, 128)
                for (lt, nn) in tiles:
                    b = psget()
                    for k in range(16):
                        sap, sc = xsrc(k, lt, nn)
                        mm(PSB[b][:, 0:nn], blk[:, k, :], sap, k == 0, k == 15, r=[wc] + sc, w=[('P', b)])
                    rd, rdc = rstd_from2([(PSB[b][:, 0:nn], [('P', b)], 128, False)], nn, 128.0)
                    stt(QN[:, lt:lt + nn], PSB[b][:, 0:nn], VT[:, 140:141], rd[:, 0:nn], ALU.mult, ALU.mult,
                        r=[('P', b), rdc, 'VT'], w=[('QNn', sfx)])
                    psput(b)
                blk, wc = wblk(even_w_in, 0, 16, 2112 + h * 128, 128)
                for (lt, nn) in tiles:
                    b = psget()
                    for k in range(16):
                        sap, sc = xsrc(k, lt, nn)
                        mm(PSB[b][:, 0:nn], blk[:, k, :], sap, k == 0, k == 15, r=[wc] + sc, w=[('P', b)])
                    rd, rdc = rstd_from2([(PSB[b][:, 0:nn], [('P', b)], 128, False)], nn, 128.0)
                    if latent:
                        stt(KN[:, lt:lt + nn], PSB[b][:, 0:nn], VT[:, 141:142], rd[:, 0:nn], ALU.mult, ALU.mult,
                            r=[('P', b), rdc, 'VT'], w=[('KNn', sfx)])
                    else:
                        stt(KF[:, lt:lt + nn], PSB[b][:, 0:nn], VT[:, 141:142], rd[:, 0:nn], ALU.mult, ALU.mult,
                            r=[('P', b), rdc, 'VT'], w=['KF'])
                        S.add('act', lambda e, o=KN[:, lt:lt + nn], i=KF[:, lt:lt + nn]: e.copy(out=o, in_=i), r=['KF'], w=[('KNn', sfx)])
                    psput(b)
                if not latent:
                    out_kT(KF, ['KF'], 128, o_nak, h * 128)
                else:
                    b = ctx_kT(c_nak, h * 128, 128, None, None, 128)
                    cp(KN[:, n:n + 512], PSB[b][:, :], r=[('P', b)], w=[('KNn', sfx)])
                    psput(b)
                blk, wc = wblk(even_w_in, 0, 16, 3136 + h * 128, 128)
                tok_major_v(blk, wc, slice(0, 128), VH, ('VHn', sfx), xsrc, 16, None if latent else o_nav, h * 128)
                if latent:
                    dma('pool', VH[:, noc:noc + 4, :], c_nav[:, h * 128:(h + 1) * 128].rearrange("(j p) c -> p j c", p=128), w=[('VHn', sfx)])
                for qt, (lt, nn) in enumerate(tiles):
                    bf = (lambda kc, qt_, h=h: nab[h * 1024 + kc * 128:h * 1024 + (kc + 1) * 128, qt_ * 512:(qt_ + 1) * 512]) if latent else None
                    attn([(QN[:, lt:lt + nn], [('QNn', sfx)])],
                         [lambda kc, KN=KN, sfx=sfx: (KN[:, kc * 128:(kc + 1) * 128], [('KNn', sfx)])],
                         lambda kc, VH=VH, sfx=sfx: (VH[:, kc, :], [('VHn', sfx)]), kchunks(qt, bf), nn, INV128,
                         OT[:, 8 + h, lt:lt + nn], [('OT', 8 + h)], ptring, bring)
                if hookh is not None:
                    hookh()
            w_out = even_w_out
        else:
            ksets = [(ab([nk], BF16), ab([nkc, 128], BF16)) for _ in range(2)]
            qsets = [ab([n], BF16) for _ in range(2)]
            KF = ab([512], F32) if not latent else None
            own_range = [(0, 4), (3, 7)]
            qi = 0
            for kvh in range(4):
                KN, VH = ksets[kvh % 2]
                ks = kvh % 2
                blk, wc = wblk(odd_w_in, 0, 16, 2048 + kvh * 128, 128)
                for (lt, nn) in tiles:
                    b = psget()
                    for k in range(16):
                        sap, sc = xsrc(k, lt, nn)
                        mm(PSB[b][:, 0:nn], blk[:, k, :], sap, k == 0, k == 15, r=[wc] + sc, w=[('P', b)])
                    rd, rdc = rstd_from2([(PSB[b][:, 0:nn], [('P', b)], 128, False)], nn, 128.0)
                    if latent:
                        t0_, t0c = tring.next()
                        stt(t0_[:, 0:nn], PSB[b][:, 0:nn], VT[:, 143:144], rd[:, 0:nn], ALU.mult, ALU.mult,
                            r=[('P', b), rdc, 'VT'], w=[t0c])
                        rope_apply(t0_[:, 0:nn], [t0c], 128, nn, PM_, PMC, COS, SIN, 'ROPE', lt, None, None,
                                   KN[:, lt:lt + nn], [('KNg', ks)])
                    else:
                        stt(KF[:, lt:lt + nn], PSB[b][:, 0:nn], VT[:, 143:144], rd[:, 0:nn], ALU.mult, ALU.mult,
                            r=[('P', b), rdc, 'VT'], w=['KF'])
                        S.add('act', lambda e, o=KN[:, lt:lt + nn], i=KF[:, lt:lt + nn]: e.copy(out=o, in_=i), r=['KF'], w=[('KNg', ks)])
                    psput(b)
                if not latent:
                    out_kT(KF, ['KF'], 128, o_gk, kvh * 128)
                else:
                    b = ctx_kT(c_gk, kvh * 128, 128, None, None, 128)
                    cp(KN[:, n:n + 512], PSB[b][:, :], r=[('P', b)], w=[('KNg', ks)])
                    psput(b)
                blk, wc = wblk(odd_w_in, 0, 16, 2560 + kvh * 128, 128)
                tok_major_v(blk, wc, slice(0, 128), VH, ('VHg', ks), xsrc, 16, None if latent else o_gv, kvh * 128)
                if latent:
                    dma('pool', VH[:, noc:noc + 4, :], c_gv[:, kvh * 128:(kvh + 1) * 128].rearrange("(j p) c -> p j c", p=128), w=[('VHg', ks)])
                for gq in range(4):
                    h = kvh * 4 + gq
                    QN = qsets[qi % 2]
                    qs = qi % 2
                    qi += 1
                    blk, wc = wblk(odd_w_in, 0, 16, h * 128, 128)
                    for (lt, nn) in tiles:
                        b = psget()
                        for k in range(16):
                            sap, sc = xsrc(k, lt, nn)
                            mm(PSB[b][:, 0:nn], blk[:, k, :], sap, k == 0, k == 15, r=[wc] + sc, w=[('P', b)])
                        rd, rdc = rstd_from2([(PSB[b][:, 0:nn], [('P', b)], 128, False)], nn, 128.0)
                        if latent:
                            t0_, t0c = tring.next()
                            stt(t0_[:, 0:nn], PSB[b][:, 0:nn], VT[:, 142:143], rd[:, 0:nn], ALU.mult, ALU.mult,
                                r=[('P', b), rdc, 'VT'], w=[t0c])
                            rope_apply(t0_[:, 0:nn], [t0c], 128, nn, PM_, PMC, COS, SIN, 'ROPE', lt, None, None,
                                       QN[:, lt:lt + nn], [('QNg', qs)])
                        else:
                            stt(QN[:, lt:lt + nn], PSB[b][:, 0:nn], VT[:, 142:143], rd[:, 0:nn], ALU.mult, ALU.mult,
                                r=[('P', b), rdc, 'VT'], w=[('QNg', qs)])
                        psput(b)
                    for qt, (lt, nn) in enumerate(tiles):
                        bf = (lambda kc, qt_: wmask[kc * 128:(kc + 1) * 128, qt_ * 512:(qt_ + 1) * 512]) if latent else None
                        attn([(QN[:, lt:lt + nn], [('QNg', qs)])],
                             [lambda kc, KN=KN, ks=ks: (KN[:, kc * 128:(kc + 1) * 128], [('KNg', ks)])],
                             lambda kc, VH=VH, ks=ks: (VH[:, kc, :], [('VHg', ks)]), kchunks(qt, bf, own_range), nn, INV128,
                             OT[:, h, lt:lt + nn], [('OT', h)], ptring, bring, sink=sinkE[:, h:h + 1], ONES=ONES)
            w_out = odd_w_out
        xr = Ring('XR', [ab([512], F32) for _ in range(2)])
        for oc in range(16):
            blk, wc = wblk(w_out, 0, 16, oc * 128, 128)
            for (lt, nn) in tiles:
                b = psget()
                for k in range(16):
                    mm(PSB[b][:, 0:nn], blk[:, k, :], OT[:, k, lt:lt + nn], k == 0, k == 15, r=[wc, ('OT', k)], w=[('P', b)])
                xt_, xtc = xr.next()
                cells = xdc(oc, T0 + lt, T0 + lt + nn)
                dma('sp', xt_[:, 0:nn], XD[:, oc, T0 + lt:T0 + lt + nn], r=cells, w=[xtc])
                stt(xt_[:, 0:nn], PSB[b][:, 0:nn], Gsc(1, oc, g), xt_[:, 0:nn], ALU.mult, ALU.add,
                    r=[('P', b), ('GMOD', 1), xtc], w=[xtc])
                psput(b)
                dma('sp', XD[:, oc, T0 + lt:T0 + lt + nn], xt_[:, 0:nn], r=[xtc], w=cells)

    def mixer(l, hookh=None):
        for (t0, nn, g) in TILES:
            prepass(1, lambda fc, t0=t0, nn=nn: (X[:, fc, t0:t0 + nn], xc(fc, t0, t0 + nn)), t0, nn, g)
        for ti, (t0, nn, g) in enumerate(TILES):
            dma('sp', XD[:, :, t0:t0 + nn], X[:, :, t0:t0 + nn], r=[c for fc in range(16) for c in xc(fc, t0, t0 + nn)],
                w=[c for fc in range(16) for c in xdc(fc, t0, t0 + nn)])
        mixer_pass(l, 512, 1024, True, hookh)
        mixer_pass(l, 0, 512, False)
        S.barrier()
        for ti, (t0, nn, g) in enumerate(TILES):
            dma('sp', X[:, :, t0:t0 + nn], XD[:, :, t0:t0 + nn], r=[c for fc in range(16) for c in xdc(fc, t0, t0 + nn)],
                w=[c for fc in range(16) for c in xc(fc, t0, t0 + nn)])
        S.barrier()

    if n_sub == 6 and not skip_ffn and cfg.get('pipe', True):
        bm = psget()
        mods_mm(0, 0, 48, bm)
        mods_fin(0, bm, [0])
        cuts0 = [48 + (96 * q) // 44 for q in range(45)]
        ffn(0, 0, hook2=lambda c: mods_mm(0, cuts0[c], cuts0[c + 1], bm))
        mods_fin(0, bm, [1, 2])
        psput(bm)
        bm1 = psget()
        hcnt = [0]
        cuts1 = [(144 * q) // 16 for q in range(17)]

        def hk():
            q = hcnt[0]
            hcnt[0] += 1
            mods_mm(1, cuts1[q], cuts1[q + 1], bm1)
        mixer(0, hk)
        assert hcnt[0] == 16, hcnt[0]
        ffn(0, 1)
        mods_fin(1, bm1, [0, 1, 2])
        psput(bm1)
        ffn(1, 0)
        mixer(1)
        ffn(1, 1)
    else:
        sub = 0
        for l in range(2):
            if sub < n_sub:
                mods(l)
            if sub < n_sub and not skip_ffn:
                ffn(l, 0)
            sub += 1
            if sub < n_sub and cfg.get('mixer', True):
                mixer(l)
            sub += 1
            if sub < n_sub and not skip_ffn:
                ffn(l, 1)
            sub += 1

    S.barrier()
    for t in range(12):
        sl = t % 3
        st = STGF[:, 4096 + sl * 2048: 4096 + (sl + 1) * 2048]
        for f4 in range(4):
            b = psget()
            for q in range(4):
                fc = f4 * 4 + q
                tp(PSB[b][:, q * 128:(q + 1) * 128], X[:, fc, t * 128:(t + 1) * 128], ident[:, :],
                   r=xc(fc, t * 128, (t + 1) * 128) + ['ident'], w=[('P', b)])
            if f4 % 2 == 0:
                cp(st[:, f4 * 512:(f4 + 1) * 512], PSB[b][:, :], r=[('P', b)], w=[('STGO', sl, f4)])
            else:
                S.add('act', lambda e, o=st[:, f4 * 512:(f4 + 1) * 512], i=PSB[b][:, :]: e.copy(out=o, in_=i),
                      r=[('P', b)], w=[('STGO', sl, f4)])
            psput(b)
        dma('sp', y_out[t * 128:(t + 1) * 128, :], st, r=[('STGO', sl, f4) for f4 in range(4)])

    S.emit(cfg.get('limit'))
    es.close()
    return nc


def _prep_inputs(inp):
    f = np.float32
    g = lambda k: np.asarray(inp[k], dtype=f)
    ident = np.eye(128, dtype=f)
    def tables(R):
        half = R // 2
        nf = half // 2
        t = np.arange(1024)
        inv = (10000.0 ** (-np.arange(nf, dtype=np.float64) / nf))
        ang_r = (t // 64)[None, :] * inv[:, None]
        ang_c = (t % 64)[None, :] * inv[:, None]
        ang = np.concatenate([ang_r, ang_r, ang_c, ang_c], axis=0)
        P = np.zeros((R, R), f)
        for base in (0, half):
            for i in range(nf):
                P[base + nf + i, base + i] = -1.0
                P[base + i, base + nf + i] = 1.0
        return np.cos(ang).astype(f), np.sin(ang).astype(f), P
    gc, gs, p128 = tables(128)
    mc, ms, p64 = tables(64)
    idx = np.arange(1024)
    wm = np.where(np.abs(idx[None, :] - idx[:, None]) <= 128, 0.0, NEG).astype(f)
    rpb = g("na_rpb")[0]
    kr, kc = idx // 64, idx % 64
    qr, qc = idx // 64, idx % 64
    rs = np.clip(qr - 4, 0, 8)
    ws = np.clip(qc - 8, 0, 48)
    rowv = (kr[:, None] >= rs[None, :]) & (kr[:, None] < rs[None, :] + 8)
    colv = (kc[:, None] >= ws[None, :]) & (kc[:, None] < ws[None, :] + 16)
    valid = rowv & colv
    ro = np.clip(kr[:, None] - qr[None, :] + 7, 0, 14)
    co = np.clip(kc[:, None] - qc[None, :], -15, 15) + 15
    flat = ro * 31 + co
    flat = np.where(valid, flat, 15 * 31)
    rp = np.concatenate([rpb.reshape(8, -1), np.full((8, 1), NEG, f)], axis=1)
    nabias = np.ascontiguousarray(rp[:, flat].reshape(8 * 1024, 1024))
    shared = {
        "ident": ident, "ropeg_c": gc, "ropeg_s": gs, "ropem_c": mc, "ropem_s": ms, "p128": p128, "p64": p64,
        "wmask": wm, "nab": nabias,
        "sinkb": np.ascontiguousarray(np.broadcast_to(g("gqa_sink")[0][None, :], (128, 16))),
        "ada_w": g("ada_w").reshape(4096, 18432),
        "ffn_w_in": g("ffn_w_in").reshape(4 * 2048, 11264),
        "ffn_w_out": g("ffn_w_out").reshape(4 * 5632, 2048),
        "even_w_in": g("even_w_in")[0], "even_w_out": g("even_w_out")[0],
        "w_q_up": g("mla_w_q_up")[0], "w_kv_up": g("mla_w_kv_up")[0],
        "odd_w_in": g("odd_w_in")[0], "odd_w_out": g("odd_w_out")[0],
    }
    maps = []
    for i in range(8):
        vec = np.zeros((640, 128), f)
        vec[0:16] = g("c_ctx").reshape(16, 128)
        vec[16:32] = g("c")[i].reshape(16, 128)
        vec[32:128] = g("norm_g").reshape(96, 128)
        vec[128:132] = g("mla_q_norm")[0].reshape(4, 128)
        vec[132:136] = g("mla_kv_norm")[0].reshape(4, 128)
        qk = g("mla_qk_norm")[0]
        vec[136] = qk[0, 0:128]; vec[137, 0:64] = qk[0, 128:192]
        vec[138] = qk[1, 0:128]; vec[139, 0:64] = qk[1, 128:192]
        vec[140:142] = g("na_qk_norm")[0]
        vec[142:144] = g("gqa_qk_norm")[0]
        vec[256:544] = g("ada_b").reshape(288, 128)
        m = dict(shared)
        m["xin"] = np.concatenate([g("x_prompt")[2 * i:2 * i + 2].reshape(512, D), g("x_sample")[i]], axis=0)
        m["c_ckv"] = g("cache_mla_ckv")[i, 0]; m["c_kr"] = g("cache_mla_krope")[i, 0]
        m["c_nak"] = g("cache_na_k")[i, 0].reshape(512, 1024); m["c_nav"] = g("cache_na_v")[i, 0].reshape(512, 1024)
        m["c_gk"] = g("cache_gqa_k")[i, 0].reshape(512, 512); m["c_gv"] = g("cache_gqa_v")[i, 0].reshape(512, 512)
        m["vecs"] = vec
        maps.append(m)
    return maps


def _assemble(results):
    n = len(results)
    f = np.float32
    yp = np.zeros((16, 256, D), f); ys = np.zeros((8, 1024, D), f)
    ckv = np.zeros((16, 1, 256, 512), f); kr = np.zeros((16, 1, 256, 64), f)
    nak = np.zeros((16, 1, 256, 8, 128), f); nav = np.zeros((16, 1, 256, 8, 128), f)
    gk = np.zeros((16, 1, 256, 4, 128), f); gv = np.zeros((16, 1, 256, 4, 128), f)
    for i in range(n):
        r = results[i]
        yp[2 * i:2 * i + 2] = r["y"][0:512].reshape(2, 256, D)
        ys[i] = r["y"][512:]
        ckv[2 * i:2 * i + 2, 0] = r["o_ckv"].reshape(2, 256, 512)
        kr[2 * i:2 * i + 2, 0] = r["o_kr"].reshape(2, 256, 64)
        nak[2 * i:2 * i + 2, 0] = r["o_nak"].reshape(2, 256, 8, 128)
        nav[2 * i:2 * i + 2, 0] = r["o_nav"].reshape(2, 256, 8, 128)
        gk[2 * i:2 * i + 2, 0] = r["o_gk"].reshape(2, 256, 4, 128)
        gv[2 * i:2 * i + 2, 0] = r["o_gv"].reshape(2, 256, 4, 128)
    return (yp, ys, ckv, kr, nak, nav, gk, gv)


def kernel(**inputs):
    maps = _prep_inputs(inputs)
    nc = build({})
    res = run_bass_kernel_spmd(nc, maps, core_ids=list(range(8)))
    return _assemble(res.results)
```

```python
import contextlib
import numpy as np
import concourse.bass as bass
import concourse.mybir as mybir
from concourse.bass_utils import run_bass_kernel_spmd

F32 = mybir.dt.float32
BF16 = mybir.dt.bfloat16
AF = mybir.ActivationFunctionType
ALU = mybir.AluOpType

ENGS = ('pe', 'act', 'dve', 'pool', 'sp')
SAME_ENG_SYNC = True
NDSEM = 8
NSLOT = 6
GROUPS = [5, 5, 5, 5, 5, 5, 5, 5, 4]
GMAX = 5
EPS = 1e-6
NEG = -30000.0

D = 2048
NTOK = 1536
DFF = 5632


class _Op:
    __slots__ = ('eng', 'fn', 'dma', 'users', 'id', 'key', 'deps', 'sem', 'val', 'selfwait')


class Sched:
    def __init__(self, nc):
        self.nc = nc
        self.ops = {e: [] for e in ENGS}
        self.cells = {}
        self.uid = 0
        self.pending = {e: [] for e in ENGS}
        self.dmas = []

    def add(self, eng, fn, r=(), w=(), dma=False):
        op = _Op()
        op.eng = eng; op.fn = fn; op.dma = dma; op.users = False
        op.id = self.uid; self.uid += 1
        op.key = ('d', op.id) if dma else eng
        op.sem = None; op.val = 0; op.selfwait = 0
        deps = {}
        cells = self.cells
        for c in r:
            st = cells.get(c)
            if st is not None and st[0] is not None:
                deps[st[0].id] = st[0]
        for c in w:
            st = cells.get(c)
            if st is not None:
                if st[0] is not None:
                    deps[st[0].id] = st[0]
                for o in st[1].values():
                    deps[o.id] = o
        for c in r:
            st = cells.get(c)
            if st is None:
                cells[c] = [None, {op.key: op}]
            else:
                st[1][op.key] = op
        for c in w:
            cells[c] = [op, {}]
        for o in self.pending[eng]:
            deps[o.id] = o
        self.pending[eng] = []
        dl = []
        for d in deps.values():
            if d is op:
                continue
            if (not d.dma) and (not dma) and d.eng == eng and (eng == 'pe' or not SAME_ENG_SYNC):
                continue
            d.users = True
            dl.append(d)
        op.deps = dl
        self.ops[eng].append(op)
        if dma:
            self.dmas.append(op)
        return op

    def barrier(self):
        last = []
        for e in ENGS:
            for op in reversed(self.ops[e]):
                if not op.dma:
                    last.append(op)
                    break
        last.extend(self.dmas)
        self.dmas = []
        for e in ENGS:
            self.pending[e] = self.pending[e] + list(last)

    def emit(self, limit=None):
        nc = self.nc
        if limit is not None:
            for e in ENGS:
                self.ops[e] = [o for o in self.ops[e] if o.id < limit]
        engsem = {e: nc.alloc_semaphore("s_" + e) for e in ENGS}
        dmasem = {q: [nc.alloc_semaphore("d_%s%d" % (q, i)) for i in range(NDSEM)] for q in ('sp', 'pool')}
        semkey = {}
        final = {}
        for e in ENGS:
            cnt = 0
            di = 0
            duse = [0] * NDSEM
            for op in self.ops[e]:
                if op.dma:
                    i = di % NDSEM
                    di += 1
                    duse[i] += 1
                    op.sem = dmasem[e][i]
                    semkey[id(op)] = (e, i)
                    op.val = 16 * duse[i]
                    op.selfwait = 16 * (duse[i] - 1)
                    final[(e, i)] = (op.sem, op.val)
                elif op.users:
                    cnt += 1
                    op.sem = engsem[e]
                    semkey[id(op)] = (e, -1)
                    op.val = cnt
            assert cnt < 60000, (e, cnt)
        ops = self.ops

        def run(e, eng):
            waited = {}
            for op in ops[e]:
                for d in op.deps:
                    k = semkey[id(d)]
                    if waited.get(k, 0) < d.val:
                        eng.wait_ge(d.sem, d.val)
                        waited[k] = d.val
                if op.dma and op.selfwait > 0:
                    k = semkey[id(op)]
                    if waited.get(k, 0) < op.selfwait:
                        eng.wait_ge(op.sem, op.selfwait)
                        waited[k] = op.selfwait
                ins = op.fn(eng)
                if op.dma:
                    ins.then_inc(op.sem, 16)
                elif op.users:
                    ins.then_inc(op.sem, 1)
            if e == 'sp':
                for k, (sem, val) in final.items():
                    if waited.get(k, 0) < val:
                        eng.wait_ge(sem, val)

        with nc.Block() as block:
            @block.tensor
            def _(eng):
                run('pe', eng)

            @block.scalar
            def _(eng):
                run('act', eng)

            @block.vector
            def _(eng):
                run('dve', eng)

            @block.gpsimd
            def _(eng):
                run('pool', eng)

            @block.sync
            def _(eng):
                run('sp', eng)


class Ring:
    def __init__(self, name, aps):
        self.name = name
        self.aps = aps
        self.i = 0

    def next(self):
        i = self.i % len(self.aps)
        self.i += 1
        return self.aps[i], (self.name, i)


def build(cfg):
    n_sub = cfg.get('n_sub', 6)
    debug = cfg.get('debug', False)
    skip_ffn = cfg.get('skip_ffn', False)
    mstop = cfg.get('mstop', 99)
    nc = bass.Bass("TRN2", target_bir_lowering=False)
    S = Sched(nc)
    es = contextlib.ExitStack()

    def din(name, shape):
        return nc.dram_tensor(name, list(shape), F32, kind="ExternalInput").ap()

    def dout(name, shape):
        return nc.dram_tensor(name, list(shape), F32, kind="ExternalOutput").ap()

    def sb(name, shape, dt):
        return es.enter_context(nc.sbuf_tensor("sb_" + name, list(shape), dt))

    xin = din("xin", [NTOK, D])
    c_ckv = din("c_ckv", [512, 512]); c_kr = din("c_kr", [512, 64])
    c_nak = din("c_nak", [512, 1024]); c_nav = din("c_nav", [512, 1024])
    c_gk = din("c_gk", [512, 512]); c_gv = din("c_gv", [512, 512])
    vecs = din("vecs", [640, 128])
    sinkb = din("sinkb", [128, 16])
    ident_d = din("ident", [128, 128])
    ropeg_c = din("ropeg_c", [128, 1024]); ropeg_s = din("ropeg_s", [128, 1024])
    ropem_c = din("ropem_c", [64, 1024]); ropem_s = din("ropem_s", [64, 1024])
    p128_d = din("p128", [128, 128]); p64_d = din("p64", [64, 64])
    wmask = din("wmask", [1024, 1024])
    nab = din("nab", [8 * 1024, 1024])
    ada_w = din("ada_w", [2 * 2048, 18432])
    ffn_w_in = din("ffn_w_in", [4 * 2048, 11264])
    ffn_w_out = din("ffn_w_out", [4 * 5632, 2048])
    even_w_in = din("even_w_in", [2048, 4160]); even_w_out = din("even_w_out", [2048, 2048])
    w_q_up = din("w_q_up", [512, 1536]); w_kv_up = din("w_kv_up", [512, 2048])
    odd_w_in = din("odd_w_in", [2048, 3072]); odd_w_out = din("odd_w_out", [2048, 2048])

    y_out = dout("y", [NTOK, D])
    o_ckv = dout("o_ckv", [512, 512]); o_kr = dout("o_kr", [512, 64])
    o_nak = dout("o_nak", [512, 1024]); o_nav = dout("o_nav", [512, 1024])
    o_gk = dout("o_gk", [512, 512]); o_gv = dout("o_gv", [512, 512])
    XD = nc.dram_tensor("xd_scratch", [128, 16, NTOK], F32, kind="ExternalOutput").ap()

    XE = 16 * NTOK * 2
    HE = GMAX * NTOK
    BIG = sb("BIG", [128, XE + HE], BF16)
    X = BIG[:, 0:XE].bitcast(F32).rearrange("p (c t) -> p c t", c=16)
    H = BIG[:, XE:XE + HE].rearrange("p (j t) -> p j t", j=GMAX)
    XMT = sb("XM", [128, 16 * NTOK], BF16)
    XM = XMT[:, :].rearrange("p (c t) -> p c t", c=16)
    WR = sb("WR", [128, NSLOT, 2048], BF16)
    VT = sb("VT", [128, 640], F32)
    MODS = sb("MODS", [128, 144, 2], F32)
    AMOD = sb("AMOD", [128, 3, 16, 2], F32)
    GMOD = sb("GMOD", [128, 3, 16, 2], F32)
    ident = sb("ident", [128, 128], F32)
    ones_f = sb("ones_f", [128, 128], F32)
    ones_b = sb("ones_b", [128, 128], BF16)
    sT = sb("sT", [128, 32], BF16)
    sinkE = sb("sinkE", [128, 16], F32)
    p128 = sb("p128", [128, 128], F32)
    p64 = sb("p64", [64, 64], F32)
    TR = sb("TR", [128, 4, 512], F32)
    RSR = sb("RSR", [128, 2, 512], F32)
    RDR = sb("RDR", [128, 3, 512], F32)
    tring = Ring('TR', [TR[:, i, :] for i in range(4)])
    rsring = Ring('RS', [RSR[:, i, :] for i in range(2)])
    rdring = Ring('RD', [RDR[:, i, :] for i in range(3)])
    PSB = [es.enter_context(nc.psum_tensor("ps%d" % i, [128, 512], F32)) for i in range(8)]
    psfree = list(range(8))

    def psget():
        b = psfree.pop(0)
        return b

    def psput(b):
        psfree.append(b)

    def dma(q, out, in_, r=(), w=()):
        return S.add(q, lambda e, o=out, i=in_: e.dma_start(out=o, in_=i), r=r, w=w, dma=True)

    def mm(ps, lhsT, rhs, start, stop, r, w):
        return S.add('pe', lambda e, a=ps, b=lhsT, c=rhs, s0=start, s1=stop: e.matmul(a, b, c, start=s0, stop=s1), r=r, w=w)

    def tp(ps, in_, idn, r, w):
        return S.add('pe', lambda e, a=ps, b=in_, c=idn: e.transpose(a, b, c), r=r, w=w)

    def act(out, in_, func, r, w, bias=None, scale=None):
        kw = {}
        if bias is not None:
            kw['bias'] = bias
        if scale is not None:
            kw['scale'] = scale
        return S.add('act', lambda e, o=out, i=in_, f=func, k=kw: e.activation(out=o, in_=i, func=f, **k), r=r, w=w)

    def tt(out, in0, in1, op, r, w, eng='dve'):
        return S.add(eng, lambda e, o=out, a=in0, b=in1, p=op: e.tensor_tensor(out=o, in0=a, in1=b, op=p), r=r, w=w)

    def ts(out, in0, s1, s2, op0, op1, r, w, eng='dve'):
        if s2 is None:
            return S.add(eng, lambda e, o=out, a=in0, x=s1, p=op0: e.tensor_scalar(out=o, in0=a, scalar1=x, scalar2=None, op0=p), r=r, w=w)
        return S.add(eng, lambda e, o=out, a=in0, x=s1, y=s2, p=op0, q=op1: e.tensor_scalar(out=o, in0=a, scalar1=x, scalar2=y, op0=p, op1=q), r=r, w=w)

    def stt(out, in0, sc, in1, op0, op1, r, w, eng='dve'):
        return S.add(eng, lambda e, o=out, a=in0, s=sc, b=in1, p=op0, q=op1: e.scalar_tensor_tensor(out=o, in0=a, scalar=s, in1=b, op0=p, op1=q), r=r, w=w)

    def cp(out, in_, r, w, eng='dve'):
        return S.add(eng, lambda e, o=out, i=in_: e.tensor_copy(out=o, in_=i), r=r, w=w)

    def recip(out, in_, r, w):
        return S.add('dve', lambda e, o=out, i=in_: e.reciprocal(out=o, in_=i), r=r, w=w)

    def memset(ap, v, w):
        return S.add('dve', lambda e, a=ap, c=v: e.memset(a, c), w=w)

    wslot = [0]

    def wload(src, a, b):
        s = wslot[0] % NSLOT
        wslot[0] += 1
        view = WR[:, s, 0:a * b].rearrange("p (a b) -> p a b", a=a)
        dma('pool', view, src, w=[('W', s)])
        return view, ('W', s)

    def wblk(wd, row0, nk, c0, nc_):
        src = wd[row0:row0 + nk * 128, c0:c0 + nc_].rearrange("(k p) n -> p k n", p=128)
        return wload(src, nk, nc_)

    def xc(fc, t0, t1):
        return [('X', fc, t) for t in range(t0 // 256, (t1 + 255) // 256)]

    def xmc(fc, t0, t1):
        return [('XM', fc, t) for t in range(t0 // 256, (t1 + 255) // 256)]

    def xdc(fc, t0, t1):
        return [('XD', fc, t) for t in range(t0 // 256, (t1 + 255) // 256)]

    dma('sp', ident[:, :], ident_d[:, :], w=['ident'])
    dma('sp', p128[:, :], p128_d[:, :], w=['p128'])
    dma('sp', p64[:, :], p64_d[:, :], w=['p64'])
    dma('sp', sinkE[:, :], sinkb[:, :], w=['sinkE'])
    memset(ones_f[:, :], 1.0, ['ones_f'])
    memset(ones_b[:, :], 1.0, ['ones_b'])
    act(sinkE[:, :], sinkE[:, :], AF.Exp, r=['sinkE'], w=['sinkE'])
    STGF = XMT[:, :].bitcast(F32)
    for t in range(5):
        st = STGF[:, t * 128:(t + 1) * 128]
        dma('sp', st, vecs[t * 128:(t + 1) * 128, :], w=[('STG', t)])
        b = psget()
        tp(PSB[b][:, 0:128], st, ident[:, :], r=[('STG', t), 'ident'], w=[('P', b)])
        cp(VT[:, t * 128:(t + 1) * 128], PSB[b][:, 0:128], r=[('P', b)], w=['VT'])
        psput(b)
    act(sT[:, :], VT[:, 0:32], AF.Silu, r=['VT'], w=['sT'])

    S.barrier()
    for t in range(12):
        sl = t % 3
        st = STGF[:, 4096 + sl * 2048: 4096 + (sl + 1) * 2048]
        dma('sp', st, xin[t * 128:(t + 1) * 128, :], w=[('STGX', sl)])
        for f4 in range(4):
            b = psget()
            for q in range(4):
                fc = f4 * 4 + q
                tp(PSB[b][:, q * 128:(q + 1) * 128], st[:, fc * 128:(fc + 1) * 128], ident[:, :],
                   r=[('STGX', sl), 'ident'], w=[('P', b)])
            dst = X[:, f4 * 4:(f4 + 1) * 4, t * 128:(t + 1) * 128]
            src = PSB[b][:, :].rearrange("p (q n) -> p q n", q=4)
            wc = []
            for q in range(4):
                wc += xc(f4 * 4 + q, t * 128, (t + 1) * 128)
            if f4 % 2 == 0:
                cp(dst, src, r=[('P', b)], w=wc)
            else:
                S.add('act', lambda e, o=dst, i=src: e.copy(out=o, in_=i), r=[('P', b)], w=wc)
            psput(b)
    S.barrier()

    def mods_mm(l, oc0, oc1, b):
        PM = PSB[b]
        for oc in range(oc0, oc1):
            blk, wc = wblk(ada_w, l * 2048, 16, oc * 128, 128)
            for k in range(16):
                mm(PM[:, 2 * oc:2 * oc + 2], blk[:, k, :], sT[:, k:32:16], k == 0, k == 15,
                   r=[wc, 'sT'], w=[('P', b)])

    def mods_fin(l, b, parts):
        PM = PSB[b]
        for i in parts:
            for g in range(2):
                tt(MODS[:, 48 * i:48 * (i + 1), g], PM[:, 96 * i + g:96 * (i + 1):2],
                   VT[:, 256 + l * 144 + 48 * i:256 + l * 144 + 48 * (i + 1)], ALU.add,
                   r=[('P', b), 'VT'], w=[('MODS', i)])
            for g in range(2):
                ts(AMOD[:, i, :, g], MODS[:, (3 * i + 1) * 16:(3 * i + 2) * 16, g], 1.0, None, ALU.add, None,
                   r=[('MODS', i)], w=[('AMOD', i)])
                tt(AMOD[:, i, :, g], AMOD[:, i, :, g], VT[:, 32 + (l * 3 + i) * 16:32 + (l * 3 + i + 1) * 16], ALU.mult,
                   r=[('AMOD', i), 'VT'], w=[('AMOD', i)])
                ts(GMOD[:, i, :, g], MODS[:, (3 * i + 2) * 16:(3 * i + 3) * 16, g], 0.5 if i != 1 else 1.0, None,
                   ALU.mult, None, r=[('MODS', i)], w=[('GMOD', i)])

    def mods(l):
        b = psget()
        mods_mm(l, 0, 144, b)
        mods_fin(l, b, [0, 1, 2])
        psput(b)

    def Asc(i, fc, g):
        return AMOD[:, i, fc, g:g + 1]

    def Bsc(i, fc, g):
        return MODS[:, 3 * i * 16 + fc, g:g + 1]

    def Gsc(i, fc, g):
        return GMOD[:, i, fc, g:g + 1]

    def rstd_from(srcs, n, dim):
        b = psget()
        ps = PSB[b][:, 0:n]
        for i, (ap, cells, kp) in enumerate(srcs):
            sq, sqc = tring.next()
            act(sq[0:kp, 0:n], ap, AF.Square, r=cells, w=[sqc])
            mm(ps, ones_f[0:kp, :], sq[0:kp, 0:n], i == 0, i == len(srcs) - 1, r=[sqc, 'ones_f'], w=[('P', b)])
        rs, rsc = rsring.next()
        ts(rs[:, 0:n], ps, 1.0 / dim, EPS, ALU.mult, ALU.add, r=[('P', b)], w=[rsc])
        psput(b)
        act(rs[:, 0:n], rs[:, 0:n], AF.Sqrt, r=[rsc], w=[rsc])
        rd, rdc = rdring.next()
        recip(rd[:, 0:n], rs[:, 0:n], r=[rsc], w=[rdc])
        return rd, rdc

    def prepass(i, src_fn, t0, n, g):
        srcs = []
        for fc in range(16):
            ap, cells = src_fn(fc)
            srcs.append((ap, cells, 128))
        rd, rdc = rstd_from(srcs, n, 2048.0)
        for fc in range(16):
            ap, cells = src_fn(fc)
            tmp, tc = tring.next()
            stt(tmp[:, 0:n], ap, Asc(i, fc, g), rd[:, 0:n], ALU.mult, ALU.mult, r=cells + [rdc, ('AMOD', i)], w=[tc])
            act(XM[:, fc, t0:t0 + n], tmp[:, 0:n], AF.Identity, r=[tc, ('MODS', i)], w=xmc(fc, t0, t0 + n),
                bias=Bsc(i, fc, g))

    TILES = [(0, 512, 0), (512, 512, 1), (1024, 512, 1)]
    GROUPS_IDX = {}
    _b = 0
    for _gi, _G in enumerate(GROUPS):
        GROUPS_IDX[_b] = _gi
        _b += _G

    def ffn(l, j, hook=None, hook2=None):
        i = 0 if j == 0 else 2
        for (t0, n, g) in TILES:
            prepass(i, lambda fc, t0=t0, n=n: (X[:, fc, t0:t0 + n], xc(fc, t0, t0 + n)), t0, n, g)
        row_in = (l * 2 + j) * 2048
        row_out = (l * 2 + j) * 5632
        base = 0
        for G in GROUPS:
            for jj in range(G):
                c = base + jj
                bg, wg = wblk(ffn_w_in, row_in, 16, c * 128, 128)
                bu, wu = wblk(ffn_w_in, row_in, 16, (44 + c) * 128, 128)
                for ti, (t0, n, g) in enumerate(TILES):
                    pg = psget(); pu = psget()
                    for k in range(16):
                        mm(PSB[pg][:, :], bg[:, k, :], XM[:, k, t0:t0 + n], k == 0, k == 15,
                           r=[wg] + xmc(k, t0, t0 + n), w=[('P', pg)])
                    for k in range(16):
                        mm(PSB[pu][:, :], bu[:, k, :], XM[:, k, t0:t0 + n], k == 0, k == 15,
                           r=[wu] + xmc(k, t0, t0 + n), w=[('P', pu)])
                    sg, sgc = tring.next()
                    act(sg[:, :], PSB[pg][:, :], AF.Silu, r=[('P', pg)], w=[sgc])
                    tt(H[:, jj, t0:t0 + n], sg[:, :], PSB[pu][:, :], ALU.mult, r=[sgc, ('P', pu)], w=[('H', jj, ti)])
                    psput(pg); psput(pu)
                if hook2 is not None:
                    hook2(c)
            for ocg in range(4):
                blks = []
                for kb in range(0, G, 4):
                    nk = min(4, G - kb)
                    blks.append(wblk(ffn_w_out, row_out + (base + kb) * 128, nk, ocg * 512, 512))
                for ti, (t0, n, g) in enumerate(TILES):
                    for o4 in range(4):
                        oc = ocg * 4 + o4
                        py = psget()
                        for jj in range(G):
                            blk, wc = blks[jj // 4]
                            mm(PSB[py][:, :], blk[:, jj % 4, o4 * 128:(o4 + 1) * 128], H[:, jj, t0:t0 + n],
                               jj == 0, jj == G - 1, r=[wc, ('H', jj, ti)], w=[('P', py)])
                        stt(X[:, oc, t0:t0 + n], PSB[py][:, :], Gsc(i, oc, g), X[:, oc, t0:t0 + n], ALU.mult, ALU.add,
                            r=[('P', py), ('GMOD', i)] + xc(oc, t0, t0 + n), w=xc(oc, t0, t0 + n))
                        psput(py)
            if hook is not None:
                hook(GROUPS_IDX[base])
            base += G


    INV192 = 1.0 / float(np.sqrt(192.0))
    INV128 = 1.0 / float(np.sqrt(128.0))
    aoff = [0]

    def areset(off=0):
        aoff[0] = off

    def ab(shape, dt):
        nel = int(np.prod(shape))
        ne16 = nel * (2 if dt == F32 else 1)
        ne16 = (ne16 + 31) // 32 * 32
        v = BIG[:, aoff[0]:aoff[0] + ne16]
        aoff[0] += ne16
        assert aoff[0] <= XE + HE, aoff[0]
        if dt == F32:
            v = v.bitcast(F32)
        v = v[:, 0:nel]
        if len(shape) == 2:
            v = v.rearrange("p (a b) -> p a b", a=shape[0])
        return v

    def rstd_from2(srcs, n, dim):
        b = psget()
        ps = PSB[b][:, 0:n]
        for i, (ap, cells, kp, presq) in enumerate(srcs):
            if presq:
                mm(ps, ones_f[0:kp, :], ap, i == 0, i == len(srcs) - 1, r=cells + ['ones_f'], w=[('P', b)])
            else:
                sq, sqc = tring.next()
                act(sq[0:kp, 0:n], ap, AF.Square, r=cells, w=[sqc])
                mm(ps, ones_f[0:kp, :], sq[0:kp, 0:n], i == 0, i == len(srcs) - 1, r=[sqc, 'ones_f'], w=[('P', b)])
        rs, rsc = rsring.next()
        ts(rs[:, 0:n], ps, 1.0 / dim, EPS, ALU.mult, ALU.add, r=[('P', b)], w=[rsc])
        psput(b)
        act(rs[:, 0:n], rs[:, 0:n], AF.Sqrt, r=[rsc], w=[rsc])
        rd, rdc = rdring.next()
        recip(rd[:, 0:n], rs[:, 0:n], r=[rsc], w=[rdc])
        return rd, rdc

    def rope_apply(src, srcc, kp, n, pmat, pmc, cs, sn, tabc, pos0, rd, rdc, out, outc, srcf=None, outf=None):
        srcf = src if srcf is None else srcf
        outf = out if outf is None else outf
        b = psget()
        mm(PSB[b][0:kp, 0:n], pmat, src, True, True, r=srcc + [pmc], w=[('P', b)])
        t1, t1c = tring.next()
        tt(t1[:, 0:n], srcf, cs[:, pos0:pos0 + n], ALU.mult, r=srcc + [tabc], w=[t1c])
        t2, t2c = tring.next()
        tt(t2[:, 0:n], PSB[b][:, 0:n], sn[:, pos0:pos0 + n], ALU.mult, r=[('P', b), tabc], w=[t2c])
        psput(b)
        if rd is None:
            tt(outf, t1[:, 0:n], t2[:, 0:n], ALU.add, r=[t1c, t2c], w=outc)
        else:
            tt(t1[:, 0:n], t1[:, 0:n], t2[:, 0:n], ALU.add, r=[t1c, t2c], w=[t1c])
            tt(outf, t1[:, 0:n], rd[:, 0:n], ALU.mult, r=[t1c, rdc], w=outc)

    def attn(qps, kps, vfn, chunks, nq, scale, out_ap, out_cells, ptring, bring, sink=None, ONES=None):
        po = psget(); pd = psget()
        started = {}
        last_idx = {}
        for idx, (kc, q0, q1, bias) in enumerate(chunks):
            last_idx[(q0, q1)] = idx
        pending = []

        def finish(item):
            idx, kc, q0, q1, pt, ptc = item
            vap, vc = vfn(kc)
            first = (q0, q1) not in started
            started[(q0, q1)] = True
            lastf = last_idx[(q0, q1)] == idx
            mm(PSB[po][:, q0:q1], vap, pt[:, q0:q1], first, lastf, r=[ptc] + vc, w=[('P', po)])
            mm(PSB[pd][:, q0:q1], ones_b[:, :], pt[:, q0:q1], first, lastf, r=[ptc, 'ones_b'], w=[('P', pd)])

        for idx, (kc, q0, q1, bias) in enumerate(chunks):
            ps = psget()
            for i, ((qap, qc), kfn) in enumerate(zip(qps, kps)):
                kap, kcells = kfn(kc)
                mm(PSB[ps][:, q0:q1], kap, qap[:, q0:q1], i == 0, i == len(qps) - 1, r=qc + kcells, w=[('P', ps)])
            pt, ptc = ptring.next()
            if bias is not None:
                bt, btc = bring.next()
                dma('sp', bt[:, q0:q1], bias, w=[btc])
                tmp, tc = tring.next()
                stt(tmp[:, q0:q1], PSB[ps][:, q0:q1], scale, bt[:, q0:q1], ALU.mult, ALU.add, r=[('P', ps), btc], w=[tc])
                act(pt[:, q0:q1], tmp[:, q0:q1], AF.Exp, r=[tc], w=[ptc])
            else:
                act(pt[:, q0:q1], PSB[ps][:, q0:q1], AF.Exp, r=[('P', ps)], w=[ptc], scale=scale)
            psput(ps)
            pending.append((idx, kc, q0, q1, pt, ptc))
            if len(pending) > 1:
                finish(pending.pop(0))
        while pending:
            finish(pending.pop(0))
        rd, rdc = rdring.next()
        if sink is not None:
            stt(rd[:, 0:nq], PSB[pd][:, 0:nq], sink, ONES[:, 0:nq], ALU.add, ALU.mult, r=[('P', pd), 'sinkE', 'ONES'], w=[rdc])
            recip(rd[:, 0:nq], rd[:, 0:nq], r=[rdc], w=[rdc])
        else:
            recip(rd[:, 0:nq], PSB[pd][:, 0:nq], r=[('P', pd)], w=[rdc])
        tt(out_ap, PSB[po][:, 0:nq], rd[:, 0:nq], ALU.mult, r=[('P', po), rdc], w=out_cells)
        psput(po); psput(pd)

    def mixer_pass(l, T0, n, latent, hookh=None):
        g = 1 if latent else 0
        ntile = n // 512
        tiles = [(i * 512, 512) for i in range(ntile)]
        nk = n + (512 if latent else 0)
        nkc = nk // 128
        noc = n // 128
        S.barrier()
        areset()
        if mstop <= 0:
            return
        if mstop <= 1:
            return
        S.barrier()
        areset()
        OT = ab([16, n], BF16)
        ptring = Ring('PT', [ab([512], BF16) for _ in range(4)])
        bring = Ring('BR', [ab([512], F32) for _ in range(2)]) if latent else None
        cst = Ring('CST', [ab([512], F32) for _ in range(2)])
        ONES = None
        if l == 1:
            ONES = ab([512], F32)
            memset(ONES[:, :], 1.0, ['ONES'])
        if latent:
            rdim = 64 if l == 0 else 128
            COS = ab([1024], F32); SIN = ab([1024], F32)
            dma('sp', COS[0:rdim, :], (ropem_c if l == 0 else ropeg_c)[:, :], w=['ROPE'])
            dma('sp', SIN[0:rdim, :], (ropem_s if l == 0 else ropeg_s)[:, :], w=['ROPE'])
            PM_, PMC = (p64[:, :], 'p64') if l == 0 else (p128[:, :], 'p128')
        base_off = aoff[0]

        def xsrc(k, lt, nn):
            return XM[:, k, T0 + lt:T0 + lt + nn], xmc(k, T0 + lt, T0 + lt + nn)

        def kchunks(qt, bias_fn=None, own_range=None):
            ch = []
            if latent:
                for kc in range(n // 128):
                    if own_range is not None and not (own_range[qt][0] <= kc <= own_range[qt][1]):
                        continue
                    ch.append((kc, 0, 512, None if bias_fn is None else bias_fn(kc, qt)))
                for kc in range(n // 128, nkc):
                    ch.append((kc, 0, 512, None))
            else:
                ch = [(0, 0, 256, None), (1, 0, 256, None), (2, 256, 512, None), (3, 256, 512, None)]
            return ch

        def tok_major_v(blk, wc, colsl, VH, vhc, srcfn, nsrc, out_d, col0):
            for t4 in range(noc // 4):
                b = psget()
                for j in range(4):
                    tc_ = t4 * 4 + j
                    for k in range(nsrc):
                        sap, sc = srcfn(k, tc_ * 128, 128)
                        mm(PSB[b][:, j * 128:(j + 1) * 128], sap, blk[:, k, colsl], k == 0, k == nsrc - 1,
                           r=[wc] + sc, w=[('P', b)])
                S.add('act', lambda e, o=VH[:, t4 * 4:(t4 + 1) * 4, :], i=PSB[b][:, :].rearrange("p (j c) -> p j c", j=4): e.copy(out=o, in_=i),
                      r=[('P', b)], w=[vhc])
                if out_d is not None:
                    st, stc = cst.next()
                    S.add('act', lambda e, o=st[:, :], i=PSB[b][:, :]: e.copy(out=o, in_=i), r=[('P', b)], w=[stc])
                    dma('sp', out_d[t4 * 512:(t4 + 1) * 512, col0:col0 + 128].rearrange("(j p) c -> p j c", p=128),
                        st[:, :].rearrange("p (j c) -> p j c", j=4), r=[stc])
                psput(b)

        def ctx_kT(cd, col0, ncol, dst, dstc, kp):
            b = psget()
            for tc_ in range(4):
                st, stc = cst.next()
                dma('sp', st[:, 0:ncol], cd[tc_ * 128:(tc_ + 1) * 128, col0:col0 + ncol], w=[stc])
                tp(PSB[b][0:ncol, tc_ * 128:(tc_ + 1) * 128], st[:, 0:ncol], ident[:, :], r=[stc, 'ident'], w=[('P', b)])
            return b

        def out_kT(src, srcc, kp, out_d, col0):
            for tc_ in range(4):
                b = psget()
                tp(PSB[b][:, 0:kp], src[0:kp, tc_ * 128:(tc_ + 1) * 128], ident[0:kp, 0:kp], r=srcc + ['ident'], w=[('P', b)])
                st, stc = cst.next()
                cp(st[:, 0:kp], PSB[b][:, 0:kp], r=[('P', b)], w=[stc])
                psput(b)
                dma('sp', out_d[tc_ * 128:(tc_ + 1) * 128, col0:col0 + kp], st[:, 0:kp], r=[stc])

        if l == 0:
            CQN = ab([4, n], BF16); CKVN = ab([4, nk], BF16)
            KRG = ab([nk], F32); SQR = ab([nk], F32)
            CKF = ab([4, 512], F32) if not latent else None
            KRAW = ab([512], F32) if not latent else None
            msets = [(ab([n], BF16), ab([n], BF16), ab([nk], BF16), ab([nk], BF16), ab([nkc, 128], BF16)) for _ in range(2)]
            for which in range(2):
                blks = [wblk(even_w_in, 0, 16, which * 512 + oc * 128, 128) for oc in range(4)]
                for (lt, nn) in tiles:
                    banks = []
                    for oc in range(4):
                        b = psget()
                        blk, wc = blks[oc]
                        for k in range(16):
                            sap, sc = xsrc(k, lt, nn)
                            mm(PSB[b][:, 0:nn], blk[:, k, :], sap, k == 0, k == 15, r=[wc] + sc, w=[('P', b)])
                        banks.append(b)
                    rd, rdc = rstd_from2([(PSB[b][:, 0:nn], [('P', b)], 128, False) for b in banks], nn, 512.0)
                    for oc in range(4):
                        b = banks[oc]
                        gsc = VT[:, 128 + which * 4 + oc:129 + which * 4 + oc]
                        if which == 0:
                            stt(CQN[:, oc, lt:lt + nn], PSB[b][:, 0:nn], gsc, rd[:, 0:nn], ALU.mult, ALU.mult,
                                r=[('P', b), rdc, 'VT'], w=['CQN'])
                        elif latent:
                            stt(CKVN[:, oc, lt:lt + nn], PSB[b][:, 0:nn], gsc, rd[:, 0:nn], ALU.mult, ALU.mult,
                                r=[('P', b), rdc, 'VT'], w=['CKVN'])
                        else:
                            stt(CKF[:, oc, lt:lt + nn], PSB[b][:, 0:nn], gsc, rd[:, 0:nn], ALU.mult, ALU.mult,
                                r=[('P', b), rdc, 'VT'], w=['CKF'])
                            S.add('act', lambda e, o=CKVN[:, oc, lt:lt + nn], i=CKF[:, oc, lt:lt + nn]: e.copy(out=o, in_=i),
                                  r=['CKF'], w=['CKVN'])
                        psput(b)
            if not latent:
                for tc_ in range(4):
                    b = psget()
                    for oc in range(4):
                        tp(PSB[b][:, oc * 128:(oc + 1) * 128], CKF[:, oc, tc_ * 128:(tc_ + 1) * 128], ident[:, :],
                           r=['CKF', 'ident'], w=[('P', b)])
                    st, stc = cst.next()
                    cp(st[:, :], PSB[b][:, :], r=[('P', b)], w=[stc])
                    psput(b)
                    dma('sp', o_ckv[tc_ * 128:(tc_ + 1) * 128, :], st[:, :], r=[stc])
            else:
                for tc_ in range(4):
                    st, stc = cst.next()
                    dma('sp', st[:, :], c_ckv[tc_ * 128:(tc_ + 1) * 128, :], w=[stc])
                    b = psget()
                    for oc in range(4):
                        tp(PSB[b][:, oc * 128:(oc + 1) * 128], st[:, oc * 128:(oc + 1) * 128], ident[:, :],
                           r=[stc, 'ident'], w=[('P', b)])
                    cp(CKVN[:, :, n + tc_ * 128:n + (tc_ + 1) * 128], PSB[b][:, :].rearrange("p (o t) -> p o t", o=4),
                       r=[('P', b)], w=['CKVN'])
                    psput(b)
            blk, wc = wblk(even_w_in, 0, 16, 1024, 64)
            gkr = VT[:, 139:140]
            for (lt, nn) in tiles:
                b = psget()
                for k in range(16):
                    sap, sc = xsrc(k, lt, nn)
                    mm(PSB[b][0:64, 0:nn], blk[:, k, :], sap, k == 0, k == 15, r=[wc] + sc, w=[('P', b)])
                act(SQR[0:64, lt:lt + nn], PSB[b][0:64, 0:nn], AF.Square, r=[('P', b)], w=['SQR'])
                if latent:
                    t0_, t0c = tring.next()
                    act(t0_[:, 0:nn], PSB[b][:, 0:nn], AF.Identity, r=[('P', b), 'VT'], w=[t0c], scale=gkr)
                    rope_apply(t0_[0:64, 0:nn], [t0c], 64, nn, PM_, PMC, COS, SIN, 'ROPE', lt, None, None,
                               KRG[0:64, lt:lt + nn], ['KRG'], srcf=t0_[:, 0:nn], outf=KRG[:, lt:lt + nn])
                else:
                    act(KRG[:, lt:lt + nn], PSB[b][:, 0:nn], AF.Identity, r=[('P', b), 'VT'], w=['KRG'], scale=gkr)
                    S.add('act', lambda e, o=KRAW[:, lt:lt + nn], i=PSB[b][:, 0:nn]: e.copy(out=o, in_=i), r=[('P', b)], w=['KRAW'])
                psput(b)
            if not latent:
                out_kT(KRAW, ['KRAW'], 64, o_kr, 0)
            else:
                b = ctx_kT(c_kr, 0, 64, None, None, 64)
                act(SQR[0:64, n:n + 512], PSB[b][0:64, :], AF.Square, r=[('P', b)], w=['SQR'])
                act(KRG[:, n:n + 512], PSB[b][:, :], AF.Identity, r=[('P', b), 'VT'], w=['KRG'], scale=gkr)
                psput(b)
            if mstop <= 2:
                return
            def mlaA(h):
                QN, QR, KN, KR, VH = msets[h % 2]
                hs = h % 2
                blk, wc = wblk(w_q_up, 0, 4, h * 192, 192)
                for (lt, nn) in tiles:
                    ba = psget(); bb = psget()
                    for k in range(4):
                        mm(PSB[ba][:, 0:nn], blk[:, k, 0:128], CQN[:, k, lt:lt + nn], k == 0, k == 3, r=[wc, 'CQN'], w=[('P', ba)])
                    for k in range(4):
                        mm(PSB[bb][0:64, 0:nn], blk[:, k, 128:192], CQN[:, k, lt:lt + nn], k == 0, k == 3, r=[wc, 'CQN'], w=[('P', bb)])
                    rd, rdc = rstd_from2([(PSB[ba][:, 0:nn], [('P', ba)], 128, False), (PSB[bb][0:64, 0:nn], [('P', bb)], 64, False)], nn, 192.0)
                    stt(QN[:, lt:lt + nn], PSB[ba][:, 0:nn], VT[:, 136:137], rd[:, 0:nn], ALU.mult, ALU.mult,
                        r=[('P', ba), rdc, 'VT'], w=[('QNm', hs)])
                    psput(ba)
                    t0_, t0c = tring.next()
                    act(t0_[:, 0:nn], PSB[bb][:, 0:nn], AF.Identity, r=[('P', bb), 'VT'], w=[t0c], scale=VT[:, 137:138])
                    psput(bb)
                    if latent:
                        rope_apply(t0_[0:64, 0:nn], [t0c], 64, nn, PM_, PMC, COS, SIN, 'ROPE', lt, rd, rdc,
                                   QR[0:64, lt:lt + nn], [('QRm', hs)], srcf=t0_[:, 0:nn], outf=QR[:, lt:lt + nn])
                    else:
                        tt(QR[:, lt:lt + nn], t0_[:, 0:nn], rd[:, 0:nn], ALU.mult, r=[t0c, rdc], w=[('QRm', hs)])
                blk, wc = wblk(w_kv_up, 0, 4, h * 256, 256)
                for kt in range(nk // 512):
                    ba = psget()
                    for k in range(4):
                        mm(PSB[ba][:, :], blk[:, k, 0:128], CKVN[:, k, kt * 512:(kt + 1) * 512], k == 0, k == 3, r=[wc, 'CKVN'], w=[('P', ba)])
                    rd, rdc = rstd_from2([(PSB[ba][:, :], [('P', ba)], 128, False), (SQR[0:64, kt * 512:(kt + 1) * 512], ['SQR'], 64, True)], 512, 192.0)
                    stt(KN[:, kt * 512:(kt + 1) * 512], PSB[ba][:, :], VT[:, 138:139], rd[:, :], ALU.mult, ALU.mult,
                        r=[('P', ba), rdc, 'VT'], w=[('KNm', hs)])
                    psput(ba)
                    tt(KR[:, kt * 512:(kt + 1) * 512], KRG[:, kt * 512:(kt + 1) * 512], rd[:, :], ALU.mult,
                       r=['KRG', rdc], w=[('KRm', hs)])
                for t4 in range(nkc // 4):
                    b = psget()
                    for j in range(4):
                        kc = t4 * 4 + j
                        for k in range(4):
                            mm(PSB[b][:, j * 128:(j + 1) * 128], CKVN[:, k, kc * 128:(kc + 1) * 128], blk[:, k, 128:256],
                               k == 0, k == 3, r=[wc, 'CKVN'], w=[('P', b)])
                    S.add('act', lambda e, o=VH[:, t4 * 4:(t4 + 1) * 4, :], i=PSB[b][:, :].rearrange("p (j c) -> p j c", j=4): e.copy(out=o, in_=i),
                          r=[('P', b)], w=[('VHm', hs)])
                    psput(b)

            def mlaB(h):
                QN, QR, KN, KR, VH = msets[h % 2]
                hs = h % 2
                for qt, (lt, nn) in enumerate(tiles):
                    attn([(QN[:, lt:lt + nn], [('QNm', hs)]), (QR[0:64, lt:lt + nn], [('QRm', hs)])],
                         [lambda kc, KN=KN, hs=hs: (KN[:, kc * 128:(kc + 1) * 128], [('KNm', hs)]), lambda kc, KR=KR, hs=hs: (KR[0:64, kc * 128:(kc + 1) * 128], [('KRm', hs)])],
                         lambda kc, VH=VH, hs=hs: (VH[:, kc, :], [('VHm', hs)]), kchunks(qt), nn, INV192,
                         OT[:, h, lt:lt + nn], [('OT', h)], ptring, bring)

            nhm = 8 if mstop > 3 else 1
            mlaA(0)
            for h in range(nhm):
                if h + 1 < nhm:
                    mlaA(h + 1)
                mlaB(h)
                if hookh is not None:
                    hookh()
            if mstop <= 4:
                return
            S.barrier()
            areset(base_off)
            sets = []
            for _ in range(2):
                sets.append((ab([n], BF16), ab([nk], BF16), ab([nkc, 128], BF16)))
            KF = ab([512], F32) if not latent else None
            def naA(h):
                QN, KN, VH = sets[h % 2]
                sfx = h % 2
                blk, wc = wblk(even_w_in, 0, 16, 1088 + h * 128, 128)
                for (lt, nn) in tiles:
                    b = psget()
                    for k in range(16):
                        sap, sc = xsrc(k, lt, nn)
                        mm(PSB[b][:, 0:nn], blk[:, k, :], sap, k == 0, k == 15, r=[wc] + sc, w=[('P', b)])
                    rd, rdc = rstd_from2([(PSB[b][:, 0:nn], [('P', b)], 128, False)], nn, 128.0)
                    stt(QN[:, lt:lt + nn], PSB[b][:, 0:nn], VT[:, 140:141], rd[:, 0:nn], ALU.mult, ALU.mult,
                        r=[('P', b), rdc, 'VT'], w=[('QNn', sfx)])
                    psput(b)
                blk, wc = wblk(even_w_in, 0, 16, 2112 + h * 128, 128)
                for (lt, nn) in tiles:
                    b = psget()
                    for k in range(16):
                        sap, sc = xsrc(k, lt, nn)
                        mm(PSB[b][:, 0:nn], blk[:, k, :], sap, k == 0, k == 15, r=[wc] + sc, w=[('P', b)])
                    rd, rdc = rstd_from2([(PSB[b][:, 0:nn], [('P', b)], 128, False)], nn, 128.0)
                    if latent:
                        stt(KN[:, lt:lt + nn], PSB[b][:, 0:nn], VT[:, 141:142], rd[:, 0:nn], ALU.mult, ALU.mult,
                            r=[('P', b), rdc, 'VT'], w=[('KNn', sfx)])
                    else:
                        stt(KF[:, lt:lt + nn], PSB[b][:, 0:nn], VT[:, 141:142], rd[:, 0:nn], ALU.mult, ALU.mult,
                            r=[('P', b), rdc, 'VT'], w=['KF'])
                        S.add('act', lambda e, o=KN[:, lt:lt + nn], i=KF[:, lt:lt + nn]: e.copy(out=o, in_=i), r=['KF'], w=[('KNn', sfx)])
                    psput(b)
                if not latent:
                    out_kT(KF, ['KF'], 128, o_nak, h * 128)
                else:
                    b = ctx_kT(c_nak, h * 128, 128, None, None, 128)
                    cp(KN[:, n:n + 512], PSB[b][:, :], r=[('P', b)], w=[('KNn', sfx)])
                    psput(b)
                blk, wc = wblk(even_w_in, 0, 16, 3136 + h * 128, 128)
                tok_major_v(blk, wc, slice(0, 128), VH, ('VHn', sfx), xsrc, 16, None if latent else o_nav, h * 128)
                if latent:
                    dma('pool', VH[:, noc:noc + 4, :], c_nav[:, h * 128:(h + 1) * 128].rearrange("(j p) c -> p j c", p=128), w=[('VHn', sfx)])

            def naB(h):
                QN, KN, VH = sets[h % 2]
                sfx = h % 2
                for qt, (lt, nn) in enumerate(tiles):
                    bf = (lambda kc, qt_, h=h: nab[h * 1024 + kc * 128:h * 1024 + (kc + 1) * 128, qt_ * 512:(qt_ + 1) * 512]) if latent else None
                    attn([(QN[:, lt:lt + nn], [('QNn', sfx)])],
                         [lambda kc, KN=KN, sfx=sfx: (KN[:, kc * 128:(kc + 1) * 128], [('KNn', sfx)])],
                         lambda kc, VH=VH, sfx=sfx: (VH[:, kc, :], [('VHn', sfx)]), kchunks(qt, bf), nn, INV128,
                         OT[:, 8 + h, lt:lt + nn], [('OT', 8 + h)], ptring, bring)

            naA(0)
            for h in range(8):
                if h + 1 < 8:
                    naA(h + 1)
                naB(h)
                if hookh is not None:
                    hookh()
            w_out = even_w_out
        else:
            ksets = [(ab([nk], BF16), ab([nkc, 128], BF16)) for _ in range(2)]
            qsets = [ab([n], BF16) for _ in range(2)]
            KF = ab([512], F32) if not latent else None
            own_range = [(0, 4), (3, 7)]
            def gkvA(kvh):
                KN, VH = ksets[kvh % 2]
                ks = kvh % 2
                blk, wc = wblk(odd_w_in, 0, 16, 2048 + kvh * 128, 128)
                for (lt, nn) in tiles:
                    b = psget()
                    for k in range(16):
                        sap, sc = xsrc(k, lt, nn)
                        mm(PSB[b][:, 0:nn], blk[:, k, :], sap, k == 0, k == 15, r=[wc] + sc, w=[('P', b)])
                    rd, rdc = rstd_from2([(PSB[b][:, 0:nn], [('P', b)], 128, False)], nn, 128.0)
                    if latent:
                        t0_, t0c = tring.next()
                        stt(t0_[:, 0:nn], PSB[b][:, 0:nn], VT[:, 143:144], rd[:, 0:nn], ALU.mult, ALU.mult,
                            r=[('P', b), rdc, 'VT'], w=[t0c])
                        rope_apply(t0_[:, 0:nn], [t0c], 128, nn, PM_, PMC, COS, SIN, 'ROPE', lt, None, None,
                                   KN[:, lt:lt + nn], [('KNg', ks)])
                    else:
                        stt(KF[:, lt:lt + nn], PSB[b][:, 0:nn], VT[:, 143:144], rd[:, 0:nn], ALU.mult, ALU.mult,
                            r=[('P', b), rdc, 'VT'], w=['KF'])
                        S.add('act', lambda e, o=KN[:, lt:lt + nn], i=KF[:, lt:lt + nn]: e.copy(out=o, in_=i), r=['KF'], w=[('KNg', ks)])
                    psput(b)
                if not latent:
                    out_kT(KF, ['KF'], 128, o_gk, kvh * 128)
                else:
                    b = ctx_kT(c_gk, kvh * 128, 128, None, None, 128)
                    cp(KN[:, n:n + 512], PSB[b][:, :], r=[('P', b)], w=[('KNg', ks)])
                    psput(b)
                blk, wc = wblk(odd_w_in, 0, 16, 2560 + kvh * 128, 128)
                tok_major_v(blk, wc, slice(0, 128), VH, ('VHg', ks), xsrc, 16, None if latent else o_gv, kvh * 128)
                if latent:
                    dma('pool', VH[:, noc:noc + 4, :], c_gv[:, kvh * 128:(kvh + 1) * 128].rearrange("(j p) c -> p j c", p=128), w=[('VHg', ks)])

            def gqA(h):
                kvh = h // 4
                QN = qsets[h % 2]
                qs = h % 2
                blk, wc = wblk(odd_w_in, 0, 16, h * 128, 128)
                for (lt, nn) in tiles:
                    b = psget()
                    for k in range(16):
                        sap, sc = xsrc(k, lt, nn)
                        mm(PSB[b][:, 0:nn], blk[:, k, :], sap, k == 0, k == 15, r=[wc] + sc, w=[('P', b)])
                    rd, rdc = rstd_from2([(PSB[b][:, 0:nn], [('P', b)], 128, False)], nn, 128.0)
                    if latent:
                        t0_, t0c = tring.next()
                        stt(t0_[:, 0:nn], PSB[b][:, 0:nn], VT[:, 142:143], rd[:, 0:nn], ALU.mult, ALU.mult,
                            r=[('P', b), rdc, 'VT'], w=[t0c])
                        rope_apply(t0_[:, 0:nn], [t0c], 128, nn, PM_, PMC, COS, SIN, 'ROPE', lt, None, None,
                                   QN[:, lt:lt + nn], [('QNg', qs)])
                    else:
                        stt(QN[:, lt:lt + nn], PSB[b][:, 0:nn], VT[:, 142:143], rd[:, 0:nn], ALU.mult, ALU.mult,
                            r=[('P', b), rdc, 'VT'], w=[('QNg', qs)])
                    psput(b)

            def gqB(h):
                kvh = h // 4
                KN, VH = ksets[kvh % 2]
                ks = kvh % 2
                QN = qsets[h % 2]
                qs = h % 2
                for qt, (lt, nn) in enumerate(tiles):
                    bf = (lambda kc, qt_: wmask[kc * 128:(kc + 1) * 128, qt_ * 512:(qt_ + 1) * 512]) if latent else None
                    attn([(QN[:, lt:lt + nn], [('QNg', qs)])],
                         [lambda kc, KN=KN, ks=ks: (KN[:, kc * 128:(kc + 1) * 128], [('KNg', ks)])],
                         lambda kc, VH=VH, ks=ks: (VH[:, kc, :], [('VHg', ks)]), kchunks(qt, bf, own_range), nn, INV128,
                         OT[:, h, lt:lt + nn], [('OT', h)], ptring, bring, sink=sinkE[:, h:h + 1], ONES=ONES)

            gkvA(0)
            gqA(0)
            for h in range(16):
                if h + 1 < 16:
                    if (h + 1) % 4 == 0:
                        gkvA((h + 1) // 4)
                    gqA(h + 1)
                gqB(h)
            w_out = odd_w_out
        xr = Ring('XR', [ab([512], F32) for _ in range(2)])
        for oc in range(16):
            blk, wc = wblk(w_out, 0, 16, oc * 128, 128)
            for (lt, nn) in tiles:
                b = psget()
                for k in range(16):
                    mm(PSB[b][:, 0:nn], blk[:, k, :], OT[:, k, lt:lt + nn], k == 0, k == 15, r=[wc, ('OT', k)], w=[('P', b)])
                xt_, xtc = xr.next()
                cells = xdc(oc, T0 + lt, T0 + lt + nn)
                dma('sp', xt_[:, 0:nn], XD[:, oc, T0 + lt:T0 + lt + nn], r=cells, w=[xtc])
                stt(xt_[:, 0:nn], PSB[b][:, 0:nn], Gsc(1, oc, g), xt_[:, 0:nn], ALU.mult, ALU.add,
                    r=[('P', b), ('GMOD', 1), xtc], w=[xtc])
                psput(b)
                dma('sp', XD[:, oc, T0 + lt:T0 + lt + nn], xt_[:, 0:nn], r=[xtc], w=cells)

    def mixer(l, hookh=None):
        for (t0, nn, g) in TILES:
            prepass(1, lambda fc, t0=t0, nn=nn: (X[:, fc, t0:t0 + nn], xc(fc, t0, t0 + nn)), t0, nn, g)
        for ti, (t0, nn, g) in enumerate(TILES):
            dma('sp', XD[:, :, t0:t0 + nn], X[:, :, t0:t0 + nn], r=[c for fc in range(16) for c in xc(fc, t0, t0 + nn)],
                w=[c for fc in range(16) for c in xdc(fc, t0, t0 + nn)])
        mixer_pass(l, 512, 1024, True, hookh)
        mixer_pass(l, 0, 512, False)
        S.barrier()
        for ti, (t0, nn, g) in enumerate(TILES):
            dma('sp', X[:, :, t0:t0 + nn], XD[:, :, t0:t0 + nn], r=[c for fc in range(16) for c in xdc(fc, t0, t0 + nn)],
                w=[c for fc in range(16) for c in xc(fc, t0, t0 + nn)])
        S.barrier()

    if n_sub == 6 and not skip_ffn and cfg.get('pipe', True):
        bm = psget()
        mods_mm(0, 0, 48, bm)
        mods_fin(0, bm, [0])
        cuts0 = [48 + (96 * q) // 44 for q in range(45)]
        ffn(0, 0, hook2=lambda c: mods_mm(0, cuts0[c], cuts0[c + 1], bm))
        mods_fin(0, bm, [1, 2])
        psput(bm)
        bm1 = psget()
        hcnt = [0]
        cuts1 = [(144 * q) // 16 for q in range(17)]

        def hk():
            q = hcnt[0]
            hcnt[0] += 1
            mods_mm(1, cuts1[q], cuts1[q + 1], bm1)
        mixer(0, hk)
        assert hcnt[0] == 16, hcnt[0]
        ffn(0, 1)
        mods_fin(1, bm1, [0, 1, 2])
        psput(bm1)
        ffn(1, 0)
        mixer(1)
        ffn(1, 1)
    else:
        sub = 0
        for l in range(2):
            if sub < n_sub:
                mods(l)
            if sub < n_sub and not skip_ffn:
                ffn(l, 0)
            sub += 1
            if sub < n_sub and cfg.get('mixer', True):
                mixer(l)
            sub += 1
            if sub < n_sub and not skip_ffn:
                ffn(l, 1)
            sub += 1

    S.barrier()
    for t in range(12):
        sl = t % 3
        st = STGF[:, 4096 + sl * 2048: 4096 + (sl + 1) * 2048]
        for f4 in range(4):
            b = psget()
            for q in range(4):
                fc = f4 * 4 + q
                tp(PSB[b][:, q * 128:(q + 1) * 128], X[:, fc, t * 128:(t + 1) * 128], ident[:, :],
                   r=xc(fc, t * 128, (t + 1) * 128) + ['ident'], w=[('P', b)])
            if f4 % 2 == 0:
                cp(st[:, f4 * 512:(f4 + 1) * 512], PSB[b][:, :], r=[('P', b)], w=[('STGO', sl, f4)])
            else:
                S.add('act', lambda e, o=st[:, f4 * 512:(f4 + 1) * 512], i=PSB[b][:, :]: e.copy(out=o, in_=i),
                      r=[('P', b)], w=[('STGO', sl, f4)])
            psput(b)
        dma('sp', y_out[t * 128:(t + 1) * 128, :], st, r=[('STGO', sl, f4) for f4 in range(4)])

    S.emit(cfg.get('limit'))
    es.close()
    return nc


def _prep_inputs(inp):
    f = np.float32
    g = lambda k: np.asarray(inp[k], dtype=f)
    ident = np.eye(128, dtype=f)
    def tables(R):
        half = R // 2
        nf = half // 2
        t = np.arange(1024)
        inv = (10000.0 ** (-np.arange(nf, dtype=np.float64) / nf))
        ang_r = (t // 64)[None, :] * inv[:, None]
        ang_c = (t % 64)[None, :] * inv[:, None]
        ang = np.concatenate([ang_r, ang_r, ang_c, ang_c], axis=0)
        P = np.zeros((R, R), f)
        for base in (0, half):
            for i in range(nf):
                P[base + nf + i, base + i] = -1.0
                P[base + i, base + nf + i] = 1.0
        return np.cos(ang).astype(f), np.sin(ang).astype(f), P
    gc, gs, p128 = tables(128)
    mc, ms, p64 = tables(64)
    idx = np.arange(1024)
    wm = np.where(np.abs(idx[None, :] - idx[:, None]) <= 128, 0.0, NEG).astype(f)
    rpb = g("na_rpb")[0]
    kr, kc = idx // 64, idx % 64
    qr, qc = idx // 64, idx % 64
    rs = np.clip(qr - 4, 0, 8)
    ws = np.clip(qc - 8, 0, 48)
    rowv = (kr[:, None] >= rs[None, :]) & (kr[:, None] < rs[None, :] + 8)
    colv = (kc[:, None] >= ws[None, :]) & (kc[:, None] < ws[None, :] + 16)
    valid = rowv & colv
    ro = np.clip(kr[:, None] - qr[None, :] + 7, 0, 14)
    co = np.clip(kc[:, None] - qc[None, :], -15, 15) + 15
    flat = ro * 31 + co
    flat = np.where(valid, flat, 15 * 31)
    rp = np.concatenate([rpb.reshape(8, -1), np.full((8, 1), NEG, f)], axis=1)
    nabias = np.ascontiguousarray(rp[:, flat].reshape(8 * 1024, 1024))
    shared = {
        "ident": ident, "ropeg_c": gc, "ropeg_s": gs, "ropem_c": mc, "ropem_s": ms, "p128": p128, "p64": p64,
        "wmask": wm, "nab": nabias,
        "sinkb": np.ascontiguousarray(np.broadcast_to(g("gqa_sink")[0][None, :], (128, 16))),
        "ada_w": g("ada_w").reshape(4096, 18432),
        "ffn_w_in": g("ffn_w_in").reshape(4 * 2048, 11264),
        "ffn_w_out": g("ffn_w_out").reshape(4 * 5632, 2048),
        "even_w_in": g("even_w_in")[0], "even_w_out": g("even_w_out")[0],
        "w_q_up": g("mla_w_q_up")[0], "w_kv_up": g("mla_w_kv_up")[0],
        "odd_w_in": g("odd_w_in")[0], "odd_w_out": g("odd_w_out")[0],
    }
    maps = []
    for i in range(8):
        vec = np.zeros((640, 128), f)
        vec[0:16] = g("c_ctx").reshape(16, 128)
        vec[16:32] = g("c")[i].reshape(16, 128)
        vec[32:128] = g("norm_g").reshape(96, 128)
        vec[128:132] = g("mla_q_norm")[0].reshape(4, 128)
        vec[132:136] = g("mla_kv_norm")[0].reshape(4, 128)
        qk = g("mla_qk_norm")[0]
        vec[136] = qk[0, 0:128]; vec[137, 0:64] = qk[0, 128:192]
        vec[138] = qk[1, 0:128]; vec[139, 0:64] = qk[1, 128:192]
        vec[140:142] = g("na_qk_norm")[0]
        vec[142:144] = g("gqa_qk_norm")[0]
        vec[256:544] = g("ada_b").reshape(288, 128)
        m = dict(shared)
        m["xin"] = np.concatenate([g("x_prompt")[2 * i:2 * i + 2].reshape(512, D), g("x_sample")[i]], axis=0)
        m["c_ckv"] = g("cache_mla_ckv")[i, 0]; m["c_kr"] = g("cache_mla_krope")[i, 0]
        m["c_nak"] = g("cache_na_k")[i, 0].reshape(512, 1024); m["c_nav"] = g("cache_na_v")[i, 0].reshape(512, 1024)
        m["c_gk"] = g("cache_gqa_k")[i, 0].reshape(512, 512); m["c_gv"] = g("cache_gqa_v")[i, 0].reshape(512, 512)
        m["vecs"] = vec
        maps.append(m)
    return maps


def _assemble(results):
    n = len(results)
    f = np.float32
    yp = np.zeros((16, 256, D), f); ys = np.zeros((8, 1024, D), f)
    ckv = np.zeros((16, 1, 256, 512), f); kr = np.zeros((16, 1, 256, 64), f)
    nak = np.zeros((16, 1, 256, 8, 128), f); nav = np.zeros((16, 1, 256, 8, 128), f)
    gk = np.zeros((16, 1, 256, 4, 128), f); gv = np.zeros((16, 1, 256, 4, 128), f)
    for i in range(n):
        r = results[i]
        yp[2 * i:2 * i + 2] = r["y"][0:512].reshape(2, 256, D)
        ys[i] = r["y"][512:]
        ckv[2 * i:2 * i + 2, 0] = r["o_ckv"].reshape(2, 256, 512)
        kr[2 * i:2 * i + 2, 0] = r["o_kr"].reshape(2, 256, 64)
        nak[2 * i:2 * i + 2, 0] = r["o_nak"].reshape(2, 256, 8, 128)
        nav[2 * i:2 * i + 2, 0] = r["o_nav"].reshape(2, 256, 8, 128)
        gk[2 * i:2 * i + 2, 0] = r["o_gk"].reshape(2, 256, 4, 128)
        gv[2 * i:2 * i + 2, 0] = r["o_gv"].reshape(2, 256, 4, 128)
    return (yp, ys, ckv, kr, nak, nav, gk, gv)


def kernel(**inputs):
    maps = _prep_inputs(inputs)
    nc = build({})
    res = run_bass_kernel_spmd(nc, maps, core_ids=list(range(8)))
    return _assemble(res.results)
```

```python
import contextlib
import numpy as np
import concourse.bass as bass
import concourse.mybir as mybir
from concourse.bass_utils import run_bass_kernel_spmd

F32 = mybir.dt.float32
BF16 = mybir.dt.bfloat16
AF = mybir.ActivationFunctionType
ALU = mybir.AluOpType

ENGS = ('pe', 'act', 'dve', 'pool', 'sp')
SAME_ENG_SYNC = True
NDSEM = 8
NSLOT = 6
GROUPS = [5, 5, 5, 5, 5, 5, 5, 5, 4]
GMAX = 5
EPS = 1e-6
NEG = -30000.0

D = 2048
NTOK = 1536
DFF = 5632


class _Op:
    __slots__ = ('eng', 'fn', 'dma', 'users', 'id', 'key', 'deps', 'sem', 'val', 'selfwait')


class Sched:
    def __init__(self, nc):
        self.nc = nc
        self.ops = {e: [] for e in ENGS}
        self.cells = {}
        self.uid = 0
        self.pending = {e: [] for e in ENGS}
        self.dmas = []

    def add(self, eng, fn, r=(), w=(), dma=False):
        op = _Op()
        op.eng = eng; op.fn = fn; op.dma = dma; op.users = False
        op.id = self.uid; self.uid += 1
        op.key = ('d', op.id) if dma else eng
        op.sem = None; op.val = 0; op.selfwait = 0
        deps = {}
        cells = self.cells
        for c in r:
            st = cells.get(c)
            if st is not None and st[0] is not None:
                deps[st[0].id] = st[0]
        for c in w:
            st = cells.get(c)
            if st is not None:
                if st[0] is not None:
                    deps[st[0].id] = st[0]
                for o in st[1].values():
                    deps[o.id] = o
        for c in r:
            st = cells.get(c)
            if st is None:
                cells[c] = [None, {op.key: op}]
            else:
                st[1][op.key] = op
        for c in w:
            cells[c] = [op, {}]
        for o in self.pending[eng]:
            deps[o.id] = o
        self.pending[eng] = []
        dl = []
        for d in deps.values():
            if d is op:
                continue
            if (not d.dma) and (not dma) and d.eng == eng and (eng == 'pe' or not SAME_ENG_SYNC):
                continue
            d.users = True
            dl.append(d)
        op.deps = dl
        self.ops[eng].append(op)
        if dma:
            self.dmas.append(op)
        return op

    def barrier(self):
        last = []
        for e in ENGS:
            for op in reversed(self.ops[e]):
                if not op.dma:
                    last.append(op)
                    break
        last.extend(self.dmas)
        self.dmas = []
        for e in ENGS:
            self.pending[e] = self.pending[e] + list(last)

    def emit(self, limit=None):
        nc = self.nc
        if limit is not None:
            for e in ENGS:
                self.ops[e] = [o for o in self.ops[e] if o.id < limit]
        engsem = {e: nc.alloc_semaphore("s_" + e) for e in ENGS}
        dmasem = {q: [nc.alloc_semaphore("d_%s%d" % (q, i)) for i in range(NDSEM)] for q in ('sp', 'pool')}
        semkey = {}
        final = {}
        for e in ENGS:
            cnt = 0
            di = 0
            duse = [0] * NDSEM
            for op in self.ops[e]:
                if op.dma:
                    i = di % NDSEM
                    di += 1
                    duse[i] += 1
                    op.sem = dmasem[e][i]
                    semkey[id(op)] = (e, i)
                    op.val = 16 * duse[i]
                    op.selfwait = 16 * (duse[i] - 1)
                    final[(e, i)] = (op.sem, op.val)
                elif op.users:
                    cnt += 1
                    op.sem = engsem[e]
                    semkey[id(op)] = (e, -1)
                    op.val = cnt
            assert cnt < 60000, (e, cnt)
        ops = self.ops

        def run(e, eng):
            waited = {}
            for op in ops[e]:
                for d in op.deps:
                    k = semkey[id(d)]
                    if waited.get(k, 0) < d.val:
                        eng.wait_ge(d.sem, d.val)
                        waited[k] = d.val
                if op.dma and op.selfwait > 0:
                    k = semkey[id(op)]
                    if waited.get(k, 0) < op.selfwait:
                        eng.wait_ge(op.sem, op.selfwait)
                        waited[k] = op.selfwait
                ins = op.fn(eng)
                if op.dma:
                    ins.then_inc(op.sem, 16)
                elif op.users:
                    ins.then_inc(op.sem, 1)
            if e == 'sp':
                for k, (sem, val) in final.items():
                    if waited.get(k, 0) < val:
                        eng.wait_ge(sem, val)

        with nc.Block() as block:
            @block.tensor
            def _(eng):
                run('pe', eng)

            @block.scalar
            def _(eng):
                run('act', eng)

            @block.vector
            def _(eng):
                run('dve', eng)

            @block.gpsimd
            def _(eng):
                run('pool', eng)

            @block.sync
            def _(eng):
                run('sp', eng)


class Ring:
    def __init__(self, name, aps):
        self.name = name
        self.aps = aps
        self.i = 0

    def next(self):
        i = self.i % len(self.aps)
        self.i += 1
        return self.aps[i], (self.name, i)


def build(cfg):
    n_sub = cfg.get('n_sub', 6)
    debug = cfg.get('debug', False)
    skip_ffn = cfg.get('skip_ffn', False)
    mstop = cfg.get('mstop', 99)
    nc = bass.Bass("TRN2", target_bir_lowering=False)
    S = Sched(nc)
    es = contextlib.ExitStack()

    def din(name, shape):
        return nc.dram_tensor(name, list(shape), F32, kind="ExternalInput").ap()

    def dout(name, shape):
        return nc.dram_tensor(name, list(shape), F32, kind="ExternalOutput").ap()

    def sb(name, shape, dt):
        return es.enter_context(nc.sbuf_tensor("sb_" + name, list(shape), dt))

    xin = din("xin", [NTOK, D])
    c_ckv = din("c_ckv", [512, 512]); c_kr = din("c_kr", [512, 64])
    c_nak = din("c_nak", [512, 1024]); c_nav = din("c_nav", [512, 1024])
    c_gk = din("c_gk", [512, 512]); c_gv = din("c_gv", [512, 512])
    vecs = din("vecs", [640, 128])
    sinkb = din("sinkb", [128, 16])
    ident_d = din("ident", [128, 128])
    ropeg_c = din("ropeg_c", [128, 1024]); ropeg_s = din("ropeg_s", [128, 1024])
    ropem_c = din("ropem_c", [64, 1024]); ropem_s = din("ropem_s", [64, 1024])
    p128_d = din("p128", [128, 128]); p64_d = din("p64", [64, 64])
    wmask = din("wmask", [1024, 1024])
    nab = din("nab", [8 * 1024, 1024])
    ada_w = din("ada_w", [2 * 2048, 18432])
    ffn_w_in = din("ffn_w_in", [4 * 2048, 11264])
    ffn_w_out = din("ffn_w_out", [4 * 5632, 2048])
    even_w_in = din("even_w_in", [2048, 4160]); even_w_out = din("even_w_out", [2048, 2048])
    w_q_up = din("w_q_up", [512, 1536]); w_kv_up = din("w_kv_up", [512, 2048])
    odd_w_in = din("odd_w_in", [2048, 3072]); odd_w_out = din("odd_w_out", [2048, 2048])

    y_out = dout("y", [NTOK, D])
    o_ckv = dout("o_ckv", [512, 512]); o_kr = dout("o_kr", [512, 64])
    o_nak = dout("o_nak", [512, 1024]); o_nav = dout("o_nav", [512, 1024])
    o_gk = dout("o_gk", [512, 512]); o_gv = dout("o_gv", [512, 512])
    XD = nc.dram_tensor("xd_scratch", [128, 16, NTOK], F32, kind="ExternalOutput").ap()

    XE = 16 * NTOK * 2
    HE = GMAX * NTOK
    BIG = sb("BIG", [128, XE + HE], BF16)
    X = BIG[:, 0:XE].bitcast(F32).rearrange("p (c t) -> p c t", c=16)
    H = BIG[:, XE:XE + HE].rearrange("p (j t) -> p j t", j=GMAX)
    XMT = sb("XM", [128, 16 * NTOK], BF16)
    XM = XMT[:, :].rearrange("p (c t) -> p c t", c=16)
    WR = sb("WR", [128, NSLOT, 2048], BF16)
    VT = sb("VT", [128, 640], F32)
    MODS = sb("MODS", [128, 144, 2], F32)
    AMOD = sb("AMOD", [128, 3, 16, 2], F32)
    GMOD = sb("GMOD", [128, 3, 16, 2], F32)
    ident = sb("ident", [128, 128], F32)
    ones_f = sb("ones_f", [128, 128], F32)
    ones_b = sb("ones_b", [128, 128], BF16)
    sT = sb("sT", [128, 32], BF16)
    sinkE = sb("sinkE", [128, 16], F32)
    p128 = sb("p128", [128, 128], F32)
    p64 = sb("p64", [64, 64], F32)
    TR = sb("TR", [128, 4, 512], F32)
    RSR = sb("RSR", [128, 2, 512], F32)
    RDR = sb("RDR", [128, 3, 512], F32)
    tring = Ring('TR', [TR[:, i, :] for i in range(4)])
    rsring = Ring('RS', [RSR[:, i, :] for i in range(2)])
    rdring = Ring('RD', [RDR[:, i, :] for i in range(3)])
    PSB = [es.enter_context(nc.psum_tensor("ps%d" % i, [128, 512], F32)) for i in range(8)]
    psfree = list(range(8))

    def psget():
        b = psfree.pop(0)
        return b

    def psput(b):
        psfree.append(b)

    def dma(q, out, in_, r=(), w=()):
        return S.add(q, lambda e, o=out, i=in_: e.dma_start(out=o, in_=i), r=r, w=w, dma=True)

    def mm(ps, lhsT, rhs, start, stop, r, w):
        return S.add('pe', lambda e, a=ps, b=lhsT, c=rhs, s0=start, s1=stop: e.matmul(a, b, c, start=s0, stop=s1), r=r, w=w)

    def tp(ps, in_, idn, r, w):
        return S.add('pe', lambda e, a=ps, b=in_, c=idn: e.transpose(a, b, c), r=r, w=w)

    def act(out, in_, func, r, w, bias=None, scale=None):
        kw = {}
        if bias is not None:
            kw['bias'] = bias
        if scale is not None:
            kw['scale'] = scale
        return S.add('act', lambda e, o=out, i=in_, f=func, k=kw: e.activation(out=o, in_=i, func=f, **k), r=r, w=w)

    def tt(out, in0, in1, op, r, w, eng='dve'):
        return S.add(eng, lambda e, o=out, a=in0, b=in1, p=op: e.tensor_tensor(out=o, in0=a, in1=b, op=p), r=r, w=w)

    def ts(out, in0, s1, s2, op0, op1, r, w, eng='dve'):
        if s2 is None:
            return S.add(eng, lambda e, o=out, a=in0, x=s1, p=op0: e.tensor_scalar(out=o, in0=a, scalar1=x, scalar2=None, op0=p), r=r, w=w)
        return S.add(eng, lambda e, o=out, a=in0, x=s1, y=s2, p=op0, q=op1: e.tensor_scalar(out=o, in0=a, scalar1=x, scalar2=y, op0=p, op1=q), r=r, w=w)

    def stt(out, in0, sc, in1, op0, op1, r, w, eng='dve'):
        return S.add(eng, lambda e, o=out, a=in0, s=sc, b=in1, p=op0, q=op1: e.scalar_tensor_tensor(out=o, in0=a, scalar=s, in1=b, op0=p, op1=q), r=r, w=w)

    def cp(out, in_, r, w, eng='dve'):
        return S.add(eng, lambda e, o=out, i=in_: e.tensor_copy(out=o, in_=i), r=r, w=w)

    def recip(out, in_, r, w):
        return S.add('dve', lambda e, o=out, i=in_: e.reciprocal(out=o, in_=i), r=r, w=w)

    def memset(ap, v, w):
        return S.add('dve', lambda e, a=ap, c=v: e.memset(a, c), w=w)

    wslot = [0]

    def wload(src, a, b):
        s = wslot[0] % NSLOT
        wslot[0] += 1
        view = WR[:, s, 0:a * b].rearrange("p (a b) -> p a b", a=a)
        dma('pool', view, src, w=[('W', s)])
        return view, ('W', s)

    def wblk(wd, row0, nk, c0, nc_):
        src = wd[row0:row0 + nk * 128, c0:c0 + nc_].rearrange("(k p) n -> p k n", p=128)
        return wload(src, nk, nc_)

    def xc(fc, t0, t1):
        return [('X', fc, t) for t in range(t0 // 256, (t1 + 255) // 256)]

    def xmc(fc, t0, t1):
        return [('XM', fc, t) for t in range(t0 // 256, (t1 + 255) // 256)]

    def xdc(fc, t0, t1):
        return [('XD', fc, t) for t in range(t0 // 256, (t1 + 255) // 256)]

    dma('sp', ident[:, :], ident_d[:, :], w=['ident'])
    dma('sp', p128[:, :], p128_d[:, :], w=['p128'])
    dma('sp', p64[:, :], p64_d[:, :], w=['p64'])
    dma('sp', sinkE[:, :], sinkb[:, :], w=['sinkE'])
    memset(ones_f[:, :], 1.0, ['ones_f'])
    memset(ones_b[:, :], 1.0, ['ones_b'])
    act(sinkE[:, :], sinkE[:, :], AF.Exp, r=['sinkE'], w=['sinkE'])
    STGF = XMT[:, :].bitcast(F32)
    for t in range(5):
        st = STGF[:, t * 128:(t + 1) * 128]
        dma('sp', st, vecs[t * 128:(t + 1) * 128, :], w=[('STG', t)])
        b = psget()
        tp(PSB[b][:, 0:128], st, ident[:, :], r=[('STG', t), 'ident'], w=[('P', b)])
        cp(VT[:, t * 128:(t + 1) * 128], PSB[b][:, 0:128], r=[('P', b)], w=['VT'])
        psput(b)
    act(sT[:, :], VT[:, 0:32], AF.Silu, r=['VT'], w=['sT'])

    S.barrier()
    for t in range(12):
        sl = t % 3
        st = STGF[:, 4096 + sl * 2048: 4096 + (sl + 1) * 2048]
        dma('sp', st, xin[t * 128:(t + 1) * 128, :], w=[('STGX', sl)])
        for f4 in range(4):
            b = psget()
            for q in range(4):
                fc = f4 * 4 + q
                tp(PSB[b][:, q * 128:(q + 1) * 128], st[:, fc * 128:(fc + 1) * 128], ident[:, :],
                   r=[('STGX', sl), 'ident'], w=[('P', b)])
            dst = X[:, f4 * 4:(f4 + 1) * 4, t * 128:(t + 1) * 128]
            src = PSB[b][:, :].rearrange("p (q n) -> p q n", q=4)
            wc = []
            for q in range(4):
                wc += xc(f4 * 4 + q, t * 128, (t + 1) * 128)
            if f4 % 2 == 0:
                cp(dst, src, r=[('P', b)], w=wc)
            else:
                S.add('act', lambda e, o=dst, i=src: e.copy(out=o, in_=i), r=[('P', b)], w=wc)
            psput(b)
    S.barrier()

    def mods_mm(l, oc0, oc1, b):
        PM = PSB[b]
        for oc in range(oc0, oc1):
            blk, wc = wblk(ada_w, l * 2048, 16, oc * 128, 128)
            for k in range(16):
                mm(PM[:, 2 * oc:2 * oc + 2], blk[:, k, :], sT[:, k:32:16], k == 0, k == 15,
                   r=[wc, 'sT'], w=[('P', b)])

    def mods_fin(l, b, parts):
        PM = PSB[b]
        for i in parts:
            for g in range(2):
                tt(MODS[:, 48 * i:48 * (i + 1), g], PM[:, 96 * i + g:96 * (i + 1):2],
                   VT[:, 256 + l * 144 + 48 * i:256 + l * 144 + 48 * (i + 1)], ALU.add,
                   r=[('P', b), 'VT'], w=[('MODS', i)])
            for g in range(2):
                ts(AMOD[:, i, :, g], MODS[:, (3 * i + 1) * 16:(3 * i + 2) * 16, g], 1.0, None, ALU.add, None,
                   r=[('MODS', i)], w=[('AMOD', i)])
                tt(AMOD[:, i, :, g], AMOD[:, i, :, g], VT[:, 32 + (l * 3 + i) * 16:32 + (l * 3 + i + 1) * 16], ALU.mult,
                   r=[('AMOD', i), 'VT'], w=[('AMOD', i)])
                ts(GMOD[:, i, :, g], MODS[:, (3 * i + 2) * 16:(3 * i + 3) * 16, g], 0.5 if i != 1 else 1.0, None,
                   ALU.mult, None, r=[('MODS', i)], w=[('GMOD', i)])

    def mods(l):
        b = psget()
        mods_mm(l, 0, 144, b)
        mods_fin(l, b, [0, 1, 2])
        psput(b)

    def Asc(i, fc, g):
        return AMOD[:, i, fc, g:g + 1]

    def Bsc(i, fc, g):
        return MODS[:, 3 * i * 16 + fc, g:g + 1]

    def Gsc(i, fc, g):
        return GMOD[:, i, fc, g:g + 1]

    def rstd_from(srcs, n, dim):
        b = psget()
        ps = PSB[b][:, 0:n]
        for i, (ap, cells, kp) in enumerate(srcs):
            sq, sqc = tring.next()
            act(sq[0:kp, 0:n], ap, AF.Square, r=cells, w=[sqc])
            mm(ps, ones_f[0:kp, :], sq[0:kp, 0:n], i == 0, i == len(srcs) - 1, r=[sqc, 'ones_f'], w=[('P', b)])
        rs, rsc = rsring.next()
        ts(rs[:, 0:n], ps, 1.0 / dim, EPS, ALU.mult, ALU.add, r=[('P', b)], w=[rsc])
        psput(b)
        act(rs[:, 0:n], rs[:, 0:n], AF.Sqrt, r=[rsc], w=[rsc])
        rd, rdc = rdring.next()
        recip(rd[:, 0:n], rs[:, 0:n], r=[rsc], w=[rdc])
        return rd, rdc

    def prepass(i, src_fn, t0, n, g):
        srcs = []
        for fc in range(16):
            ap, cells = src_fn(fc)
            srcs.append((ap, cells, 128))
        rd, rdc = rstd_from(srcs, n, 2048.0)
        for fc in range(16):
            ap, cells = src_fn(fc)
            tmp, tc = tring.next()
            stt(tmp[:, 0:n], ap, Asc(i, fc, g), rd[:, 0:n], ALU.mult, ALU.mult, r=cells + [rdc, ('AMOD', i)], w=[tc])
            act(XM[:, fc, t0:t0 + n], tmp[:, 0:n], AF.Identity, r=[tc, ('MODS', i)], w=xmc(fc, t0, t0 + n),
                bias=Bsc(i, fc, g))

    TILES = [(0, 512, 0), (512, 512, 1), (1024, 512, 1)]
    GROUPS_IDX = {}
    _b = 0
    for _gi, _G in enumerate(GROUPS):
        GROUPS_IDX[_b] = _gi
        _b += _G

    def ffn(l, j, hook=None, hook2=None):
        i = 0 if j == 0 else 2
        for (t0, n, g) in TILES:
            prepass(i, lambda fc, t0=t0, n=n: (X[:, fc, t0:t0 + n], xc(fc, t0, t0 + n)), t0, n, g)
        row_in = (l * 2 + j) * 2048
        row_out = (l * 2 + j) * 5632
        base = 0
        for G in GROUPS:
            for jj in range(G):
                c = base + jj
                bg, wg = wblk(ffn_w_in, row_in, 16, c * 128, 128)
                bu, wu = wblk(ffn_w_in, row_in, 16, (44 + c) * 128, 128)
                for ti, (t0, n, g) in enumerate(TILES):
                    pg = psget(); pu = psget()
                    for k in range(16):
                        mm(PSB[pg][:, :], bg[:, k, :], XM[:, k, t0:t0 + n], k == 0, k == 15,
                           r=[wg] + xmc(k, t0, t0 + n), w=[('P', pg)])
                    for k in range(16):
                        mm(PSB[pu][:, :], bu[:, k, :], XM[:, k, t0:t0 + n], k == 0, k == 15,
                           r=[wu] + xmc(k, t0, t0 + n), w=[('P', pu)])
                    sg, sgc = tring.next()
                    act(sg[:, :], PSB[pg][:, :], AF.Silu, r=[('P', pg)], w=[sgc])
                    tt(H[:, jj, t0:t0 + n], sg[:, :], PSB[pu][:, :], ALU.mult, r=[sgc, ('P', pu)], w=[('H', jj, ti)])
                    psput(pg); psput(pu)
                if hook2 is not None:
                    hook2(c)
            for ocg in range(4):
                blks = []
                for kb in range(0, G, 4):
                    nk = min(4, G - kb)
                    blks.append(wblk(ffn_w_out, row_out + (base + kb) * 128, nk, ocg * 512, 512))
                for ti, (t0, n, g) in enumerate(TILES):
                    for o4 in range(4):
                        oc = ocg * 4 + o4
                        py = psget()
                        for jj in range(G):
                            blk, wc = blks[jj // 4]
                            mm(PSB[py][:, :], blk[:, jj % 4, o4 * 128:(o4 + 1) * 128], H[:, jj, t0:t0 + n],
                               jj == 0, jj == G - 1, r=[wc, ('H', jj, ti)], w=[('P', py)])
                        stt(X[:, oc, t0:t0 + n], PSB[py][:, :], Gsc(i, oc, g), X[:, oc, t0:t0 + n], ALU.mult, ALU.add,
                            r=[('P', py), ('GMOD', i)] + xc(oc, t0, t0 + n), w=xc(oc, t0, t0 + n))
                        psput(py)
            if hook is not None:
                hook(GROUPS_IDX[base])
            base += G


    INV192 = 1.0 / float(np.sqrt(192.0))
    INV128 = 1.0 / float(np.sqrt(128.0))
    aoff = [0]

    def areset(off=0):
        aoff[0] = off

    def ab(shape, dt):
        nel = int(np.prod(shape))
        ne16 = nel * (2 if dt == F32 else 1)
        ne16 = (ne16 + 31) // 32 * 32
        v = BIG[:, aoff[0]:aoff[0] + ne16]
        aoff[0] += ne16
        assert aoff[0] <= XE + HE, aoff[0]
        if dt == F32:
            v = v.bitcast(F32)
        v = v[:, 0:nel]
        if len(shape) == 2:
            v = v.rearrange("p (a b) -> p a b", a=shape[0])
        return v

    def rstd_from2(srcs, n, dim):
        b = psget()
        ps = PSB[b][:, 0:n]
        for i, (ap, cells, kp, presq) in enumerate(srcs):
            if presq:
                mm(ps, ones_f[0:kp, :], ap, i == 0, i == len(srcs) - 1, r=cells + ['ones_f'], w=[('P', b)])
            else:
                sq, sqc = tring.next()
                act(sq[0:kp, 0:n], ap, AF.Square, r=cells, w=[sqc])
                mm(ps, ones_f[0:kp, :], sq[0:kp, 0:n], i == 0, i == len(srcs) - 1, r=[sqc, 'ones_f'], w=[('P', b)])
        rs, rsc = rsring.next()
        ts(rs[:, 0:n], ps, 1.0 / dim, EPS, ALU.mult, ALU.add, r=[('P', b)], w=[rsc])
        psput(b)
        act(rs[:, 0:n], rs[:, 0:n], AF.Sqrt, r=[rsc], w=[rsc])
        rd, rdc = rdring.next()
        recip(rd[:, 0:n], rs[:, 0:n], r=[rsc], w=[rdc])
        return rd, rdc

    def rope_apply(src, srcc, kp, n, pmat, pmc, cs, sn, tabc, pos0, rd, rdc, out, outc, srcf=None, outf=None):
        srcf = src if srcf is None else srcf
        outf = out if outf is None else outf
        b = psget()
        mm(PSB[b][0:kp, 0:n], pmat, src, True, True, r=srcc + [pmc], w=[('P', b)])
        t1, t1c = tring.next()
        tt(t1[:, 0:n], srcf, cs[:, pos0:pos0 + n], ALU.mult, r=srcc + [tabc], w=[t1c])
        t2, t2c = tring.next()
        tt(t2[:, 0:n], PSB[b][:, 0:n], sn[:, pos0:pos0 + n], ALU.mult, r=[('P', b), tabc], w=[t2c])
        psput(b)
        if rd is None:
            tt(outf, t1[:, 0:n], t2[:, 0:n], ALU.add, r=[t1c, t2c], w=outc)
        else:
            tt(t1[:, 0:n], t1[:, 0:n], t2[:, 0:n], ALU.add, r=[t1c, t2c], w=[t1c])
            tt(outf, t1[:, 0:n], rd[:, 0:n], ALU.mult, r=[t1c, rdc], w=outc)

    def attn(qps, kps, vfn, chunks, nq, scale, out_ap, out_cells, ptring, bring, sink=None, ONES=None):
        po = psget(); pd = psget()
        started = {}
        last_idx = {}
        for idx, (kc, q0, q1, bias) in enumerate(chunks):
            last_idx[(q0, q1)] = idx
        pending = []

        def finish(item):
            idx, kc, q0, q1, pt, ptc = item
            vap, vc = vfn(kc)
            first = (q0, q1) not in started
            started[(q0, q1)] = True
            lastf = last_idx[(q0, q1)] == idx
            mm(PSB[po][:, q0:q1], vap, pt[:, q0:q1], first, lastf, r=[ptc] + vc, w=[('P', po)])
            mm(PSB[pd][:, q0:q1], ones_b[:, :], pt[:, q0:q1], first, lastf, r=[ptc, 'ones_b'], w=[('P', pd)])

        for idx, (kc, q0, q1, bias) in enumerate(chunks):
            ps = psget()
            for i, ((qap, qc), kfn) in enumerate(zip(qps, kps)):
                kap, kcells = kfn(kc)
                mm(PSB[ps][:, q0:q1], kap, qap[:, q0:q1], i == 0, i == len(qps) - 1, r=qc + kcells, w=[('P', ps)])
            pt, ptc = ptring.next()
            if bias is not None:
                bt, btc = bring.next()
                dma('sp', bt[:, q0:q1], bias, w=[btc])
                tmp, tc = tring.next()
                stt(tmp[:, q0:q1], PSB[ps][:, q0:q1], scale, bt[:, q0:q1], ALU.mult, ALU.add, r=[('P', ps), btc], w=[tc])
                act(pt[:, q0:q1], tmp[:, q0:q1], AF.Exp, r=[tc], w=[ptc])
            else:
                act(pt[:, q0:q1], PSB[ps][:, q0:q1], AF.Exp, r=[('P', ps)], w=[ptc], scale=scale)
            psput(ps)
            pending.append((idx, kc, q0, q1, pt, ptc))
            if len(pending) > 3:
                finish(pending.pop(0))
        while pending:
            finish(pending.pop(0))
        rd, rdc = rdring.next()
        if sink is not None:
            stt(rd[:, 0:nq], PSB[pd][:, 0:nq], sink, ONES[:, 0:nq], ALU.add, ALU.mult, r=[('P', pd), 'sinkE', 'ONES'], w=[rdc])
            recip(rd[:, 0:nq], rd[:, 0:nq], r=[rdc], w=[rdc])
        else:
            recip(rd[:, 0:nq], PSB[pd][:, 0:nq], r=[('P', pd)], w=[rdc])
        tt(out_ap, PSB[po][:, 0:nq], rd[:, 0:nq], ALU.mult, r=[('P', po), rdc], w=out_cells)
        psput(po); psput(pd)

    def mixer_pass(l, T0, n, latent, hookh=None):
        g = 1 if latent else 0
        ntile = n // 512
        tiles = [(i * 512, 512) for i in range(ntile)]
        nk = n + (512 if latent else 0)
        nkc = nk // 128
        noc = n // 128
        S.barrier()
        areset()
        if mstop <= 0:
            return
        if mstop <= 1:
            return
        S.barrier()
        areset()
        OT = ab([16, n], BF16)
        ptring = Ring('PT', [ab([512], BF16) for _ in range(4)])
        bring = Ring('BR', [ab([512], F32) for _ in range(2)]) if latent else None
        cst = Ring('CST', [ab([512], F32) for _ in range(2)])
        ONES = None
        if l == 1:
            ONES = ab([512], F32)
            memset(ONES[:, :], 1.0, ['ONES'])
        if latent:
            rdim = 64 if l == 0 else 128
            COS = ab([1024], F32); SIN = ab([1024], F32)
            dma('sp', COS[0:rdim, :], (ropem_c if l == 0 else ropeg_c)[:, :], w=['ROPE'])
            dma('sp', SIN[0:rdim, :], (ropem_s if l == 0 else ropeg_s)[:, :], w=['ROPE'])
            PM_, PMC = (p64[:, :], 'p64') if l == 0 else (p128[:, :], 'p128')
        base_off = aoff[0]

        def xsrc(k, lt, nn):
            return XM[:, k, T0 + lt:T0 + lt + nn], xmc(k, T0 + lt, T0 + lt + nn)

        def kchunks(qt, bias_fn=None, own_range=None):
            ch = []
            if latent:
                for kc in range(n // 128):
                    if own_range is not None and not (own_range[qt][0] <= kc <= own_range[qt][1]):
                        continue
                    ch.append((kc, 0, 512, None if bias_fn is None else bias_fn(kc, qt)))
                for kc in range(n // 128, nkc):
                    ch.append((kc, 0, 512, None))
            else:
                ch = [(0, 0, 256, None), (1, 0, 256, None), (2, 256, 512, None), (3, 256, 512, None)]
            return ch

        def tok_major_v(blk, wc, colsl, VH, vhc, srcfn, nsrc, out_d, col0):
            for t4 in range(noc // 4):
                b = psget()
                for j in range(4):
                    tc_ = t4 * 4 + j
                    for k in range(nsrc):
                        sap, sc = srcfn(k, tc_ * 128, 128)
                        mm(PSB[b][:, j * 128:(j + 1) * 128], sap, blk[:, k, colsl], k == 0, k == nsrc - 1,
                           r=[wc] + sc, w=[('P', b)])
                S.add('act', lambda e, o=VH[:, t4 * 4:(t4 + 1) * 4, :], i=PSB[b][:, :].rearrange("p (j c) -> p j c", j=4): e.copy(out=o, in_=i),
                      r=[('P', b)], w=[vhc])
                if out_d is not None:
                    st, stc = cst.next()
                    S.add('act', lambda e, o=st[:, :], i=PSB[b][:, :]: e.copy(out=o, in_=i), r=[('P', b)], w=[stc])
                    dma('sp', out_d[t4 * 512:(t4 + 1) * 512, col0:col0 + 128].rearrange("(j p) c -> p j c", p=128),
                        st[:, :].rearrange("p (j c) -> p j c", j=4), r=[stc])
                psput(b)

        def ctx_kT(cd, col0, ncol, dst, dstc, kp):
            b = psget()
            for tc_ in range(4):
                st, stc = cst.next()
                dma('sp', st[:, 0:ncol], cd[tc_ * 128:(tc_ + 1) * 128, col0:col0 + ncol], w=[stc])
                tp(PSB[b][0:ncol, tc_ * 128:(tc_ + 1) * 128], st[:, 0:ncol], ident[:, :], r=[stc, 'ident'], w=[('P', b)])
            return b

        def out_kT(src, srcc, kp, out_d, col0):
            for tc_ in range(4):
                b = psget()
                tp(PSB[b][:, 0:kp], src[0:kp, tc_ * 128:(tc_ + 1) * 128], ident[0:kp, 0:kp], r=srcc + ['ident'], w=[('P', b)])
                st, stc = cst.next()
                cp(st[:, 0:kp], PSB[b][:, 0:kp], r=[('P', b)], w=[stc])
                psput(b)
                dma('sp', out_d[tc_ * 128:(tc_ + 1) * 128, col0:col0 + kp], st[:, 0:kp], r=[stc])

        if l == 0:
            CQN = ab([4, n], BF16); CKVN = ab([4, nk], BF16)
            KRG = ab([nk], F32); SQR = ab([nk], F32)
            CKF = ab([4, 512], F32) if not latent else None
            KRAW = ab([512], F32) if not latent else None
            msets = [(ab([n], BF16), ab([n], BF16), ab([nk], BF16), ab([nk], BF16), ab([nkc, 128], BF16)) for _ in range(2)]
            for which in range(2):
                blks = [wblk(even_w_in, 0, 16, which * 512 + oc * 128, 128) for oc in range(4)]
                for (lt, nn) in tiles:
                    banks = []
                    for oc in range(4):
                        b = psget()
                        blk, wc = blks[oc]
                        for k in range(16):
                            sap, sc = xsrc(k, lt, nn)
                            mm(PSB[b][:, 0:nn], blk[:, k, :], sap, k == 0, k == 15, r=[wc] + sc, w=[('P', b)])
                        banks.append(b)
                    rd, rdc = rstd_from2([(PSB[b][:, 0:nn], [('P', b)], 128, False) for b in banks], nn, 512.0)
                    for oc in range(4):
                        b = banks[oc]
                        gsc = VT[:, 128 + which * 4 + oc:129 + which * 4 + oc]
                        if which == 0:
                            stt(CQN[:, oc, lt:lt + nn], PSB[b][:, 0:nn], gsc, rd[:, 0:nn], ALU.mult, ALU.mult,
                                r=[('P', b), rdc, 'VT'], w=['CQN'])
                        elif latent:
                            stt(CKVN[:, oc, lt:lt + nn], PSB[b][:, 0:nn], gsc, rd[:, 0:nn], ALU.mult, ALU.mult,
                                r=[('P', b), rdc, 'VT'], w=['CKVN'])
                        else:
                            stt(CKF[:, oc, lt:lt + nn], PSB[b][:, 0:nn], gsc, rd[:, 0:nn], ALU.mult, ALU.mult,
                                r=[('P', b), rdc, 'VT'], w=['CKF'])
                            S.add('act', lambda e, o=CKVN[:, oc, lt:lt + nn], i=CKF[:, oc, lt:lt + nn]: e.copy(out=o, in_=i),
                                  r=['CKF'], w=['CKVN'])
                        psput(b)
            if not latent:
                for tc_ in range(4):
                    b = psget()
                    for oc in range(4):
                        tp(PSB[b][:, oc * 128:(oc + 1) * 128], CKF[:, oc, tc_ * 128:(tc_ + 1) * 128], ident[:, :],
                           r=['CKF', 'ident'], w=[('P', b)])
                    st, stc = cst.next()
                    cp(st[:, :], PSB[b][:, :], r=[('P', b)], w=[stc])
                    psput(b)
                    dma('sp', o_ckv[tc_ * 128:(tc_ + 1) * 128, :], st[:, :], r=[stc])
            else:
                for tc_ in range(4):
                    st, stc = cst.next()
                    dma('sp', st[:, :], c_ckv[tc_ * 128:(tc_ + 1) * 128, :], w=[stc])
                    b = psget()
                    for oc in range(4):
                        tp(PSB[b][:, oc * 128:(oc + 1) * 128], st[:, oc * 128:(oc + 1) * 128], ident[:, :],
                           r=[stc, 'ident'], w=[('P', b)])
                    cp(CKVN[:, :, n + tc_ * 128:n + (tc_ + 1) * 128], PSB[b][:, :].rearrange("p (o t) -> p o t", o=4),
                       r=[('P', b)], w=['CKVN'])
                    psput(b)
            blk, wc = wblk(even_w_in, 0, 16, 1024, 64)
            gkr = VT[:, 139:140]
            for (lt, nn) in tiles:
                b = psget()
                for k in range(16):
                    sap, sc = xsrc(k, lt, nn)
                    mm(PSB[b][0:64, 0:nn], blk[:, k, :], sap, k == 0, k == 15, r=[wc] + sc, w=[('P', b)])
                act(SQR[0:64, lt:lt + nn], PSB[b][0:64, 0:nn], AF.Square, r=[('P', b)], w=['SQR'])
                if latent:
                    t0_, t0c = tring.next()
                    act(t0_[:, 0:nn], PSB[b][:, 0:nn], AF.Identity, r=[('P', b), 'VT'], w=[t0c], scale=gkr)
                    rope_apply(t0_[0:64, 0:nn], [t0c], 64, nn, PM_, PMC, COS, SIN, 'ROPE', lt, None, None,
                               KRG[0:64, lt:lt + nn], ['KRG'], srcf=t0_[:, 0:nn], outf=KRG[:, lt:lt + nn])
                else:
                    act(KRG[:, lt:lt + nn], PSB[b][:, 0:nn], AF.Identity, r=[('P', b), 'VT'], w=['KRG'], scale=gkr)
                    S.add('act', lambda e, o=KRAW[:, lt:lt + nn], i=PSB[b][:, 0:nn]: e.copy(out=o, in_=i), r=[('P', b)], w=['KRAW'])
                psput(b)
            if not latent:
                out_kT(KRAW, ['KRAW'], 64, o_kr, 0)
            else:
                b = ctx_kT(c_kr, 0, 64, None, None, 64)
                act(SQR[0:64, n:n + 512], PSB[b][0:64, :], AF.Square, r=[('P', b)], w=['SQR'])
                act(KRG[:, n:n + 512], PSB[b][:, :], AF.Identity, r=[('P', b), 'VT'], w=['KRG'], scale=gkr)
                psput(b)
            if mstop <= 2:
                return
            def mlaA(h):
                QN, QR, KN, KR, VH = msets[h % 2]
                hs = h % 2
                blk, wc = wblk(w_q_up, 0, 4, h * 192, 192)
                for (lt, nn) in tiles:
                    ba = psget(); bb = psget()
                    for k in range(4):
                        mm(PSB[ba][:, 0:nn], blk[:, k, 0:128], CQN[:, k, lt:lt + nn], k == 0, k == 3, r=[wc, 'CQN'], w=[('P', ba)])
                    for k in range(4):
                        mm(PSB[bb][0:64, 0:nn], blk[:, k, 128:192], CQN[:, k, lt:lt + nn], k == 0, k == 3, r=[wc, 'CQN'], w=[('P', bb)])
                    rd, rdc = rstd_from2([(PSB[ba][:, 0:nn], [('P', ba)], 128, False), (PSB[bb][0:64, 0:nn], [('P', bb)], 64, False)], nn, 192.0)
                    stt(QN[:, lt:lt + nn], PSB[ba][:, 0:nn], VT[:, 136:137], rd[:, 0:nn], ALU.mult, ALU.mult,
                        r=[('P', ba), rdc, 'VT'], w=[('QNm', hs)])
                    psput(ba)
                    t0_, t0c = tring.next()
                    act(t0_[:, 0:nn], PSB[bb][:, 0:nn], AF.Identity, r=[('P', bb), 'VT'], w=[t0c], scale=VT[:, 137:138])
                    psput(bb)
                    if latent:
                        rope_apply(t0_[0:64, 0:nn], [t0c], 64, nn, PM_, PMC, COS, SIN, 'ROPE', lt, rd, rdc,
                                   QR[0:64, lt:lt + nn], [('QRm', hs)], srcf=t0_[:, 0:nn], outf=QR[:, lt:lt + nn])
                    else:
                        tt(QR[:, lt:lt + nn], t0_[:, 0:nn], rd[:, 0:nn], ALU.mult, r=[t0c, rdc], w=[('QRm', hs)])
                blk, wc = wblk(w_kv_up, 0, 4, h * 256, 256)
                for kt in range(nk // 512):
                    ba = psget()
                    for k in range(4):
                        mm(PSB[ba][:, :], blk[:, k, 0:128], CKVN[:, k, kt * 512:(kt + 1) * 512], k == 0, k == 3, r=[wc, 'CKVN'], w=[('P', ba)])
                    rd, rdc = rstd_from2([(PSB[ba][:, :], [('P', ba)], 128, False), (SQR[0:64, kt * 512:(kt + 1) * 512], ['SQR'], 64, True)], 512, 192.0)
                    stt(KN[:, kt * 512:(kt + 1) * 512], PSB[ba][:, :], VT[:, 138:139], rd[:, :], ALU.mult, ALU.mult,
                        r=[('P', ba), rdc, 'VT'], w=[('KNm', hs)])
                    psput(ba)
                    tt(KR[:, kt * 512:(kt + 1) * 512], KRG[:, kt * 512:(kt + 1) * 512], rd[:, :], ALU.mult,
                       r=['KRG', rdc], w=[('KRm', hs)])
                for t4 in range(nkc // 4):
                    b = psget()
                    for j in range(4):
                        kc = t4 * 4 + j
                        for k in range(4):
                            mm(PSB[b][:, j * 128:(j + 1) * 128], CKVN[:, k, kc * 128:(kc + 1) * 128], blk[:, k, 128:256],
                               k == 0, k == 3, r=[wc, 'CKVN'], w=[('P', b)])
                    S.add('act', lambda e, o=VH[:, t4 * 4:(t4 + 1) * 4, :], i=PSB[b][:, :].rearrange("p (j c) -> p j c", j=4): e.copy(out=o, in_=i),
                          r=[('P', b)], w=[('VHm', hs)])
                    psput(b)

            def mlaB(h):
                QN, QR, KN, KR, VH = msets[h % 2]
                hs = h % 2
                for qt, (lt, nn) in enumerate(tiles):
                    attn([(QN[:, lt:lt + nn], [('QNm', hs)]), (QR[0:64, lt:lt + nn], [('QRm', hs)])],
                         [lambda kc, KN=KN, hs=hs: (KN[:, kc * 128:(kc + 1) * 128], [('KNm', hs)]), lambda kc, KR=KR, hs=hs: (KR[0:64, kc * 128:(kc + 1) * 128], [('KRm', hs)])],
                         lambda kc, VH=VH, hs=hs: (VH[:, kc, :], [('VHm', hs)]), kchunks(qt), nn, INV192,
                         OT[:, h, lt:lt + nn], [('OT', h)], ptring, bring)

            nhm = 8 if mstop > 3 else 1
            mlaA(0)
            for h in range(nhm):
                if h + 1 < nhm:
                    mlaA(h + 1)
                mlaB(h)
                if hookh is not None:
                    hookh()
            if mstop <= 4:
                return
            S.barrier()
            areset(base_off)
            sets = []
            for _ in range(2):
                sets.append((ab([n], BF16), ab([nk], BF16), ab([nkc, 128], BF16)))
            KF = ab([512], F32) if not latent else None
            def naA(h):
                QN, KN, VH = sets[h % 2]
                sfx = h % 2
                blk, wc = wblk(even_w_in, 0, 16, 1088 + h * 128, 128)
                for (lt, nn) in tiles:
                    b = psget()
                    for k in range(16):
                        sap, sc = xsrc(k, lt, nn)
                        mm(PSB[b][:, 0:nn], blk[:, k, :], sap, k == 0, k == 15, r=[wc] + sc, w=[('P', b)])
                    rd, rdc = rstd_from2([(PSB[b][:, 0:nn], [('P', b)], 128, False)], nn, 128.0)
                    stt(QN[:, lt:lt + nn], PSB[b][:, 0:nn], VT[:, 140:141], rd[:, 0:nn], ALU.mult, ALU.mult,
                        r=[('P', b), rdc, 'VT'], w=[('QNn', sfx)])
                    psput(b)
                blk, wc = wblk(even_w_in, 0, 16, 2112 + h * 128, 128)
                for (lt, nn) in tiles:
                    b = psget()
                    for k in range(16):
                        sap, sc = xsrc(k, lt, nn)
                        mm(PSB[b][:, 0:nn], blk[:, k, :], sap, k == 0, k == 15, r=[wc] + sc, w=[('P', b)])
                    rd, rdc = rstd_from2([(PSB[b][:, 0:nn], [('P', b)], 128, False)], nn, 128.0)
                    if latent:
                        stt(KN[:, lt:lt + nn], PSB[b][:, 0:nn], VT[:, 141:142], rd[:, 0:nn], ALU.mult, ALU.mult,
                            r=[('P', b), rdc, 'VT'], w=[('KNn', sfx)])
                    else:
                        stt(KF[:, lt:lt + nn], PSB[b][:, 0:nn], VT[:, 141:142], rd[:, 0:nn], ALU.mult, ALU.mult,
                            r=[('P', b), rdc, 'VT'], w=['KF'])
                        S.add('act', lambda e, o=KN[:, lt:lt + nn], i=KF[:, lt:lt + nn]: e.copy(out=o, in_=i), r=['KF'], w=[('KNn', sfx)])
                    psput(b)
                if not latent:
                    out_kT(KF, ['KF'], 128, o_nak, h * 128)
                else:
                    b = ctx_kT(c_nak, h * 128, 128, None, None, 128)
                    cp(KN[:, n:n + 512], PSB[b][:, :], r=[('P', b)], w=[('KNn', sfx)])
                    psput(b)
                blk, wc = wblk(even_w_in, 0, 16, 3136 + h * 128, 128)
                tok_major_v(blk, wc, slice(0, 128), VH, ('VHn', sfx), xsrc, 16, None if latent else o_nav, h * 128)
                if latent:
                    dma('pool', VH[:, noc:noc + 4, :], c_nav[:, h * 128:(h + 1) * 128].rearrange("(j p) c -> p j c", p=128), w=[('VHn', sfx)])

            def naB(h):
                QN, KN, VH = sets[h % 2]
                sfx = h % 2
                for qt, (lt, nn) in enumerate(tiles):
                    bf = (lambda kc, qt_, h=h: nab[h * 1024 + kc * 128:h * 1024 + (kc + 1) * 128, qt_ * 512:(qt_ + 1) * 512]) if latent else None
                    attn([(QN[:, lt:lt + nn], [('QNn', sfx)])],
                         [lambda kc, KN=KN, sfx=sfx: (KN[:, kc * 128:(kc + 1) * 128], [('KNn', sfx)])],
                         lambda kc, VH=VH, sfx=sfx: (VH[:, kc, :], [('VHn', sfx)]), kchunks(qt, bf), nn, INV128,
                         OT[:, 8 + h, lt:lt + nn], [('OT', 8 + h)], ptring, bring)

            naA(0)
            for h in range(8):
                if h + 1 < 8:
                    naA(h + 1)
                naB(h)
                if hookh is not None:
                    hookh()
            w_out = even_w_out
        else:
            ksets = [(ab([nk], BF16), ab([nkc, 128], BF16)) for _ in range(2)]
            qsets = [ab([n], BF16) for _ in range(2)]
            KF = ab([512], F32) if not latent else None
            own_range = [(0, 4), (3, 7)]
            def gkvA(kvh):
                KN, VH = ksets[kvh % 2]
                ks = kvh % 2
                blk, wc = wblk(odd_w_in, 0, 16, 2048 + kvh * 128, 128)
                for (lt, nn) in tiles:
                    b = psget()
                    for k in range(16):
                        sap, sc = xsrc(k, lt, nn)
                        mm(PSB[b][:, 0:nn], blk[:, k, :], sap, k == 0, k == 15, r=[wc] + sc, w=[('P', b)])
                    rd, rdc = rstd_from2([(PSB[b][:, 0:nn], [('P', b)], 128, False)], nn, 128.0)
                    if latent:
                        t0_, t0c = tring.next()
                        stt(t0_[:, 0:nn], PSB[b][:, 0:nn], VT[:, 143:144], rd[:, 0:nn], ALU.mult, ALU.mult,
                            r=[('P', b), rdc, 'VT'], w=[t0c])
                        rope_apply(t0_[:, 0:nn], [t0c], 128, nn, PM_, PMC, COS, SIN, 'ROPE', lt, None, None,
                                   KN[:, lt:lt + nn], [('KNg', ks)])
                    else:
                        stt(KF[:, lt:lt + nn], PSB[b][:, 0:nn], VT[:, 143:144], rd[:, 0:nn], ALU.mult, ALU.mult,
                            r=[('P', b), rdc, 'VT'], w=['KF'])
                        S.add('act', lambda e, o=KN[:, lt:lt + nn], i=KF[:, lt:lt + nn]: e.copy(out=o, in_=i), r=['KF'], w=[('KNg', ks)])
                    psput(b)
                if not latent:
                    out_kT(KF, ['KF'], 128, o_gk, kvh * 128)
                else:
                    b = ctx_kT(c_gk, kvh * 128, 128, None, None, 128)
                    cp(KN[:, n:n + 512], PSB[b][:, :], r=[('P', b)], w=[('KNg', ks)])
                    psput(b)
                blk, wc = wblk(odd_w_in, 0, 16, 2560 + kvh * 128, 128)
                tok_major_v(blk, wc, slice(0, 128), VH, ('VHg', ks), xsrc, 16, None if latent else o_gv, kvh * 128)
                if latent:
                    dma('pool', VH[:, noc:noc + 4, :], c_gv[:, kvh * 128:(kvh + 1) * 128].rearrange("(j p) c -> p j c", p=128), w=[('VHg', ks)])

            def gqA(h):
                kvh = h // 4
                QN = qsets[h % 2]
                qs = h % 2
                blk, wc = wblk(odd_w_in, 0, 16, h * 128, 128)
                for (lt, nn) in tiles:
                    b = psget()
                    for k in range(16):
                        sap, sc = xsrc(k, lt, nn)
                        mm(PSB[b][:, 0:nn], blk[:, k, :], sap, k == 0, k == 15, r=[wc] + sc, w=[('P', b)])
                    rd, rdc = rstd_from2([(PSB[b][:, 0:nn], [('P', b)], 128, False)], nn, 128.0)
                    if latent:
                        t0_, t0c = tring.next()
                        stt(t0_[:, 0:nn], PSB[b][:, 0:nn], VT[:, 142:143], rd[:, 0:nn], ALU.mult, ALU.mult,
                            r=[('P', b), rdc, 'VT'], w=[t0c])
                        rope_apply(t0_[:, 0:nn], [t0c], 128, nn, PM_, PMC, COS, SIN, 'ROPE', lt, None, None,
                                   QN[:, lt:lt + nn], [('QNg', qs)])
                    else:
                        stt(QN[:, lt:lt + nn], PSB[b][:, 0:nn], VT[:, 142:143], rd[:, 0:nn], ALU.mult, ALU.mult,
                            r=[('P', b), rdc, 'VT'], w=[('QNg', qs)])
                    psput(b)

            def gqB(h):
                kvh = h // 4
                KN, VH = ksets[kvh % 2]
                ks = kvh % 2
                QN = qsets[h % 2]
                qs = h % 2
                for qt, (lt, nn) in enumerate(tiles):
                    bf = (lambda kc, qt_: wmask[kc * 128:(kc + 1) * 128, qt_ * 512:(qt_ + 1) * 512]) if latent else None
                    attn([(QN[:, lt:lt + nn], [('QNg', qs)])],
                         [lambda kc, KN=KN, ks=ks: (KN[:, kc * 128:(kc + 1) * 128], [('KNg', ks)])],
                         lambda kc, VH=VH, ks=ks: (VH[:, kc, :], [('VHg', ks)]), kchunks(qt, bf, own_range), nn, INV128,
                         OT[:, h, lt:lt + nn], [('OT', h)], ptring, bring, sink=sinkE[:, h:h + 1], ONES=ONES)

            gkvA(0)
            gqA(0)
            for h in range(16):
                if h + 1 < 16:
                    if (h + 1) % 4 == 0:
                        gkvA((h + 1) // 4)
                    gqA(h + 1)
                gqB(h)
            w_out = odd_w_out
        xr = Ring('XR', [ab([512], F32) for _ in range(2)])
        for oc in range(16):
            blk, wc = wblk(w_out, 0, 16, oc * 128, 128)
            for (lt, nn) in tiles:
                b = psget()
                for k in range(16):
                    mm(PSB[b][:, 0:nn], blk[:, k, :], OT[:, k, lt:lt + nn], k == 0, k == 15, r=[wc, ('OT', k)], w=[('P', b)])
                xt_, xtc = xr.next()
                cells = xdc(oc, T0 + lt, T0 + lt + nn)
                dma('sp', xt_[:, 0:nn], XD[:, oc, T0 + lt:T0 + lt + nn], r=cells, w=[xtc])
                stt(xt_[:, 0:nn], PSB[b][:, 0:nn], Gsc(1, oc, g), xt_[:, 0:nn], ALU.mult, ALU.add,
                    r=[('P', b), ('GMOD', 1), xtc], w=[xtc])
                psput(b)
                dma('sp', XD[:, oc, T0 + lt:T0 + lt + nn], xt_[:, 0:nn], r=[xtc], w=cells)

    def mixer(l, hookh=None):
        for (t0, nn, g) in TILES:
            prepass(1, lambda fc, t0=t0, nn=nn: (X[:, fc, t0:t0 + nn], xc(fc, t0, t0 + nn)), t0, nn, g)
        for ti, (t0, nn, g) in enumerate(TILES):
            dma('sp', XD[:, :, t0:t0 + nn], X[:, :, t0:t0 + nn], r=[c for fc in range(16) for c in xc(fc, t0, t0 + nn)],
                w=[c for fc in range(16) for c in xdc(fc, t0, t0 + nn)])
        mixer_pass(l, 512, 1024, True, hookh)
        mixer_pass(l, 0, 512, False)
        S.barrier()
        for ti, (t0, nn, g) in enumerate(TILES):
            dma('sp', X[:, :, t0:t0 + nn], XD[:, :, t0:t0 + nn], r=[c for fc in range(16) for c in xdc(fc, t0, t0 + nn)],
                w=[c for fc in range(16) for c in xc(fc, t0, t0 + nn)])
        S.barrier()

    if n_sub == 6 and not skip_ffn and cfg.get('pipe', True):
        bm = psget()
        mods_mm(0, 0, 48, bm)
        mods_fin(0, bm, [0])
        cuts0 = [48 + (96 * q) // 44 for q in range(45)]
        ffn(0, 0, hook2=lambda c: mods_mm(0, cuts0[c], cuts0[c + 1], bm))
        mods_fin(0, bm, [1, 2])
        psput(bm)
        bm1 = psget()
        hcnt = [0]
        cuts1 = [(144 * q) // 16 for q in range(17)]

        def hk():
            q = hcnt[0]
            hcnt[0] += 1
            mods_mm(1, cuts1[q], cuts1[q + 1], bm1)
        mixer(0, hk)
        assert hcnt[0] == 16, hcnt[0]
        ffn(0, 1)
        mods_fin(1, bm1, [0, 1, 2])
        psput(bm1)
        ffn(1, 0)
        mixer(1)
        ffn(1, 1)
    else:
        sub = 0
        for l in range(2):
            if sub < n_sub:
                mods(l)
            if sub < n_sub and not skip_ffn:
                ffn(l, 0)
            sub += 1
            if sub < n_sub and cfg.get('mixer', True):
                mixer(l)
            sub += 1
            if sub < n_sub and not skip_ffn:
                ffn(l, 1)
            sub += 1

    S.barrier()
    for t in range(12):
        sl = t % 3
        st = STGF[:, 4096 + sl * 2048: 4096 + (sl + 1) * 2048]
        for f4 in range(4):
            b = psget()
            for q in range(4):
                fc = f4 * 4 + q
                tp(PSB[b][:, q * 128:(q + 1) * 128], X[:, fc, t * 128:(t + 1) * 128], ident[:, :],
                   r=xc(fc, t * 128, (t + 1) * 128) + ['ident'], w=[('P', b)])
            if f4 % 2 == 0:
                cp(st[:, f4 * 512:(f4 + 1) * 512], PSB[b][:, :], r=[('P', b)], w=[('STGO', sl, f4)])
            else:
                S.add('act', lambda e, o=st[:, f4 * 512:(f4 + 1) * 512], i=PSB[b][:, :]: e.copy(out=o, in_=i),
                      r=[('P', b)], w=[('STGO', sl, f4)])
            psput(b)
        dma('sp', y_out[t * 128:(t + 1) * 128, :], st, r=[('STGO', sl, f4) for f4 in range(4)])

    S.emit(cfg.get('limit'))
    es.close()
    return nc


def _prep_inputs(inp):
    f = np.float32
    g = lambda k: np.asarray(inp[k], dtype=f)
    ident = np.eye(128, dtype=f)
    def tables(R):
        half = R // 2
        nf = half // 2
        t = np.arange(1024)
        inv = (10000.0 ** (-np.arange(nf, dtype=np.float64) / nf))
        ang_r = (t // 64)[None, :] * inv[:, None]
        ang_c = (t % 64)[None, :] * inv[:, None]
        ang = np.concatenate([ang_r, ang_r, ang_c, ang_c], axis=0)
        P = np.zeros((R, R), f)
        for base in (0, half):
            for i in range(nf):
                P[base + nf + i, base + i] = -1.0
                P[base + i, base + nf + i] = 1.0
        return np.cos(ang).astype(f), np.sin(ang).astype(f), P
    gc, gs, p128 = tables(128)
    mc, ms, p64 = tables(64)
    idx = np.arange(1024)
    wm = np.where(np.abs(idx[None, :] - idx[:, None]) <= 128, 0.0, NEG).astype(f)
    rpb = g("na_rpb")[0]
    kr, kc = idx // 64, idx % 64
    qr, qc = idx // 64, idx % 64
    rs = np.clip(qr - 4, 0, 8)
    ws = np.clip(qc - 8, 0, 48)
    rowv = (kr[:, None] >= rs[None, :]) & (kr[:, None] < rs[None, :] + 8)
    colv = (kc[:, None] >= ws[None, :]) & (kc[:, None] < ws[None, :] + 16)
    valid = rowv & colv
    ro = np.clip(kr[:, None] - qr[None, :] + 7, 0, 14)
    co = np.clip(kc[:, None] - qc[None, :], -15, 15) + 15
    flat = ro * 31 + co
    flat = np.where(valid, flat, 15 * 31)
    rp = np.concatenate([rpb.reshape(8, -1), np.full((8, 1), NEG, f)], axis=1)
    nabias = np.ascontiguousarray(rp[:, flat].reshape(8 * 1024, 1024))
    shared = {
        "ident": ident, "ropeg_c": gc, "ropeg_s": gs, "ropem_c": mc, "ropem_s": ms, "p128": p128, "p64": p64,
        "wmask": wm, "nab": nabias,
        "sinkb": np.ascontiguousarray(np.broadcast_to(g("gqa_sink")[0][None, :], (128, 16))),
        "ada_w": g("ada_w").reshape(4096, 18432),
        "ffn_w_in": g("ffn_w_in").reshape(4 * 2048, 11264),
        "ffn_w_out": g("ffn_w_out").reshape(4 * 5632, 2048),
        "even_w_in": g("even_w_in")[0], "even_w_out": g("even_w_out")[0],
        "w_q_up": g("mla_w_q_up")[0], "w_kv_up": g("mla_w_kv_up")[0],
        "odd_w_in": g("odd_w_in")[0], "odd_w_out": g("odd_w_out")[0],
    }
    maps = []
    for i in range(8):
        vec = np.zeros((640, 128), f)
        vec[0:16] = g("c_ctx").reshape(16, 128)
        vec[16:32] = g("c")[i].reshape(16, 128)
        vec[32:128] = g("norm_g").reshape(96, 128)
        vec[128:132] = g("mla_q_norm")[0].reshape(4, 128)
        vec[132:136] = g("mla_kv_norm")[0].reshape(4, 128)
        qk = g("mla_qk_norm")[0]
        vec[136] = qk[0, 0:128]; vec[137, 0:64] = qk[0, 128:192]
        vec[138] = qk[1, 0:128]; vec[139, 0:64] = qk[1, 128:192]
        vec[140:142] = g("na_qk_norm")[0]
        vec[142:144] = g("gqa_qk_norm")[0]
        vec[256:544] = g("ada_b").reshape(288, 128)
        m = dict(shared)
        m["xin"] = np.concatenate([g("x_prompt")[2 * i:2 * i + 2].reshape(512, D), g("x_sample")[i]], axis=0)
        m["c_ckv"] = g("cache_mla_ckv")[i, 0]; m["c_kr"] = g("cache_mla_krope")[i, 0]
        m["c_nak"] = g("cache_na_k")[i, 0].reshape(512, 1024); m["c_nav"] = g("cache_na_v")[i, 0].reshape(512, 1024)
        m["c_gk"] = g("cache_gqa_k")[i, 0].reshape(512, 512); m["c_gv"] = g("cache_gqa_v")[i, 0].reshape(512, 512)
        m["vecs"] = vec
        maps.append(m)
    return maps


def _assemble(results):
    n = len(results)
    f = np.float32
    yp = np.zeros((16, 256, D), f); ys = np.zeros((8, 1024, D), f)
    ckv = np.zeros((16, 1, 256, 512), f); kr = np.zeros((16, 1, 256, 64), f)
    nak = np.zeros((16, 1, 256, 8, 128), f); nav = np.zeros((16, 1, 256, 8, 128), f)
    gk = np.zeros((16, 1, 256, 4, 128), f); gv = np.zeros((16, 1, 256, 4, 128), f)
    for i in range(n):
        r = results[i]
        yp[2 * i:2 * i + 2] = r["y"][0:512].reshape(2, 256, D)
        ys[i] = r["y"][512:]
        ckv[2 * i:2 * i + 2, 0] = r["o_ckv"].reshape(2, 256, 512)
        kr[2 * i:2 * i + 2, 0] = r["o_kr"].reshape(2, 256, 64)
        nak[2 * i:2 * i + 2, 0] = r["o_nak"].reshape(2, 256, 8, 128)
        nav[2 * i:2 * i + 2, 0] = r["o_nav"].reshape(2, 256, 8, 128)
        gk[2 * i:2 * i + 2, 0] = r["o_gk"].reshape(2, 256, 4, 128)
        gv[2 * i:2 * i + 2, 0] = r["o_gv"].reshape(2, 256, 4, 128)
    return (yp, ys, ckv, kr, nak, nav, gk, gv)


def kernel(**inputs):
    maps = _prep_inputs(inputs)
    nc = build({})
    res = run_bass_kernel_spmd(nc, maps, core_ids=list(range(8)))
    return _assemble(res.results)
```

```python
import contextlib
import numpy as np
import concourse.bass as bass
import concourse.mybir as mybir
from concourse.bass_utils import run_bass_kernel_spmd

F32 = mybir.dt.float32
BF16 = mybir.dt.bfloat16
AF = mybir.ActivationFunctionType
ALU = mybir.AluOpType

ENGS = ('pe', 'act', 'dve', 'pool', 'sp')
SAME_ENG_SYNC = True
NDSEM = 8
NSLOT = 6
GROUPS = [5, 5, 5, 5, 5, 5, 5, 5, 4]
GMAX = 5
EPS = 1e-6
NEG = -30000.0

D = 2048
NTOK = 1536
DFF = 5632


class _Op:
    __slots__ = ('eng', 'fn', 'dma', 'users', 'id', 'key', 'deps', 'sem', 'val', 'selfwait')


class Sched:
    def __init__(self, nc):
        self.nc = nc
        self.ops = {e: [] for e in ENGS}
        self.cells = {}
        self.uid = 0
        self.pending = {e: [] for e in ENGS}
        self.dmas = []

    def add(self, eng, fn, r=(), w=(), dma=False):
        op = _Op()
        op.eng = eng; op.fn = fn; op.dma = dma; op.users = False
        op.id = self.uid; self.uid += 1
        op.key = ('d', op.id) if dma else eng
        op.sem = None; op.val = 0; op.selfwait = 0
        deps = {}
        cells = self.cells
        for c in r:
            st = cells.get(c)
            if st is not None and st[0] is not None:
                deps[st[0].id] = st[0]
        for c in w:
            st = cells.get(c)
            if st is not None:
                if st[0] is not None:
                    deps[st[0].id] = st[0]
                for o in st[1].values():
                    deps[o.id] = o
        for c in r:
            st = cells.get(c)
            if st is None:
                cells[c] = [None, {op.key: op}]
            else:
                st[1][op.key] = op
        for c in w:
            cells[c] = [op, {}]
        for o in self.pending[eng]:
            deps[o.id] = o
        self.pending[eng] = []
        dl = []
        for d in deps.values():
            if d is op:
                continue
            if (not d.dma) and (not dma) and d.eng == eng and (eng == 'pe' or not SAME_ENG_SYNC):
                continue
            d.users = True
            dl.append(d)
        op.deps = dl
        self.ops[eng].append(op)
        if dma:
            self.dmas.append(op)
        return op

    def barrier(self):
        last = []
        for e in ENGS:
            for op in reversed(self.ops[e]):
                if not op.dma:
                    last.append(op)
                    break
        last.extend(self.dmas)
        self.dmas = []
        for e in ENGS:
            self.pending[e] = self.pending[e] + list(last)

    def emit(self, limit=None):
        nc = self.nc
        if limit is not None:
            for e in ENGS:
                self.ops[e] = [o for o in self.ops[e] if o.id < limit]
        engsem = {e: nc.alloc_semaphore("s_" + e) for e in ENGS}
        dmasem = {q: [nc.alloc_semaphore("d_%s%d" % (q, i)) for i in range(NDSEM)] for q in ('sp', 'pool')}
        semkey = {}
        final = {}
        for e in ENGS:
            cnt = 0
            di = 0
            duse = [0] * NDSEM
            for op in self.ops[e]:
                if op.dma:
                    i = di % NDSEM
                    di += 1
                    duse[i] += 1
                    op.sem = dmasem[e][i]
                    semkey[id(op)] = (e, i)
                    op.val = 16 * duse[i]
                    op.selfwait = 16 * (duse[i] - 1)
                    final[(e, i)] = (op.sem, op.val)
                elif op.users:
                    cnt += 1
                    op.sem = engsem[e]
                    semkey[id(op)] = (e, -1)
                    op.val = cnt
            assert cnt < 60000, (e, cnt)
        ops = self.ops

        def run(e, eng):
            waited = {}
            for op in ops[e]:
                for d in op.deps:
                    k = semkey[id(d)]
                    if waited.get(k, 0) < d.val:
                        eng.wait_ge(d.sem, d.val)
                        waited[k] = d.val
                if op.dma and op.selfwait > 0:
                    k = semkey[id(op)]
                    if waited.get(k, 0) < op.selfwait:
                        eng.wait_ge(op.sem, op.selfwait)
                        waited[k] = op.selfwait
                ins = op.fn(eng)
                if op.dma:
                    ins.then_inc(op.sem, 16)
                elif op.users:
                    ins.then_inc(op.sem, 1)
            if e == 'sp':
                for k, (sem, val) in final.items():
                    if waited.get(k, 0) < val:
                        eng.wait_ge(sem, val)

        with nc.Block() as block:
            @block.tensor
            def _(eng):
                run('pe', eng)

            @block.scalar
            def _(eng):
                run('act', eng)

            @block.vector
            def _(eng):
                run('dve', eng)

            @block.gpsimd
            def _(eng):
                run('pool', eng)

            @block.sync
            def _(eng):
                run('sp', eng)


class Ring:
    def __init__(self, name, aps):
        self.name = name
        self.aps = aps
        self.i = 0

    def next(self):
        i = self.i % len(self.aps)
        self.i += 1
        return self.aps[i], (self.name, i)


def build(cfg):
    n_sub = cfg.get('n_sub', 6)
    debug = cfg.get('debug', False)
    skip_ffn = cfg.get('skip_ffn', False)
    mstop = cfg.get('mstop', 99)
    nc = bass.Bass("TRN2", target_bir_lowering=False)
    S = Sched(nc)
    es = contextlib.ExitStack()

    def din(name, shape):
        return nc.dram_tensor(name, list(shape), F32, kind="ExternalInput").ap()

    def dout(name, shape):
        return nc.dram_tensor(name, list(shape), F32, kind="ExternalOutput").ap()

    def sb(name, shape, dt):
        return es.enter_context(nc.sbuf_tensor("sb_" + name, list(shape), dt))

    xin = din("xin", [NTOK, D])
    c_ckv = din("c_ckv", [512, 512]); c_kr = din("c_kr", [512, 64])
    c_nak = din("c_nak", [512, 1024]); c_nav = din("c_nav", [512, 1024])
    c_gk = din("c_gk", [512, 512]); c_gv = din("c_gv", [512, 512])
    vecs = din("vecs", [640, 128])
    sinkb = din("sinkb", [128, 16])
    ident_d = din("ident", [128, 128])
    ropeg_c = din("ropeg_c", [128, 1024]); ropeg_s = din("ropeg_s", [128, 1024])
    ropem_c = din("ropem_c", [64, 1024]); ropem_s = din("ropem_s", [64, 1024])
    p128_d = din("p128", [128, 128]); p64_d = din("p64", [64, 64])
    wmask = din("wmask", [1024, 1024])
    nab = din("nab", [8 * 1024, 1024])
    ada_w = din("ada_w", [2 * 2048, 18432])
    ffn_w_in = din("ffn_w_in", [4 * 2048, 11264])
    ffn_w_out = din("ffn_w_out", [4 * 5632, 2048])
    even_w_in = din("even_w_in", [2048, 4160]); even_w_out = din("even_w_out", [2048, 2048])
    w_q_up = din("w_q_up", [512, 1536]); w_kv_up = din("w_kv_up", [512, 2048])
    odd_w_in = din("odd_w_in", [2048, 3072]); odd_w_out = din("odd_w_out", [2048, 2048])

    y_out = dout("y", [NTOK, D])
    o_ckv = dout("o_ckv", [512, 512]); o_kr = dout("o_kr", [512, 64])
    o_nak = dout("o_nak", [512, 1024]); o_nav = dout("o_nav", [512, 1024])
    o_gk = dout("o_gk", [512, 512]); o_gv = dout("o_gv", [512, 512])
    XD = nc.dram_tensor("xd_scratch", [128, 16, NTOK], F32, kind="ExternalOutput").ap()

    XE = 16 * NTOK * 2
    HE = GMAX * NTOK
    BIG = sb("BIG", [128, XE + HE], BF16)
    X = BIG[:, 0:XE].bitcast(F32).rearrange("p (c t) -> p c t", c=16)
    H = BIG[:, XE:XE + HE].rearrange("p (j t) -> p j t", j=GMAX)
    XMT = sb("XM", [128, 16 * NTOK], BF16)
    XM = XMT[:, :].rearrange("p (c t) -> p c t", c=16)
    WR = sb("WR", [128, NSLOT, 2048], BF16)
    VT = sb("VT", [128, 640], F32)
    MODS = sb("MODS", [128, 144, 2], F32)
    AMOD = sb("AMOD", [128, 3, 16, 2], F32)
    GMOD = sb("GMOD", [128, 3, 16, 2], F32)
    ident = sb("ident", [128, 128], F32)
    ones_f = sb("ones_f", [128, 128], F32)
    ones_b = sb("ones_b", [128, 128], BF16)
    sT = sb("sT", [128, 32], BF16)
    sinkE = sb("sinkE", [128, 16], F32)
    p128 = sb("p128", [128, 128], F32)
    p64 = sb("p64", [64, 64], F32)
    TR = sb("TR", [128, 4, 512], F32)
    RSR = sb("RSR", [128, 2, 512], F32)
    RDR = sb("RDR", [128, 3, 512], F32)
    tring = Ring('TR', [TR[:, i, :] for i in range(4)])
    rsring = Ring('RS', [RSR[:, i, :] for i in range(2)])
    rdring = Ring('RD', [RDR[:, i, :] for i in range(3)])
    PSB = [es.enter_context(nc.psum_tensor("ps%d" % i, [128, 512], F32)) for i in range(8)]
    psfree = list(range(8))

    def psget():
        b = psfree.pop(0)
        return b

    def psput(b):
        psfree.append(b)

    def dma(q, out, in_, r=(), w=()):
        return S.add(q, lambda e, o=out, i=in_: e.dma_start(out=o, in_=i), r=r, w=w, dma=True)

    def mm(ps, lhsT, rhs, start, stop, r, w):
        return S.add('pe', lambda e, a=ps, b=lhsT, c=rhs, s0=start, s1=stop: e.matmul(a, b, c, start=s0, stop=s1), r=r, w=w)

    def tp(ps, in_, idn, r, w):
        return S.add('pe', lambda e, a=ps, b=in_, c=idn: e.transpose(a, b, c), r=r, w=w)

    def act(out, in_, func, r, w, bias=None, scale=None):
        kw = {}
        if bias is not None:
            kw['bias'] = bias
        if scale is not None:
            kw['scale'] = scale
        return S.add('act', lambda e, o=out, i=in_, f=func, k=kw: e.activation(out=o, in_=i, func=f, **k), r=r, w=w)

    def tt(out, in0, in1, op, r, w, eng='dve'):
        return S.add(eng, lambda e, o=out, a=in0, b=in1, p=op: e.tensor_tensor(out=o, in0=a, in1=b, op=p), r=r, w=w)

    def ts(out, in0, s1, s2, op0, op1, r, w, eng='dve'):
        if s2 is None:
            return S.add(eng, lambda e, o=out, a=in0, x=s1, p=op0: e.tensor_scalar(out=o, in0=a, scalar1=x, scalar2=None, op0=p), r=r, w=w)
        return S.add(eng, lambda e, o=out, a=in0, x=s1, y=s2, p=op0, q=op1: e.tensor_scalar(out=o, in0=a, scalar1=x, scalar2=y, op0=p, op1=q), r=r, w=w)

    def stt(out, in0, sc, in1, op0, op1, r, w, eng='dve'):
        return S.add(eng, lambda e, o=out, a=in0, s=sc, b=in1, p=op0, q=op1: e.scalar_tensor_tensor(out=o, in0=a, scalar=s, in1=b, op0=p, op1=q), r=r, w=w)

    def cp(out, in_, r, w, eng='dve'):
        return S.add(eng, lambda e, o=out, i=in_: e.tensor_copy(out=o, in_=i), r=r, w=w)

    def recip(out, in_, r, w):
        return S.add('dve', lambda e, o=out, i=in_: e.reciprocal(out=o, in_=i), r=r, w=w)

    def memset(ap, v, w):
        return S.add('dve', lambda e, a=ap, c=v: e.memset(a, c), w=w)

    wslot = [0]

    def wload(src, a, b):
        s = wslot[0] % NSLOT
        wslot[0] += 1
        view = WR[:, s, 0:a * b].rearrange("p (a b) -> p a b", a=a)
        dma('pool', view, src, w=[('W', s)])
        return view, ('W', s)

    def wblk(wd, row0, nk, c0, nc_):
        src = wd[row0:row0 + nk * 128, c0:c0 + nc_].rearrange("(k p) n -> p k n", p=128)
        return wload(src, nk, nc_)

    def xc(fc, t0, t1):
        return [('X', fc, t) for t in range(t0 // 256, (t1 + 255) // 256)]

    def xmc(fc, t0, t1):
        return [('XM', fc, t) for t in range(t0 // 256, (t1 + 255) // 256)]

    def xdc(fc, t0, t1):
        return [('XD', fc, t) for t in range(t0 // 256, (t1 + 255) // 256)]

    dma('sp', ident[:, :], ident_d[:, :], w=['ident'])
    dma('sp', p128[:, :], p128_d[:, :], w=['p128'])
    dma('sp', p64[:, :], p64_d[:, :], w=['p64'])
    dma('sp', sinkE[:, :], sinkb[:, :], w=['sinkE'])
    memset(ones_f[:, :], 1.0, ['ones_f'])
    memset(ones_b[:, :], 1.0, ['ones_b'])
    act(sinkE[:, :], sinkE[:, :], AF.Exp, r=['sinkE'], w=['sinkE'])
    STGF = XMT[:, :].bitcast(F32)
    for t in range(5):
        st = STGF[:, t * 128:(t + 1) * 128]
        dma('sp', st, vecs[t * 128:(t + 1) * 128, :], w=[('STG', t)])
        b = psget()
        tp(PSB[b][:, 0:128], st, ident[:, :], r=[('STG', t), 'ident'], w=[('P', b)])
        cp(VT[:, t * 128:(t + 1) * 128], PSB[b][:, 0:128], r=[('P', b)], w=['VT'])
        psput(b)
    act(sT[:, :], VT[:, 0:32], AF.Silu, r=['VT'], w=['sT'])

    S.barrier()
    for t in range(12):
        sl = t % 3
        st = STGF[:, 4096 + sl * 2048: 4096 + (sl + 1) * 2048]
        dma('sp', st, xin[t * 128:(t + 1) * 128, :], w=[('STGX', sl)])
        for f4 in range(4):
            b = psget()
            for q in range(4):
                fc = f4 * 4 + q
                tp(PSB[b][:, q * 128:(q + 1) * 128], st[:, fc * 128:(fc + 1) * 128], ident[:, :],
                   r=[('STGX', sl), 'ident'], w=[('P', b)])
            dst = X[:, f4 * 4:(f4 + 1) * 4, t * 128:(t + 1) * 128]
            src = PSB[b][:, :].rearrange("p (q n) -> p q n", q=4)
            wc = []
            for q in range(4):
                wc += xc(f4 * 4 + q, t * 128, (t + 1) * 128)
            if f4 % 2 == 0:
                cp(dst, src, r=[('P', b)], w=wc)
            else:
                S.add('act', lambda e, o=dst, i=src: e.copy(out=o, in_=i), r=[('P', b)], w=wc)
            psput(b)
    S.barrier()

    def mods_mm(l, oc0, oc1, b):
        PM = PSB[b]
        for oc in range(oc0, oc1):
            blk, wc = wblk(ada_w, l * 2048, 16, oc * 128, 128)
            for k in range(16):
                mm(PM[:, 2 * oc:2 * oc + 2], blk[:, k, :], sT[:, k:32:16], k == 0, k == 15,
                   r=[wc, 'sT'], w=[('P', b)])

    def mods_fin(l, b, parts):
        PM = PSB[b]
        for i in parts:
            for g in range(2):
                tt(MODS[:, 48 * i:48 * (i + 1), g], PM[:, 96 * i + g:96 * (i + 1):2],
                   VT[:, 256 + l * 144 + 48 * i:256 + l * 144 + 48 * (i + 1)], ALU.add,
                   r=[('P', b), 'VT'], w=[('MODS', i)])
            for g in range(2):
                ts(AMOD[:, i, :, g], MODS[:, (3 * i + 1) * 16:(3 * i + 2) * 16, g], 1.0, None, ALU.add, None,
                   r=[('MODS', i)], w=[('AMOD', i)])
                tt(AMOD[:, i, :, g], AMOD[:, i, :, g], VT[:, 32 + (l * 3 + i) * 16:32 + (l * 3 + i + 1) * 16], ALU.mult,
                   r=[('AMOD', i), 'VT'], w=[('AMOD', i)])
                ts(GMOD[:, i, :, g], MODS[:, (3 * i + 2) * 16:(3 * i + 3) * 16, g], 0.5 if i != 1 else 1.0, None,
                   ALU.mult, None, r=[('MODS', i)], w=[('GMOD', i)])

    def mods(l):
        b = psget()
        mods_mm(l, 0, 144, b)
        mods_fin(l, b, [0, 1, 2])
        psput(b)

    def Asc(i, fc, g):
        return AMOD[:, i, fc, g:g + 1]

    def Bsc(i, fc, g):
        return MODS[:, 3 * i * 16 + fc, g:g + 1]

    def Gsc(i, fc, g):
        return GMOD[:, i, fc, g:g + 1]

    def rstd_from(srcs, n, dim):
        b = psget()
        ps = PSB[b][:, 0:n]
        for i, (ap, cells, kp) in enumerate(srcs):
            sq, sqc = tring.next()
            act(sq[0:kp, 0:n], ap, AF.Square, r=cells, w=[sqc])
            mm(ps, ones_f[0:kp, :], sq[0:kp, 0:n], i == 0, i == len(srcs) - 1, r=[sqc, 'ones_f'], w=[('P', b)])
        rs, rsc = rsring.next()
        ts(rs[:, 0:n], ps, 1.0 / dim, EPS, ALU.mult, ALU.add, r=[('P', b)], w=[rsc])
        psput(b)
        act(rs[:, 0:n], rs[:, 0:n], AF.Sqrt, r=[rsc], w=[rsc])
        rd, rdc = rdring.next()
        recip(rd[:, 0:n], rs[:, 0:n], r=[rsc], w=[rdc])
        return rd, rdc

    def prepass(i, src_fn, t0, n, g):
        srcs = []
        for fc in range(16):
            ap, cells = src_fn(fc)
            srcs.append((ap, cells, 128))
        rd, rdc = rstd_from(srcs, n, 2048.0)
        for fc in range(16):
            ap, cells = src_fn(fc)
            tmp, tc = tring.next()
            stt(tmp[:, 0:n], ap, Asc(i, fc, g), rd[:, 0:n], ALU.mult, ALU.mult, r=cells + [rdc, ('AMOD', i)], w=[tc])
            act(XM[:, fc, t0:t0 + n], tmp[:, 0:n], AF.Identity, r=[tc, ('MODS', i)], w=xmc(fc, t0, t0 + n),
                bias=Bsc(i, fc, g))

    TILES = [(0, 512, 0), (512, 512, 1), (1024, 512, 1)]
    GROUPS_IDX = {}
    _b = 0
    for _gi, _G in enumerate(GROUPS):
        GROUPS_IDX[_b] = _gi
        _b += _G

    def ffn(l, j, hook=None, hook2=None):
        i = 0 if j == 0 else 2
        for (t0, n, g) in TILES:
            prepass(i, lambda fc, t0=t0, n=n: (X[:, fc, t0:t0 + n], xc(fc, t0, t0 + n)), t0, n, g)
        row_in = (l * 2 + j) * 2048
        row_out = (l * 2 + j) * 5632
        base = 0
        for G in GROUPS:
            for jj in range(G):
                c = base + jj
                bg, wg = wblk(ffn_w_in, row_in, 16, c * 128, 128)
                bu, wu = wblk(ffn_w_in, row_in, 16, (44 + c) * 128, 128)
                for ti, (t0, n, g) in enumerate(TILES):
                    pg = psget(); pu = psget()
                    for k in range(16):
                        mm(PSB[pg][:, :], bg[:, k, :], XM[:, k, t0:t0 + n], k == 0, k == 15,
                           r=[wg] + xmc(k, t0, t0 + n), w=[('P', pg)])
                    for k in range(16):
                        mm(PSB[pu][:, :], bu[:, k, :], XM[:, k, t0:t0 + n], k == 0, k == 15,
                           r=[wu] + xmc(k, t0, t0 + n), w=[('P', pu)])
                    sg, sgc = tring.next()
                    act(sg[:, :], PSB[pg][:, :], AF.Silu, r=[('P', pg)], w=[sgc])
                    tt(H[:, jj, t0:t0 + n], sg[:, :], PSB[pu][:, :], ALU.mult, r=[sgc, ('P', pu)], w=[('H', jj, ti)])
                    psput(pg); psput(pu)
                if hook2 is not None:
                    hook2(c)
            for ocg in range(4):
                blks = []
                for kb in range(0, G, 4):
                    nk = min(4, G - kb)
                    blks.append(wblk(ffn_w_out, row_out + (base + kb) * 128, nk, ocg * 512, 512))
                for ti, (t0, n, g) in enumerate(TILES):
                    for o4 in range(4):
                        oc = ocg * 4 + o4
                        py = psget()
                        for jj in range(G):
                            blk, wc = blks[jj // 4]
                            mm(PSB[py][:, :], blk[:, jj % 4, o4 * 128:(o4 + 1) * 128], H[:, jj, t0:t0 + n],
                               jj == 0, jj == G - 1, r=[wc, ('H', jj, ti)], w=[('P', py)])
                        stt(X[:, oc, t0:t0 + n], PSB[py][:, :], Gsc(i, oc, g), X[:, oc, t0:t0 + n], ALU.mult, ALU.add,
                            r=[('P', py), ('GMOD', i)] + xc(oc, t0, t0 + n), w=xc(oc, t0, t0 + n))
                        psput(py)
            if hook is not None:
                hook(GROUPS_IDX[base])
            base += G


    INV192 = 1.0 / float(np.sqrt(192.0))
    INV128 = 1.0 / float(np.sqrt(128.0))
    aoff = [0]

    def areset(off=0):
        aoff[0] = off

    def ab(shape, dt):
        nel = int(np.prod(shape))
        ne16 = nel * (2 if dt == F32 else 1)
        ne16 = (ne16 + 31) // 32 * 32
        v = BIG[:, aoff[0]:aoff[0] + ne16]
        aoff[0] += ne16
        assert aoff[0] <= XE + HE, aoff[0]
        if dt == F32:
            v = v.bitcast(F32)
        v = v[:, 0:nel]
        if len(shape) == 2:
            v = v.rearrange("p (a b) -> p a b", a=shape[0])
        return v

    def rstd_from2(srcs, n, dim):
        b = psget()
        ps = PSB[b][:, 0:n]
        for i, (ap, cells, kp, presq) in enumerate(srcs):
            if presq:
                mm(ps, ones_f[0:kp, :], ap, i == 0, i == len(srcs) - 1, r=cells + ['ones_f'], w=[('P', b)])
            else:
                sq, sqc = tring.next()
                act(sq[0:kp, 0:n], ap, AF.Square, r=cells, w=[sqc])
                mm(ps, ones_f[0:kp, :], sq[0:kp, 0:n], i == 0, i == len(srcs) - 1, r=[sqc, 'ones_f'], w=[('P', b)])
        rs, rsc = rsring.next()
        ts(rs[:, 0:n], ps, 1.0 / dim, EPS, ALU.mult, ALU.add, r=[('P', b)], w=[rsc])
        psput(b)
        act(rs[:, 0:n], rs[:, 0:n], AF.Sqrt, r=[rsc], w=[rsc])
        rd, rdc = rdring.next()
        recip(rd[:, 0:n], rs[:, 0:n], r=[rsc], w=[rdc])
        return rd, rdc

    def rope_apply(src, srcc, kp, n, pmat, pmc, cs, sn, tabc, pos0, rd, rdc, out, outc, srcf=None, outf=None):
        srcf = src if srcf is None else srcf
        outf = out if outf is None else outf
        b = psget()
        mm(PSB[b][0:kp, 0:n], pmat, src, True, True, r=srcc + [pmc], w=[('P', b)])
        t1, t1c = tring.next()
        tt(t1[:, 0:n], srcf, cs[:, pos0:pos0 + n], ALU.mult, r=srcc + [tabc], w=[t1c])
        t2, t2c = tring.next()
        tt(t2[:, 0:n], PSB[b][:, 0:n], sn[:, pos0:pos0 + n], ALU.mult, r=[('P', b), tabc], w=[t2c])
        psput(b)
        if rd is None:
            tt(outf, t1[:, 0:n], t2[:, 0:n], ALU.add, r=[t1c, t2c], w=outc)
        else:
            tt(t1[:, 0:n], t1[:, 0:n], t2[:, 0:n], ALU.add, r=[t1c, t2c], w=[t1c])
            tt(outf, t1[:, 0:n], rd[:, 0:n], ALU.mult, r=[t1c, rdc], w=outc)

    def attn(qps, kps, vfn, chunks, nq, scale, out_ap, out_cells, ptring, bring, sink=None, ONES=None):
        po = psget(); pd = psget()
        started = {}
        last_idx = {}
        for idx, (kc, q0, q1, bias) in enumerate(chunks):
            last_idx[(q0, q1)] = idx
        pending = []

        def finish(item):
            idx, kc, q0, q1, pt, ptc = item
            vap, vc = vfn(kc)
            first = (q0, q1) not in started
            started[(q0, q1)] = True
            lastf = last_idx[(q0, q1)] == idx
            mm(PSB[po][:, q0:q1], vap, pt[:, q0:q1], first, lastf, r=[ptc] + vc, w=[('P', po)])
            mm(PSB[pd][:, q0:q1], ones_b[:, :], pt[:, q0:q1], first, lastf, r=[ptc, 'ones_b'], w=[('P', pd)])

        for idx, (kc, q0, q1, bias) in enumerate(chunks):
            ps = psget()
            for i, ((qap, qc), kfn) in enumerate(zip(qps, kps)):
                kap, kcells = kfn(kc)
                mm(PSB[ps][:, q0:q1], kap, qap[:, q0:q1], i == 0, i == len(qps) - 1, r=qc + kcells, w=[('P', ps)])
            pt, ptc = ptring.next()
            if bias is not None:
                bt, btc = bring.next()
                dma('sp', bt[:, q0:q1], bias, w=[btc])
                tmp, tc = tring.next()
                stt(tmp[:, q0:q1], PSB[ps][:, q0:q1], scale, bt[:, q0:q1], ALU.mult, ALU.add, r=[('P', ps), btc], w=[tc])
                act(pt[:, q0:q1], tmp[:, q0:q1], AF.Exp, r=[tc], w=[ptc])
            else:
                act(pt[:, q0:q1], PSB[ps][:, q0:q1], AF.Exp, r=[('P', ps)], w=[ptc], scale=scale)
            psput(ps)
            pending.append((idx, kc, q0, q1, pt, ptc))
            if len(pending) > 3:
                finish(pending.pop(0))
        while pending:
            finish(pending.pop(0))
        rd, rdc = rdring.next()
        if sink is not None:
            stt(rd[:, 0:nq], PSB[pd][:, 0:nq], sink, ONES[:, 0:nq], ALU.add, ALU.mult, r=[('P', pd), 'sinkE', 'ONES'], w=[rdc])
            recip(rd[:, 0:nq], rd[:, 0:nq], r=[rdc], w=[rdc])
        else:
            recip(rd[:, 0:nq], PSB[pd][:, 0:nq], r=[('P', pd)], w=[rdc])
        tt(out_ap, PSB[po][:, 0:nq], rd[:, 0:nq], ALU.mult, r=[('P', po), rdc], w=out_cells)
        psput(po); psput(pd)

    def mixer_pass(l, T0, n, latent, hookh=None):
        g = 1 if latent else 0
        ntile = n // 512
        tiles = [(i * 512, 512) for i in range(ntile)]
        nk = n + (512 if latent else 0)
        nkc = nk // 128
        noc = n // 128
        S.barrier()
        areset()
        if mstop <= 0:
            return
        if mstop <= 1:
            return
        S.barrier()
        areset()
        OT = ab([16, n], BF16)
        ptring = Ring('PT', [ab([512], BF16) for _ in range(4)])
        bring = Ring('BR', [ab([512], F32) for _ in range(6)]) if (latent and l == 1) else None
        cst = Ring('CST', [ab([512], F32) for _ in range(2)])
        ONES = None
        if l == 1:
            ONES = ab([512], F32)
            memset(ONES[:, :], 1.0, ['ONES'])
        if latent:
            rdim = 64 if l == 0 else 128
            COS = ab([1024], F32); SIN = ab([1024], F32)
            dma('sp', COS[0:rdim, :], (ropem_c if l == 0 else ropeg_c)[:, :], w=['ROPE'])
            dma('sp', SIN[0:rdim, :], (ropem_s if l == 0 else ropeg_s)[:, :], w=['ROPE'])
            PM_, PMC = (p64[:, :], 'p64') if l == 0 else (p128[:, :], 'p128')
        base_off = aoff[0]

        def xsrc(k, lt, nn):
            return XM[:, k, T0 + lt:T0 + lt + nn], xmc(k, T0 + lt, T0 + lt + nn)

        def kchunks(qt, bias_fn=None, own_range=None):
            ch = []
            if latent:
                for kc in range(n // 128):
                    if own_range is not None and not (own_range[qt][0] <= kc <= own_range[qt][1]):
                        continue
                    ch.append((kc, 0, 512, None if bias_fn is None else bias_fn(kc, qt)))
                for kc in range(n // 128, nkc):
                    ch.append((kc, 0, 512, None))
            else:
                ch = [(0, 0, 256, None), (1, 0, 256, None), (2, 256, 512, None), (3, 256, 512, None)]
            return ch

        def tok_major_v(blk, wc, colsl, VH, vhc, srcfn, nsrc, out_d, col0):
            for t4 in range(noc // 4):
                b = psget()
                for j in range(4):
                    tc_ = t4 * 4 + j
                    for k in range(nsrc):
                        sap, sc = srcfn(k, tc_ * 128, 128)
                        mm(PSB[b][:, j * 128:(j + 1) * 128], sap, blk[:, k, colsl], k == 0, k == nsrc - 1,
                           r=[wc] + sc, w=[('P', b)])
                S.add('act', lambda e, o=VH[:, t4 * 4:(t4 + 1) * 4, :], i=PSB[b][:, :].rearrange("p (j c) -> p j c", j=4): e.copy(out=o, in_=i),
                      r=[('P', b)], w=[vhc])
                if out_d is not None:
                    st, stc = cst.next()
                    S.add('act', lambda e, o=st[:, :], i=PSB[b][:, :]: e.copy(out=o, in_=i), r=[('P', b)], w=[stc])
                    dma('sp', out_d[t4 * 512:(t4 + 1) * 512, col0:col0 + 128].rearrange("(j p) c -> p j c", p=128),
                        st[:, :].rearrange("p (j c) -> p j c", j=4), r=[stc])
                psput(b)

        def ctx_kT(cd, col0, ncol, dst, dstc, kp):
            b = psget()
            for tc_ in range(4):
                st, stc = cst.next()
                dma('sp', st[:, 0:ncol], cd[tc_ * 128:(tc_ + 1) * 128, col0:col0 + ncol], w=[stc])
                tp(PSB[b][0:ncol, tc_ * 128:(tc_ + 1) * 128], st[:, 0:ncol], ident[:, :], r=[stc, 'ident'], w=[('P', b)])
            return b

        def out_kT(src, srcc, kp, out_d, col0):
            for tc_ in range(4):
                b = psget()
                tp(PSB[b][:, 0:kp], src[0:kp, tc_ * 128:(tc_ + 1) * 128], ident[0:kp, 0:kp], r=srcc + ['ident'], w=[('P', b)])
                st, stc = cst.next()
                cp(st[:, 0:kp], PSB[b][:, 0:kp], r=[('P', b)], w=[stc])
                psput(b)
                dma('sp', out_d[tc_ * 128:(tc_ + 1) * 128, col0:col0 + kp], st[:, 0:kp], r=[stc])

        if l == 0:
            CQN = ab([4, n], BF16); CKVN = ab([4, nk], BF16)
            KRG = ab([nk], F32); SQR = ab([nk], F32)
            CKF = ab([4, 512], F32) if not latent else None
            KRAW = ab([512], F32) if not latent else None
            msets = [(ab([n], BF16), ab([n], BF16), ab([nk], BF16), ab([nk], BF16), ab([nkc, 128], BF16)) for _ in range(2)]
            for which in range(2):
                blks = [wblk(even_w_in, 0, 16, which * 512 + oc * 128, 128) for oc in range(4)]
                for (lt, nn) in tiles:
                    banks = []
                    for oc in range(4):
                        b = psget()
                        blk, wc = blks[oc]
                        for k in range(16):
                            sap, sc = xsrc(k, lt, nn)
                            mm(PSB[b][:, 0:nn], blk[:, k, :], sap, k == 0, k == 15, r=[wc] + sc, w=[('P', b)])
                        banks.append(b)
                    rd, rdc = rstd_from2([(PSB[b][:, 0:nn], [('P', b)], 128, False) for b in banks], nn, 512.0)
                    for oc in range(4):
                        b = banks[oc]
                        gsc = VT[:, 128 + which * 4 + oc:129 + which * 4 + oc]
                        if which == 0:
                            stt(CQN[:, oc, lt:lt + nn], PSB[b][:, 0:nn], gsc, rd[:, 0:nn], ALU.mult, ALU.mult,
                                r=[('P', b), rdc, 'VT'], w=['CQN'])
                        elif latent:
                            stt(CKVN[:, oc, lt:lt + nn], PSB[b][:, 0:nn], gsc, rd[:, 0:nn], ALU.mult, ALU.mult,
                                r=[('P', b), rdc, 'VT'], w=['CKVN'])
                        else:
                            stt(CKF[:, oc, lt:lt + nn], PSB[b][:, 0:nn], gsc, rd[:, 0:nn], ALU.mult, ALU.mult,
                                r=[('P', b), rdc, 'VT'], w=['CKF'])
                            S.add('act', lambda e, o=CKVN[:, oc, lt:lt + nn], i=CKF[:, oc, lt:lt + nn]: e.copy(out=o, in_=i),
                                  r=['CKF'], w=['CKVN'])
                        psput(b)
            if not latent:
                for tc_ in range(4):
                    b = psget()
                    for oc in range(4):
                        tp(PSB[b][:, oc * 128:(oc + 1) * 128], CKF[:, oc, tc_ * 128:(tc_ + 1) * 128], ident[:, :],
                           r=['CKF', 'ident'], w=[('P', b)])
                    st, stc = cst.next()
                    cp(st[:, :], PSB[b][:, :], r=[('P', b)], w=[stc])
                    psput(b)
                    dma('sp', o_ckv[tc_ * 128:(tc_ + 1) * 128, :], st[:, :], r=[stc])
            else:
                for tc_ in range(4):
                    st, stc = cst.next()
                    dma('sp', st[:, :], c_ckv[tc_ * 128:(tc_ + 1) * 128, :], w=[stc])
                    b = psget()
                    for oc in range(4):
                        tp(PSB[b][:, oc * 128:(oc + 1) * 128], st[:, oc * 128:(oc + 1) * 128], ident[:, :],
                           r=[stc, 'ident'], w=[('P', b)])
                    cp(CKVN[:, :, n + tc_ * 128:n + (tc_ + 1) * 128], PSB[b][:, :].rearrange("p (o t) -> p o t", o=4),
                       r=[('P', b)], w=['CKVN'])
                    psput(b)
            blk, wc = wblk(even_w_in, 0, 16, 1024, 64)
            gkr = VT[:, 139:140]
            for (lt, nn) in tiles:
                b = psget()
                for k in range(16):
                    sap, sc = xsrc(k, lt, nn)
                    mm(PSB[b][0:64, 0:nn], blk[:, k, :], sap, k == 0, k == 15, r=[wc] + sc, w=[('P', b)])
                act(SQR[0:64, lt:lt + nn], PSB[b][0:64, 0:nn], AF.Square, r=[('P', b)], w=['SQR'])
                if latent:
                    t0_, t0c = tring.next()
                    act(t0_[:, 0:nn], PSB[b][:, 0:nn], AF.Identity, r=[('P', b), 'VT'], w=[t0c], scale=gkr)
                    rope_apply(t0_[0:64, 0:nn], [t0c], 64, nn, PM_, PMC, COS, SIN, 'ROPE', lt, None, None,
                               KRG[0:64, lt:lt + nn], ['KRG'], srcf=t0_[:, 0:nn], outf=KRG[:, lt:lt + nn])
                else:
                    act(KRG[:, lt:lt + nn], PSB[b][:, 0:nn], AF.Identity, r=[('P', b), 'VT'], w=['KRG'], scale=gkr)
                    S.add('act', lambda e, o=KRAW[:, lt:lt + nn], i=PSB[b][:, 0:nn]: e.copy(out=o, in_=i), r=[('P', b)], w=['KRAW'])
                psput(b)
            if not latent:
                out_kT(KRAW, ['KRAW'], 64, o_kr, 0)
            else:
                b = ctx_kT(c_kr, 0, 64, None, None, 64)
                act(SQR[0:64, n:n + 512], PSB[b][0:64, :], AF.Square, r=[('P', b)], w=['SQR'])
                act(KRG[:, n:n + 512], PSB[b][:, :], AF.Identity, r=[('P', b), 'VT'], w=['KRG'], scale=gkr)
                psput(b)
            if mstop <= 2:
                return
            def mlaA(h):
                QN, QR, KN, KR, VH = msets[h % 2]
                hs = h % 2
                blk, wc = wblk(w_q_up, 0, 4, h * 192, 192)
                for (lt, nn) in tiles:
                    ba = psget(); bb = psget()
                    for k in range(4):
                        mm(PSB[ba][:, 0:nn], blk[:, k, 0:128], CQN[:, k, lt:lt + nn], k == 0, k == 3, r=[wc, 'CQN'], w=[('P', ba)])
                    for k in range(4):
                        mm(PSB[bb][0:64, 0:nn], blk[:, k, 128:192], CQN[:, k, lt:lt + nn], k == 0, k == 3, r=[wc, 'CQN'], w=[('P', bb)])
                    rd, rdc = rstd_from2([(PSB[ba][:, 0:nn], [('P', ba)], 128, False), (PSB[bb][0:64, 0:nn], [('P', bb)], 64, False)], nn, 192.0)
                    stt(QN[:, lt:lt + nn], PSB[ba][:, 0:nn], VT[:, 136:137], rd[:, 0:nn], ALU.mult, ALU.mult,
                        r=[('P', ba), rdc, 'VT'], w=[('QNm', hs)])
                    psput(ba)
                    t0_, t0c = tring.next()
                    act(t0_[:, 0:nn], PSB[bb][:, 0:nn], AF.Identity, r=[('P', bb), 'VT'], w=[t0c], scale=VT[:, 137:138])
                    psput(bb)
                    if latent:
                        rope_apply(t0_[0:64, 0:nn], [t0c], 64, nn, PM_, PMC, COS, SIN, 'ROPE', lt, rd, rdc,
                                   QR[0:64, lt:lt + nn], [('QRm', hs)], srcf=t0_[:, 0:nn], outf=QR[:, lt:lt + nn])
                    else:
                        tt(QR[:, lt:lt + nn], t0_[:, 0:nn], rd[:, 0:nn], ALU.mult, r=[t0c, rdc], w=[('QRm', hs)])
                blk, wc = wblk(w_kv_up, 0, 4, h * 256, 256)
                for kt in range(nk // 512):
                    ba = psget()
                    for k in range(4):
                        mm(PSB[ba][:, :], blk[:, k, 0:128], CKVN[:, k, kt * 512:(kt + 1) * 512], k == 0, k == 3, r=[wc, 'CKVN'], w=[('P', ba)])
                    rd, rdc = rstd_from2([(PSB[ba][:, :], [('P', ba)], 128, False), (SQR[0:64, kt * 512:(kt + 1) * 512], ['SQR'], 64, True)], 512, 192.0)
                    stt(KN[:, kt * 512:(kt + 1) * 512], PSB[ba][:, :], VT[:, 138:139], rd[:, :], ALU.mult, ALU.mult,
                        r=[('P', ba), rdc, 'VT'], w=[('KNm', hs)])
                    psput(ba)
                    tt(KR[:, kt * 512:(kt + 1) * 512], KRG[:, kt * 512:(kt + 1) * 512], rd[:, :], ALU.mult,
                       r=['KRG', rdc], w=[('KRm', hs)])
                for t4 in range(nkc // 4):
                    b = psget()
                    for j in range(4):
                        kc = t4 * 4 + j
                        for k in range(4):
                            mm(PSB[b][:, j * 128:(j + 1) * 128], CKVN[:, k, kc * 128:(kc + 1) * 128], blk[:, k, 128:256],
                               k == 0, k == 3, r=[wc, 'CKVN'], w=[('P', b)])
                    S.add('act', lambda e, o=VH[:, t4 * 4:(t4 + 1) * 4, :], i=PSB[b][:, :].rearrange("p (j c) -> p j c", j=4): e.copy(out=o, in_=i),
                          r=[('P', b)], w=[('VHm', hs)])
                    psput(b)

            def mlaB(h):
                QN, QR, KN, KR, VH = msets[h % 2]
                hs = h % 2
                for qt, (lt, nn) in enumerate(tiles):
                    attn([(QN[:, lt:lt + nn], [('QNm', hs)]), (QR[0:64, lt:lt + nn], [('QRm', hs)])],
                         [lambda kc, KN=KN, hs=hs: (KN[:, kc * 128:(kc + 1) * 128], [('KNm', hs)]), lambda kc, KR=KR, hs=hs: (KR[0:64, kc * 128:(kc + 1) * 128], [('KRm', hs)])],
                         lambda kc, VH=VH, hs=hs: (VH[:, kc, :], [('VHm', hs)]), kchunks(qt), nn, INV192,
                         OT[:, h, lt:lt + nn], [('OT', h)], ptring, bring)

            nhm = 8 if mstop > 3 else 1
            mlaA(0)
            for h in range(nhm):
                if h + 1 < nhm:
                    mlaA(h + 1)
                mlaB(h)
                if hookh is not None:
                    hookh()
            if mstop <= 4:
                return
            S.barrier()
            areset(base_off)
            sets = []
            for _ in range(2):
                sets.append((ab([n], BF16), ab([nk], BF16), ab([nkc, 128], BF16)))
            KF = ab([512], F32) if not latent else None
            bring = Ring('BRn', [ab([512], F32) for _ in range(6)]) if latent else None
            def naA(h):
                QN, KN, VH = sets[h % 2]
                sfx = h % 2
                blk, wc = wblk(even_w_in, 0, 16, 1088 + h * 128, 128)
                for (lt, nn) in tiles:
                    b = psget()
                    for k in range(16):
                        sap, sc = xsrc(k, lt, nn)
                        mm(PSB[b][:, 0:nn], blk[:, k, :], sap, k == 0, k == 15, r=[wc] + sc, w=[('P', b)])
                    rd, rdc = rstd_from2([(PSB[b][:, 0:nn], [('P', b)], 128, False)], nn, 128.0)
                    stt(QN[:, lt:lt + nn], PSB[b][:, 0:nn], VT[:, 140:141], rd[:, 0:nn], ALU.mult, ALU.mult,
                        r=[('P', b), rdc, 'VT'], w=[('QNn', sfx)])
                    psput(b)
                blk, wc = wblk(even_w_in, 0, 16, 2112 + h * 128, 128)
                for (lt, nn) in tiles:
                    b = psget()
                    for k in range(16):
                        sap, sc = xsrc(k, lt, nn)
                        mm(PSB[b][:, 0:nn], blk[:, k, :], sap, k == 0, k == 15, r=[wc] + sc, w=[('P', b)])
                    rd, rdc = rstd_from2([(PSB[b][:, 0:nn], [('P', b)], 128, False)], nn, 128.0)
                    if latent:
                        stt(KN[:, lt:lt + nn], PSB[b][:, 0:nn], VT[:, 141:142], rd[:, 0:nn], ALU.mult, ALU.mult,
                            r=[('P', b), rdc, 'VT'], w=[('KNn', sfx)])
                    else:
                        stt(KF[:, lt:lt + nn], PSB[b][:, 0:nn], VT[:, 141:142], rd[:, 0:nn], ALU.mult, ALU.mult,
                            r=[('P', b), rdc, 'VT'], w=['KF'])
                        S.add('act', lambda e, o=KN[:, lt:lt + nn], i=KF[:, lt:lt + nn]: e.copy(out=o, in_=i), r=['KF'], w=[('KNn', sfx)])
                    psput(b)
                if not latent:
                    out_kT(KF, ['KF'], 128, o_nak, h * 128)
                else:
                    b = ctx_kT(c_nak, h * 128, 128, None, None, 128)
                    cp(KN[:, n:n + 512], PSB[b][:, :], r=[('P', b)], w=[('KNn', sfx)])
                    psput(b)
                blk, wc = wblk(even_w_in, 0, 16, 3136 + h * 128, 128)
                tok_major_v(blk, wc, slice(0, 128), VH, ('VHn', sfx), xsrc, 16, None if latent else o_nav, h * 128)
                if latent:
                    dma('pool', VH[:, noc:noc + 4, :], c_nav[:, h * 128:(h + 1) * 128].rearrange("(j p) c -> p j c", p=128), w=[('VHn', sfx)])

            def naB(h):
                QN, KN, VH = sets[h % 2]
                sfx = h % 2
                for qt, (lt, nn) in enumerate(tiles):
                    bf = (lambda kc, qt_, h=h: nab[h * 1024 + kc * 128:h * 1024 + (kc + 1) * 128, qt_ * 512:(qt_ + 1) * 512]) if latent else None
                    attn([(QN[:, lt:lt + nn], [('QNn', sfx)])],
                         [lambda kc, KN=KN, sfx=sfx: (KN[:, kc * 128:(kc + 1) * 128], [('KNn', sfx)])],
                         lambda kc, VH=VH, sfx=sfx: (VH[:, kc, :], [('VHn', sfx)]), kchunks(qt, bf), nn, INV128,
                         OT[:, 8 + h, lt:lt + nn], [('OT', 8 + h)], ptring, bring)

            naA(0)
            for h in range(8):
                if h + 1 < 8:
                    naA(h + 1)
                naB(h)
                if hookh is not None:
                    hookh()
            w_out = even_w_out
        else:
            ksets = [(ab([nk], BF16), ab([nkc, 128], BF16)) for _ in range(2)]
            qsets = [ab([n], BF16) for _ in range(2)]
            KF = ab([512], F32) if not latent else None
            own_range = [(0, 4), (3, 7)]
            def gkvA(kvh):
                KN, VH = ksets[kvh % 2]
                ks = kvh % 2
                blk, wc = wblk(odd_w_in, 0, 16, 2048 + kvh * 128, 128)
                for (lt, nn) in tiles:
                    b = psget()
                    for k in range(16):
                        sap, sc = xsrc(k, lt, nn)
                        mm(PSB[b][:, 0:nn], blk[:, k, :], sap, k == 0, k == 15, r=[wc] + sc, w=[('P', b)])
                    rd, rdc = rstd_from2([(PSB[b][:, 0:nn], [('P', b)], 128, False)], nn, 128.0)
                    if latent:
                        t0_, t0c = tring.next()
                        stt(t0_[:, 0:nn], PSB[b][:, 0:nn], VT[:, 143:144], rd[:, 0:nn], ALU.mult, ALU.mult,
                            r=[('P', b), rdc, 'VT'], w=[t0c])
                        rope_apply(t0_[:, 0:nn], [t0c], 128, nn, PM_, PMC, COS, SIN, 'ROPE', lt, None, None,
                                   KN[:, lt:lt + nn], [('KNg', ks)])
                    else:
                        stt(KF[:, lt:lt + nn], PSB[b][:, 0:nn], VT[:, 143:144], rd[:, 0:nn], ALU.mult, ALU.mult,
                            r=[('P', b), rdc, 'VT'], w=['KF'])
                        S.add('act', lambda e, o=KN[:, lt:lt + nn], i=KF[:, lt:lt + nn]: e.copy(out=o, in_=i), r=['KF'], w=[('KNg', ks)])
                    psput(b)
                if not latent:
                    out_kT(KF, ['KF'], 128, o_gk, kvh * 128)
                else:
                    b = ctx_kT(c_gk, kvh * 128, 128, None, None, 128)
                    cp(KN[:, n:n + 512], PSB[b][:, :], r=[('P', b)], w=[('KNg', ks)])
                    psput(b)
                blk, wc = wblk(odd_w_in, 0, 16, 2560 + kvh * 128, 128)
                tok_major_v(blk, wc, slice(0, 128), VH, ('VHg', ks), xsrc, 16, None if latent else o_gv, kvh * 128)
                if latent:
                    dma('pool', VH[:, noc:noc + 4, :], c_gv[:, kvh * 128:(kvh + 1) * 128].rearrange("(j p) c -> p j c", p=128), w=[('VHg', ks)])

            def gqA(h):
                kvh = h // 4
                QN = qsets[h % 2]
                qs = h % 2
                blk, wc = wblk(odd_w_in, 0, 16, h * 128, 128)
                for (lt, nn) in tiles:
                    b = psget()
                    for k in range(16):
                        sap, sc = xsrc(k, lt, nn)
                        mm(PSB[b][:, 0:nn], blk[:, k, :], sap, k == 0, k == 15, r=[wc] + sc, w=[('P', b)])
                    rd, rdc = rstd_from2([(PSB[b][:, 0:nn], [('P', b)], 128, False)], nn, 128.0)
                    if latent:
                        t0_, t0c = tring.next()
                        stt(t0_[:, 0:nn], PSB[b][:, 0:nn], VT[:, 142:143], rd[:, 0:nn], ALU.mult, ALU.mult,
                            r=[('P', b), rdc, 'VT'], w=[t0c])
                        rope_apply(t0_[:, 0:nn], [t0c], 128, nn, PM_, PMC, COS, SIN, 'ROPE', lt, None, None,
                                   QN[:, lt:lt + nn], [('QNg', qs)])
                    else:
                        stt(QN[:, lt:lt + nn], PSB[b][:, 0:nn], VT[:, 142:143], rd[:, 0:nn], ALU.mult, ALU.mult,
                            r=[('P', b), rdc, 'VT'], w=[('QNg', qs)])
                    psput(b)

            def gqB(h):
                kvh = h // 4
                KN, VH = ksets[kvh % 2]
                ks = kvh % 2
                QN = qsets[h % 2]
                qs = h % 2
                for qt, (lt, nn) in enumerate(tiles):
                    bf = (lambda kc, qt_: wmask[kc * 128:(kc + 1) * 128, qt_ * 512:(qt_ + 1) * 512]) if latent else None
                    attn([(QN[:, lt:lt + nn], [('QNg', qs)])],
                         [lambda kc, KN=KN, ks=ks: (KN[:, kc * 128:(kc + 1) * 128], [('KNg', ks)])],
                         lambda kc, VH=VH, ks=ks: (VH[:, kc, :], [('VHg', ks)]), kchunks(qt, bf, own_range), nn, INV128,
                         OT[:, h, lt:lt + nn], [('OT', h)], ptring, bring, sink=sinkE[:, h:h + 1], ONES=ONES)

            gkvA(0)
            gqA(0)
            for h in range(16):
                if h + 1 < 16:
                    if (h + 1) % 4 == 0:
                        gkvA((h + 1) // 4)
                    gqA(h + 1)
                gqB(h)
            w_out = odd_w_out
        xr = Ring('XR', [ab([512], F32) for _ in range(2)])
        for oc in range(16):
            blk, wc = wblk(w_out, 0, 16, oc * 128, 128)
            for (lt, nn) in tiles:
                b = psget()
                for k in range(16):
                    mm(PSB[b][:, 0:nn], blk[:, k, :], OT[:, k, lt:lt + nn], k == 0, k == 15, r=[wc, ('OT', k)], w=[('P', b)])
                xt_, xtc = xr.next()
                cells = xdc(oc, T0 + lt, T0 + lt + nn)
                dma('sp', xt_[:, 0:nn], XD[:, oc, T0 + lt:T0 + lt + nn], r=cells, w=[xtc])
                stt(xt_[:, 0:nn], PSB[b][:, 0:nn], Gsc(1, oc, g), xt_[:, 0:nn], ALU.mult, ALU.add,
                    r=[('P', b), ('GMOD', 1), xtc], w=[xtc])
                psput(b)
                dma('sp', XD[:, oc, T0 + lt:T0 + lt + nn], xt_[:, 0:nn], r=[xtc], w=cells)

    def mixer(l, hookh=None):
        for (t0, nn, g) in TILES:
            prepass(1, lambda fc, t0=t0, nn=nn: (X[:, fc, t0:t0 + nn], xc(fc, t0, t0 + nn)), t0, nn, g)
        for ti, (t0, nn, g) in enumerate(TILES):
            dma('sp', XD[:, :, t0:t0 + nn], X[:, :, t0:t0 + nn], r=[c for fc in range(16) for c in xc(fc, t0, t0 + nn)],
                w=[c for fc in range(16) for c in xdc(fc, t0, t0 + nn)])
        mixer_pass(l, 512, 1024, True, hookh)
        mixer_pass(l, 0, 512, False)
        S.barrier()
        for ti, (t0, nn, g) in enumerate(TILES):
            dma('sp', X[:, :, t0:t0 + nn], XD[:, :, t0:t0 + nn], r=[c for fc in range(16) for c in xdc(fc, t0, t0 + nn)],
                w=[c for fc in range(16) for c in xc(fc, t0, t0 + nn)])
        S.barrier()

    if n_sub == 6 and not skip_ffn and cfg.get('pipe', True):
        bm = psget()
        mods_mm(0, 0, 48, bm)
        mods_fin(0, bm, [0])
        cuts0 = [48 + (96 * q) // 44 for q in range(45)]
        ffn(0, 0, hook2=lambda c: mods_mm(0, cuts0[c], cuts0[c + 1], bm))
        mods_fin(0, bm, [1, 2])
        psput(bm)
        bm1 = psget()
        hcnt = [0]
        cuts1 = [(144 * q) // 16 for q in range(17)]

        def hk():
            q = hcnt[0]
            hcnt[0] += 1
            mods_mm(1, cuts1[q], cuts1[q + 1], bm1)
        mixer(0, hk)
        assert hcnt[0] == 16, hcnt[0]
        ffn(0, 1)
        mods_fin(1, bm1, [0, 1, 2])
        psput(bm1)
        ffn(1, 0)
        mixer(1)
        ffn(1, 1)
    else:
        sub = 0
        for l in range(2):
            if sub < n_sub:
                mods(l)
            if sub < n_sub and not skip_ffn:
                ffn(l, 0)
            sub += 1
            if sub < n_sub and cfg.get('mixer', True):
                mixer(l)
            sub += 1
            if sub < n_sub and not skip_ffn:
                ffn(l, 1)
            sub += 1

    S.barrier()
    for t in range(12):
        sl = t % 3
        st = STGF[:, 4096 + sl * 2048: 4096 + (sl + 1) * 2048]
        for f4 in range(4):
            b = psget()
            for q in range(4):
                fc = f4 * 4 + q
                tp(PSB[b][:, q * 128:(q + 1) * 128], X[:, fc, t * 128:(t + 1) * 128], ident[:, :],
                   r=xc(fc, t * 128, (t + 1) * 128) + ['ident'], w=[('P', b)])
            if f4 % 2 == 0:
                cp(st[:, f4 * 512:(f4 + 1) * 512], PSB[b][:, :], r=[('P', b)], w=[('STGO', sl, f4)])
            else:
                S.add('act', lambda e, o=st[:, f4 * 512:(f4 + 1) * 512], i=PSB[b][:, :]: e.copy(out=o, in_=i),
                      r=[('P', b)], w=[('STGO', sl, f4)])
            psput(b)
        dma('sp', y_out[t * 128:(t + 1) * 128, :], st, r=[('STGO', sl, f4) for f4 in range(4)])

    S.emit(cfg.get('limit'))
    es.close()
    return nc


def _prep_inputs(inp):
    f = np.float32
    g = lambda k: np.asarray(inp[k], dtype=f)
    ident = np.eye(128, dtype=f)
    def tables(R):
        half = R // 2
        nf = half // 2
        t = np.arange(1024)
        inv = (10000.0 ** (-np.arange(nf, dtype=np.float64) / nf))
        ang_r = (t // 64)[None, :] * inv[:, None]
        ang_c = (t % 64)[None, :] * inv[:, None]
        ang = np.concatenate([ang_r, ang_r, ang_c, ang_c], axis=0)
        P = np.zeros((R, R), f)
        for base in (0, half):
            for i in range(nf):
                P[base + nf + i, base + i] = -1.0
                P[base + i, base + nf + i] = 1.0
        return np.cos(ang).astype(f), np.sin(ang).astype(f), P
    gc, gs, p128 = tables(128)
    mc, ms, p64 = tables(64)
    idx = np.arange(1024)
    wm = np.where(np.abs(idx[None, :] - idx[:, None]) <= 128, 0.0, NEG).astype(f)
    rpb = g("na_rpb")[0]
    kr, kc = idx // 64, idx % 64
    qr, qc = idx // 64, idx % 64
    rs = np.clip(qr - 4, 0, 8)
    ws = np.clip(qc - 8, 0, 48)
    rowv = (kr[:, None] >= rs[None, :]) & (kr[:, None] < rs[None, :] + 8)
    colv = (kc[:, None] >= ws[None, :]) & (kc[:, None] < ws[None, :] + 16)
    valid = rowv & colv
    ro = np.clip(kr[:, None] - qr[None, :] + 7, 0, 14)
    co = np.clip(kc[:, None] - qc[None, :], -15, 15) + 15
    flat = ro * 31 + co
    flat = np.where(valid, flat, 15 * 31)
    rp = np.concatenate([rpb.reshape(8, -1), np.full((8, 1), NEG, f)], axis=1)
    nabias = np.ascontiguousarray(rp[:, flat].reshape(8 * 1024, 1024))
    shared = {
        "ident": ident, "ropeg_c": gc, "ropeg_s": gs, "ropem_c": mc, "ropem_s": ms, "p128": p128, "p64": p64,
        "wmask": wm, "nab": nabias,
        "sinkb": np.ascontiguousarray(np.broadcast_to(g("gqa_sink")[0][None, :], (128, 16))),
        "ada_w": g("ada_w").reshape(4096, 18432),
        "ffn_w_in": g("ffn_w_in").reshape(4 * 2048, 11264),
        "ffn_w_out": g("ffn_w_out").reshape(4 * 5632, 2048),
        "even_w_in": g("even_w_in")[0], "even_w_out": g("even_w_out")[0],
        "w_q_up": g("mla_w_q_up")[0], "w_kv_up": g("mla_w_kv_up")[0],
        "odd_w_in": g("odd_w_in")[0], "odd_w_out": g("odd_w_out")[0],
    }
    maps = []
    for i in range(8):
        vec = np.zeros((640, 128), f)
        vec[0:16] = g("c_ctx").reshape(16, 128)
        vec[16:32] = g("c")[i].reshape(16, 128)
        vec[32:128] = g("norm_g").reshape(96, 128)
        vec[128:132] = g("mla_q_norm")[0].reshape(4, 128)
        vec[132:136] = g("mla_kv_norm")[0].reshape(4, 128)
        qk = g("mla_qk_norm")[0]
        vec[136] = qk[0, 0:128]; vec[137, 0:64] = qk[0, 128:192]
        vec[138] = qk[1, 0:128]; vec[139, 0:64] = qk[1, 128:192]
        vec[140:142] = g("na_qk_norm")[0]
        vec[142:144] = g("gqa_qk_norm")[0]
        vec[256:544] = g("ada_b").reshape(288, 128)
        m = dict(shared)
        m["xin"] = np.concatenate([g("x_prompt")[2 * i:2 * i + 2].reshape(512, D), g("x_sample")[i]], axis=0)
        m["c_ckv"] = g("cache_mla_ckv")[i, 0]; m["c_kr"] = g("cache_mla_krope")[i, 0]
        m["c_nak"] = g("cache_na_k")[i, 0].reshape(512, 1024); m["c_nav"] = g("cache_na_v")[i, 0].reshape(512, 1024)
        m["c_gk"] = g("cache_gqa_k")[i, 0].reshape(512, 512); m["c_gv"] = g("cache_gqa_v")[i, 0].reshape(512, 512)
        m["vecs"] = vec
        maps.append(m)
    return maps


def _assemble(results):
    n = len(results)
    f = np.float32
    yp = np.zeros((16, 256, D), f); ys = np.zeros((8, 1024, D), f)
    ckv = np.zeros((16, 1, 256, 512), f); kr = np.zeros((16, 1, 256, 64), f)
    nak = np.zeros((16, 1, 256, 8, 128), f); nav = np.zeros((16, 1, 256, 8, 128), f)
    gk = np.zeros((16, 1, 256, 4, 128), f); gv = np.zeros((16, 1, 256, 4, 128), f)
    for i in range(n):
        r = results[i]
        yp[2 * i:2 * i + 2] = r["y"][0:512].reshape(2, 256, D)
        ys[i] = r["y"][512:]
        ckv[2 * i:2 * i + 2, 0] = r["o_ckv"].reshape(2, 256, 512)
        kr[2 * i:2 * i + 2, 0] = r["o_kr"].reshape(2, 256, 64)
        nak[2 * i:2 * i + 2, 0] = r["o_nak"].reshape(2, 256, 8, 128)
        nav[2 * i:2 * i + 2, 0] = r["o_nav"].reshape(2, 256, 8, 128)
        gk[2 * i:2 * i + 2, 0] = r["o_gk"].reshape(2, 256, 4, 128)
        gv[2 * i:2 * i + 2, 0] = r["o_gv"].reshape(2, 256, 4, 128)
    return (yp, ys, ckv, kr, nak, nav, gk, gv)


def kernel(**inputs):
    maps = _prep_inputs(inputs)
    nc = build({})
    res = run_bass_kernel_spmd(nc, maps, core_ids=list(range(8)))
    return _assemble(res.results)
```
